# Optimizing a Trainium2 kernel written in Bass

```python
import math
import jax
import jax.numpy as jnp
from jax import lax
import numpy as np

D_MODEL = 2048
BATCH = 4
SEQ = 2048
DEPTH = 1
DEC_BATCH = 128
DEC_SEQ = 1
PAST_LEN = 16384
PAGE_SIZE = 128

HGRN_WIDTH = D_MODEL // 2
HGRN_HEAD_DIM = 128
HGRN_HEADS = HGRN_WIDTH // HGRN_HEAD_DIM
HGRN_CHUNK = 16
S5_WIDTH = D_MODEL - HGRN_WIDTH
S5_GROUP = 16
S5_GROUPS = S5_WIDTH // S5_GROUP
S5_STATE = 64
FFN_HIDDEN = -(-8 * D_MODEL // (3 * 256)) * 256
IN_PROJ_WIDTH = 4 * HGRN_WIDTH + S5_WIDTH + 2 * D_MODEL
RMS_EPS = 1e-6

kernel_name = 'hgrn2_s5_gated_hybrid_step'


def rmsnorm(x, g):
    xf = x.astype(jnp.float32)
    xf = xf * lax.rsqrt(jnp.mean(xf * xf, axis=-1, keepdims=True) + RMS_EPS)
    return (xf * g.astype(jnp.float32)).astype(x.dtype)


def hgrn_lower_bounds(lb_logits):
    return jnp.cumsum(jax.nn.softmax(lb_logits.astype(jnp.float32), axis=0), axis=0)


def hgrn2_recurrence(q, k, v, g, S0):
    Bsz, T, H, _ = q.shape
    C = min(HGRN_CHUNK, T)
    n_chunks = -(-T // C)
    pad = n_chunks * C - T

    def prep(a):
        a = jnp.pad(a, ((0, 0), (0, pad), (0, 0), (0, 0)))
        return a.reshape(Bsz, n_chunks, C, H, a.shape[-1]).transpose(1, 0, 3, 2, 4)

    qc, kc, vc, gc = prep(q), prep(k), prep(v), prep(g)
    b = jnp.cumsum(gc, axis=3)
    mid = C // 2
    b_mid = b[:, :, :, mid:mid + 1]
    b_last = b[:, :, :, -1:]
    scores = jnp.einsum('nbhtk,nbhsk->nbhts', qc * jnp.exp(b - b_mid), kc * jnp.exp(b_mid - b))
    causal = jnp.tril(jnp.ones((C, C), dtype=bool))
    scores = jnp.where(causal, scores, 0.0)
    o_intra = jnp.einsum('nbhts,nbhsv->nbhtv', scores, vc)
    q_in = qc * jnp.exp(b)
    k_upd = kc * jnp.exp(b_last - b)
    decay = jnp.exp(b_last[:, :, :, 0, :])

    def step(S, inp):
        q_i, k_u, v_i, dec = inp
        o_inter = jnp.einsum('bhtk,bhkv->bhtv', q_i, S)
        S = dec[..., None] * S + jnp.einsum('bhsk,bhsv->bhkv', k_u, v_i)
        return S, o_inter

    S_final, o_inter = lax.scan(step, S0, (q_in, k_upd, vc, decay))
    o = (o_intra + o_inter).transpose(1, 0, 3, 2, 4).reshape(Bsz, n_chunks * C, H, vc.shape[-1])[:, :T]
    return o, S_final


def s5_scan(u, lam_re, lam_im, log_dt, B_re, B_im, C_re, C_im, d_skip, h0_re, h0_im):
    dt = jnp.exp(log_dt)[:, None]
    mag = jnp.exp(lam_re * dt)
    ab_re = mag * jnp.cos(lam_im * dt)
    ab_im = mag * jnp.sin(lam_im * dt)
    den = lam_re * lam_re + lam_im * lam_im
    nr = ab_re - 1.0
    ni = ab_im
    z_re = (nr * lam_re + ni * lam_im) / den
    z_im = (ni * lam_re - nr * lam_im) / den
    bb_re = z_re[:, :, None] * B_re - z_im[:, :, None] * B_im
    bb_im = z_re[:, :, None] * B_im + z_im[:, :, None] * B_re
    bu_re = jnp.einsum('gnp,btgp->btgn', bb_re, u)
    bu_im = jnp.einsum('gnp,btgp->btgn', bb_im, u)
    bu_re = bu_re.at[:, 0].add(ab_re * h0_re - ab_im * h0_im)
    bu_im = bu_im.at[:, 0].add(ab_re * h0_im + ab_im * h0_re)
    T = u.shape[1]
    a_re = jnp.broadcast_to(ab_re, (1, T) + ab_re.shape)
    a_im = jnp.broadcast_to(ab_im, (1, T) + ab_im.shape)

    def combine(e1, e2):
        a1r, a1i, b1r, b1i = e1
        a2r, a2i, b2r, b2i = e2
        return (a2r * a1r - a2i * a1i,
                a2r * a1i + a2i * a1r,
                a2r * b1r - a2i * b1i + b2r,
                a2r * b1i + a2i * b1r + b2i)

    _, _, h_re, h_im = lax.associative_scan(combine, (a_re, a_im, bu_re, bu_im), axis=1)
    y = (jnp.einsum('gpn,btgn->btgp', C_re, h_re) - jnp.einsum('gpn,btgn->btgp', C_im, h_im)
         + d_skip.reshape(S5_GROUPS, S5_GROUP) * u)
    return y, h_re[:, -1], h_im[:, -1]


def decoder_layer(x, S0, h0_re, h0_im, l, w):
    Bsz, T, _ = x.shape
    f32 = jnp.float32
    h = rmsnorm(x, w['norm1_g'][l])
    proj = h @ w['w_in'][l]
    hw = HGRN_WIDTH
    splits = [hw, 2 * hw, 3 * hw, 4 * hw, 4 * hw + S5_WIDTH, 4 * hw + S5_WIDTH + D_MODEL]
    q, f_logit, i_in, og, u_in, gate_a, gate_b = jnp.split(proj, splits, axis=-1)

    lb = hgrn_lower_bounds(w['lb_logits'])[l]
    f = lb + (1.0 - lb) * jax.nn.sigmoid(f_logit.astype(f32))

    def heads(a):
        return a.astype(f32).reshape(Bsz, T, HGRN_HEADS, HGRN_HEAD_DIM)

    o_h, S_new = hgrn2_recurrence(heads(q), heads(1.0 - f), heads(i_in), heads(jnp.log(f)), S0.astype(f32))
    o_h = o_h * lax.rsqrt(jnp.mean(o_h * o_h, axis=-1, keepdims=True) + RMS_EPS)
    o_h = o_h * w['hgrn_norm_g'][l].astype(f32).reshape(HGRN_HEADS, HGRN_HEAD_DIM)
    o_a = (o_h.reshape(Bsz, T, hw) * jax.nn.silu(og.astype(f32))).astype(x.dtype)

    u = u_in.astype(f32).reshape(Bsz, T, S5_GROUPS, S5_GROUP)
    y5, h_re, h_im = s5_scan(u, w['s5_lam_re'][l].astype(f32), w['s5_lam_im'][l].astype(f32),
                             w['s5_log_dt'][l].astype(f32), w['s5_B_re'][l].astype(f32),
                             w['s5_B_im'][l].astype(f32), w['s5_C_re'][l].astype(f32),
                             w['s5_C_im'][l].astype(f32), w['s5_D'][l].astype(f32),
                             h0_re.astype(f32), h0_im.astype(f32))
    z = jax.nn.gelu(y5.reshape(Bsz, T, S5_WIDTH), approximate=False).astype(x.dtype)
    zg = z @ w['w_s5_glu'][l]
    o_b = zg[..., :S5_WIDTH] * jax.nn.sigmoid(zg[..., S5_WIDTH:])

    merged = (jax.nn.sigmoid(gate_a) * (o_a @ w['w_proj_a'][l])
              + jax.nn.sigmoid(gate_b) * (o_b @ w['w_proj_b'][l]))
    x = x + merged @ w['w_out'][l]

    h2 = rmsnorm(x, w['norm2_g'][l])
    up = h2 @ w['w_ffn_up'][l]
    x = x + (jax.nn.silu(up[..., :FFN_HIDDEN]) * up[..., FFN_HIDDEN:]) @ w['w_ffn_down'][l]
    return x, S_new, h_re, h_im


def setup_inputs(seed: int = 0) -> dict:
    key = jax.random.key(seed)
    ks = jax.random.split(key, 26)
    nrm = jax.random.normal
    L, D, hw, sw = DEPTH, D_MODEL, HGRN_WIDTH, S5_WIDTH
    G, N, P = S5_GROUPS, S5_STATE, S5_GROUP
    inp = {}
    inp['x_prompt'] = nrm(ks[0], (BATCH, SEQ, D), jnp.float32)
    inp['x_sample'] = nrm(ks[1], (DEC_BATCH, DEC_SEQ, D), jnp.float32)
    inp['state_hgrn'] = 0.3 * nrm(ks[2], (L, DEC_BATCH, HGRN_HEADS, HGRN_HEAD_DIM, HGRN_HEAD_DIM), jnp.float32)
    inp['state_s5_re'] = 0.1 * nrm(ks[3], (L, DEC_BATCH, G, N), jnp.float32)
    inp['state_s5_im'] = 0.1 * nrm(ks[4], (L, DEC_BATCH, G, N), jnp.float32)
    inp['lb_logits'] = 0.1 * nrm(ks[5], (L + 1, hw), jnp.float32)
    inp['norm1_g'] = 1.0 + 0.02 * nrm(ks[6], (L, D), jnp.float32)
    inp['w_in'] = nrm(ks[7], (L, D, IN_PROJ_WIDTH), jnp.float32) * D ** -0.5
    inp['hgrn_norm_g'] = 1.0 + 0.02 * nrm(ks[8], (L, hw), jnp.float32)
    inp['s5_lam_re'] = -0.5 + 0.01 * nrm(ks[9], (L, G, N), jnp.float32)
    inp['s5_lam_im'] = (math.pi * jnp.arange(N, dtype=jnp.float32))[None, None, :] + 0.01 * nrm(ks[10], (L, G, N), jnp.float32)
    inp['s5_log_dt'] = jax.random.uniform(ks[11], (L, G), jnp.float32, math.log(1e-3), math.log(1e-1))
    inp['s5_B_re'] = nrm(ks[12], (L, G, N, P), jnp.float32) * (2 * P) ** -0.5
    inp['s5_B_im'] = nrm(ks[13], (L, G, N, P), jnp.float32) * (2 * P) ** -0.5
    inp['s5_C_re'] = nrm(ks[14], (L, G, P, N), jnp.float32) * N ** -0.5
    inp['s5_C_im'] = nrm(ks[15], (L, G, P, N), jnp.float32) * N ** -0.5
    inp['s5_D'] = nrm(ks[16], (L, sw), jnp.float32)
    inp['w_s5_glu'] = nrm(ks[17], (L, sw, 2 * sw), jnp.float32) * sw ** -0.5
    inp['w_proj_a'] = nrm(ks[18], (L, hw, D), jnp.float32) * hw ** -0.5
    inp['w_proj_b'] = nrm(ks[19], (L, sw, D), jnp.float32) * sw ** -0.5
    inp['w_out'] = nrm(ks[20], (L, D, D), jnp.float32) * D ** -0.5
    inp['norm2_g'] = 1.0 + 0.02 * nrm(ks[21], (L, D), jnp.float32)
    inp['w_ffn_up'] = nrm(ks[22], (L, D, 2 * FFN_HIDDEN), jnp.float32) * D ** -0.5
    inp['w_ffn_down'] = nrm(ks[23], (L, FFN_HIDDEN, D), jnp.float32) * FFN_HIDDEN ** -0.5
    inp['final_norm_g'] = 1.0 + 0.02 * nrm(ks[24], (D,), jnp.float32)
    return inp


def reference(x_prompt, x_sample, state_hgrn, state_s5_re, state_s5_im, lb_logits, norm1_g, w_in,
              hgrn_norm_g, s5_lam_re, s5_lam_im, s5_log_dt, s5_B_re, s5_B_im, s5_C_re, s5_C_im, s5_D,
              w_s5_glu, w_proj_a, w_proj_b, w_out, norm2_g, w_ffn_up, w_ffn_down, final_norm_g):
    w = {'lb_logits': lb_logits, 'norm1_g': norm1_g, 'w_in': w_in, 'hgrn_norm_g': hgrn_norm_g,
         's5_lam_re': s5_lam_re, 's5_lam_im': s5_lam_im, 's5_log_dt': s5_log_dt,
         's5_B_re': s5_B_re, 's5_B_im': s5_B_im, 's5_C_re': s5_C_re, 's5_C_im': s5_C_im, 's5_D': s5_D,
         'w_s5_glu': w_s5_glu, 'w_proj_a': w_proj_a, 'w_proj_b': w_proj_b, 'w_out': w_out,
         'norm2_g': norm2_g, 'w_ffn_up': w_ffn_up, 'w_ffn_down': w_ffn_down}
    bp = x_prompt.shape[0]
    xp, xs = x_prompt, x_sample
    p_h, p_re, p_im, s_h, s_re, s_im = [], [], [], [], [], []
    for l in range(DEPTH):
        zero_h = jnp.zeros((bp, HGRN_HEADS, HGRN_HEAD_DIM, HGRN_HEAD_DIM), jnp.float32)
        zero_s5 = jnp.zeros((bp, S5_GROUPS, S5_STATE), jnp.float32)
        xp, sh, sr, si = decoder_layer(xp, zero_h, zero_s5, zero_s5, l, w)
        p_h.append(sh); p_re.append(sr); p_im.append(si)
        xs, sh, sr, si = decoder_layer(xs, state_hgrn[l], state_s5_re[l], state_s5_im[l], l, w)
        s_h.append(sh); s_re.append(sr); s_im.append(si)
    y_prompt = rmsnorm(xp, final_norm_g)
    y_sample = rmsnorm(xs, final_norm_g)
    return (y_prompt, y_sample, jnp.stack(p_h), jnp.stack(p_re), jnp.stack(p_im),
            jnp.stack(s_h), jnp.stack(s_re), jnp.stack(s_im))
```

```python
import math
import types
import numpy as np
import concourse.bass as bass
import concourse.mybir as mybir
from concourse.bass_utils import run_bass_kernel_spmd

F32 = mybir.dt.float32
BF16 = mybir.dt.bfloat16
AF = mybir.ActivationFunctionType
ALU = mybir.AluOpType
ENGS = ("pe", "act", "dve", "pool", "sp")
EPS = 1e-6


def _freeze(fn):
    if fn.__closure__ is None:
        return fn
    cells = []
    for c in fn.__closure__:
        try:
            cells.append(types.CellType(c.cell_contents))
        except ValueError:
            cells.append(c)
    return types.FunctionType(fn.__code__, fn.__globals__, fn.__name__, fn.__defaults__, tuple(cells))


class Prog:
    def __init__(self, nc, n_dma_sems=40):
        self.nc = nc
        self.ops = {e: [] for e in ENGS}
        self.cnt = {e: 0 for e in ENGS}
        self.sems = {}
        self.known = {e: {} for e in ENGS}
        self.last_w = {}
        self.readers = {}
        self.n_dma_sems = n_dma_sems
        self.dma_sems = []
        self.dma_cnt = []
        self.dma_rr = 0
        self.out_tokens = []
        self._ctx = []
        self.dry = False

    def setup(self):
        nc = self.nc
        for e in ("pe", "act", "dve", "pool"):
            cm = nc.semaphore("s_" + e)
            self.sems[e] = cm.__enter__()
            self._ctx.append(cm)
        for i in range(self.n_dma_sems):
            cm = nc.semaphore("d_%d" % i)
            self.dma_sems.append(cm.__enter__())
            self._ctx.append(cm)
            self.dma_cnt.append(0)

    def close(self):
        for cm in reversed(self._ctx):
            cm.__exit__(None, None, None)

    def _waits(self, eng, reads, writes):
        toks = []
        for k in reads:
            t = self.last_w.get(k)
            if t is not None:
                toks.append(t)
        for k in writes:
            t = self.last_w.get(k)
            if t is not None:
                toks.append(t)
            toks.extend(self.readers.get(k, ()))
        need = {}
        for (sem, val, teng) in toks:
            if teng == "pe" and eng == "pe":
                continue
            key = id(sem)
            if self.known[eng].get(key, 0) >= val:
                continue
            if key not in need or need[key][1] < val:
                need[key] = (sem, val)
        for key, (sem, val) in need.items():
            self.known[eng][key] = val
        return list(need.values())

    def _record(self, tok, reads, writes):
        for k in writes:
            self.last_w[k] = tok
            self.readers[k] = []
        for k in reads:
            lst = self.readers.setdefault(k, [])
            lst.append(tok)
            if len(lst) > 64:
                best = {}
                for t in lst:
                    if id(t[0]) not in best or best[id(t[0])][1] < t[1]:
                        best[id(t[0])] = t
                self.readers[k] = list(best.values())

    def op(self, eng, fn, reads=(), writes=(), track=True):
        if self.dry:
            return None
        fn = _freeze(fn)
        waits = self._waits(eng, reads, writes)
        tok = (self.sems[eng], self.cnt[eng] + 1, eng)
        if track:
            self.cnt[eng] += 1
        self.ops[eng].append((waits, fn, (self.sems[eng], 1) if track else None))
        self._record(tok, reads, writes)
        return tok

    def dma(self, eng, out, in_, reads=(), writes=(), is_output=False, **kw):
        if self.dry:
            return None
        j = self.dma_rr
        self.dma_rr = (self.dma_rr + 1) % self.n_dma_sems
        sem = self.dma_sems[j]
        waits = self._waits(eng, reads, writes)
        prev = self.dma_cnt[j]
        if prev > 0 and self.known[eng].get(id(sem), 0) < prev:
            waits.append((sem, prev))
            self.known[eng][id(sem)] = prev
        self.dma_cnt[j] += 16
        tok = (sem, self.dma_cnt[j], "dma")

        def fn(e, out=out, in_=in_, kw=kw):
            return e.dma_start(out=out, in_=in_, **kw)

        self.ops[eng].append((waits, fn, (sem, 16)))
        self._record(tok, reads, writes)
        if is_output:
            self.out_tokens.append(tok)
        return tok

    def barrier(self):
        if self.dry:
            return
        for e in ENGS:
            waits = []
            for e2 in ("pe", "act", "dve", "pool"):
                if e2 != e and self.cnt[e2] > 0 and self.known[e].get(id(self.sems[e2]), 0) < self.cnt[e2]:
                    waits.append((self.sems[e2], self.cnt[e2]))
                    self.known[e][id(self.sems[e2])] = self.cnt[e2]
            for j, sem in enumerate(self.dma_sems):
                if self.dma_cnt[j] > 0 and self.known[e].get(id(sem), 0) < self.dma_cnt[j]:
                    waits.append((sem, self.dma_cnt[j]))
                    self.known[e][id(sem)] = self.dma_cnt[j]
            self.ops[e].append((waits, None, None))

    def emit(self):
        nc = self.nc
        need = {}
        for (sem, val, _) in self.out_tokens:
            if id(sem) not in need or need[id(sem)][1] < val:
                need[id(sem)] = (sem, val)
        self.ops["sp"].append((list(need.values()), None, None))
        with nc.Block() as block:
            def run(e, lst):
                for waits, fn, inc in lst:
                    for (sem, val) in waits:
                        e.wait_ge(sem, val)
                    if fn is None:
                        continue
                    inst = fn(e)
                    if inc is not None:
                        inst.then_inc(inc[0], inc[1])

            @block.tensor
            def _(e):
                run(e, self.ops["pe"])

            @block.scalar
            def _(e):
                run(e, self.ops["act"])

            @block.vector
            def _(e):
                run(e, self.ops["dve"])

            @block.gpsimd
            def _(e):
                run(e, self.ops["pool"])

            @block.sync
            def _(e):
                run(e, self.ops["sp"])


SBUF_LEFT = [0]
FULL_CFG = dict(D=2048, FH=5632, BT=256, NPRE=4, NMAIN=4, NS=16)


def build(cfg):
    D, FH, BT, NPRE, NMAIN, NS = cfg["D"], cfg["FH"], cfg["BT"], cfg["NPRE"], cfg["NMAIN"], cfg["NS"]
    HW = D // 2
    SW = D // 2
    KD = D // 128
    H = HW // 128
    GT = SW // 128
    G = SW // 16
    FT = FH // 128
    NT = BT // 128
    NCH = BT // 8
    TB = BT + NS
    NCHT = NCH + NS
    WIN = 4 * HW + SW + 2 * D
    DB = min(512, D)
    NDB = D // DB
    assert TB <= 512 and NS == 16

    nc = bass.Bass("TRN2", target_bir_lowering=False)

    def din(name, shape):
        return nc.dram_tensor(name, list(shape), F32, kind="ExternalInput").ap()

    def dout(name, shape):
        return nc.dram_tensor(name, list(shape), F32, kind="ExternalOutput").ap()

    xpre = din("xpre", [NPRE * BT, D])
    xmain = din("xmain", [NMAIN * BT, D])
    xs = din("xs", [NS, D])
    sh0 = din("sh0", [NS, H, 128, 128])
    s5re0 = din("s5re0", [NS, G, 64])
    s5im0 = din("s5im0", [NS, G, 64])
    lb_logits = din("lb_logits", [2, HW])
    norm1_g = din("norm1_g", [D])
    w_in = din("w_in", [D, WIN])
    hgrn_g = din("hgrn_norm_g", [HW])
    lam_re = din("s5_lam_re", [G, 64])
    lam_im = din("s5_lam_im", [G, 64])
    log_dt = din("s5_log_dt", [G])
    B_re = din("s5_B_re", [G, 64, 16])
    B_im = din("s5_B_im", [G, 64, 16])
    C_re = din("s5_C_re", [G, 16, 64])
    C_im = din("s5_C_im", [G, 16, 64])
    s5_D = din("s5_D", [SW])
    w_glu = din("w_s5_glu", [SW, 2 * SW])
    w_pa = din("w_proj_a", [HW, D])
    w_pb = din("w_proj_b", [SW, D])
    w_out = din("w_out", [D, D])
    norm2_g = din("norm2_g", [D])
    w_up = din("w_ffn_up", [D, 2 * FH])
    w_down = din("w_ffn_down", [FH, D])
    fnorm_g = din("final_norm_g", [D])

    y_main = dout("y_main", [NMAIN * BT, D])
    y_s = dout("y_s", [NS, D])
    sh_p = dout("sh_p", [H, 128, 128])
    s5re_p = dout("s5re_p", [G, 64])
    s5im_p = dout("s5im_p", [G, 64])
    sh_s = dout("sh_s", [NS, H, 128, 128])
    s5re_s = dout("s5re_s", [NS, G, 64])
    s5im_s = dout("s5im_s", [NS, G, 64])

    P = Prog(nc)
    P.setup()
    ctxs = []

    def sb(name, shape, dt=F32):
        cm = nc.sbuf_tensor(name, list(shape), dt)
        t = cm.__enter__()
        ctxs.append(cm)
        return t

    NPS = 8
    psb = []
    for i in range(NPS):
        cm = nc.psum_tensor("ps%d" % i, [128, 512], F32)
        psb.append(cm.__enter__())
        ctxs.append(cm)
    ps_rr = [0]

    def ps():
        i = ps_rr[0]
        if P.dry:
            return psb[i], ("ps", i)
        ps_rr[0] = (i + 1) % (NPS - 1)
        return psb[i], ("ps", i)

    ident = sb("ident", [128, 128])
    ones = sb("ones", [128, 128])
    mut = sb("mut", [128, 128])
    mbc = sb("mbc", [128, 128])
    P.op("pool", lambda e: e.memset(ones[:], 1.0), writes=["ones"])
    P.op("pool", lambda e: e.memset(ident[:], 1.0), writes=["ident"])
    P.op("pool", lambda e: e.affine_select(out=ident[:], in_=ident[:], pattern=[[-1, 128]], compare_op=ALU.is_ge, fill=0.0, base=0, channel_multiplier=1), reads=["ident"], writes=["ident"])
    P.op("pool", lambda e: e.affine_select(out=ident[:], in_=ident[:], pattern=[[1, 128]], compare_op=ALU.is_ge, fill=0.0, base=0, channel_multiplier=-1), reads=["ident"], writes=["ident"])
    P.op("pool", lambda e: e.memset(mut[:], 1.0), writes=["mut"])
    P.op("pool", lambda e: e.affine_select(out=mut[:], in_=mut[:], pattern=[[1, 128]], compare_op=ALU.is_ge, fill=0.0, base=0, channel_multiplier=-1), reads=["mut"], writes=["mut"])
    P.op("pool", lambda e: e.memset(mbc[:], 1.0), writes=["mbc"])
    P.op("pool", lambda e: e.affine_select(out=mbc[:].rearrange("p (t q) -> p t q", q=16), in_=mbc[:].rearrange("p (t q) -> p t q", q=16), pattern=[[16, 8], [0, 16]], compare_op=ALU.is_ge, fill=0.0, base=15, channel_multiplier=-1), reads=["mbc"], writes=["mbc"])

    g1T = sb("g1T", [128, KD])
    g2T = sb("g2T", [128, KD])
    ghT = sb("ghT", [128, H])
    lbT = sb("lbT", [128, H])
    omlT = sb("omlT", [128, H])
    l1T = sb("l1T", [128, H])
    fgB = sb("fgB", [128, D])
    P.dma("sp", g1T[:], norm1_g.rearrange("(k p) -> p k", p=128), writes=["g1T"], allow_slow_non_contiguous=True)
    P.dma("sp", g2T[:], norm2_g.rearrange("(k p) -> p k", p=128), writes=["g2T"], allow_slow_non_contiguous=True)
    P.dma("sp", ghT[:], hgrn_g.rearrange("(k p) -> p k", p=128), writes=["ghT"], allow_slow_non_contiguous=True)
    P.dma("sp", lbT[:], lb_logits[0].rearrange("(k p) -> p k", p=128), writes=["lbT"], allow_slow_non_contiguous=True)
    P.dma("sp", l1T[:], lb_logits[1].rearrange("(k p) -> p k", p=128), writes=["l1T"], allow_slow_non_contiguous=True)
    P.dma("sp", fgB[:], fnorm_g.partition_broadcast(128), writes=["fgB"])
    P.op("dve", lambda e: e.tensor_tensor(out=lbT[:], in0=lbT[:], in1=l1T[:], op=ALU.subtract), reads=["lbT", "l1T"], writes=["lbT"])
    P.op("act", lambda e: e.activation(out=lbT[:], in_=lbT[:], func=AF.Sigmoid), reads=["lbT"], writes=["lbT"])
    P.op("dve", lambda e: e.tensor_scalar(out=omlT[:], in0=lbT[:], scalar1=-1.0, scalar2=1.0, op0=ALU.mult, op1=ALU.add), reads=["lbT"], writes=["omlT"])

    NW = 4
    NSTG = 2
    NBG = 1
    DEPTH = 3
    WSLOT = 16 * 128
    wring = sb("wring", [128, NW, WSLOT], BF16)
    wstage = sb("wstage", [128, NSTG, WSLOT])
    bgbuf = sb("bgbuf", [128, NBG, WSLOT], BF16)
    wspecs = []
    w_idx = [0]
    w_issued = [0]
    CAST_ENGS = ("act", "dve", "act", "dve", "act", "dve")

    wscr_state = {"tid": None, "ap": None, "seen": set(), "nfirst": 0, "bg": [], "bgpos": 0, "nbg": 0}
    blk_marks = []

    def setup_wscr():
        keys = {}
        tid = []
        for (src, kt, C) in wspecs:
            k = (str(src), kt, C)
            if k not in keys:
                keys[k] = len(keys)
            tid.append(keys[k])
        wscr_state["tid"] = tid
        npre_specs = blk_marks[NPRE] if len(blk_marks) > NPRE else len(wspecs)
        pre_t = set(tid[:npre_specs])
        bg = []
        for n in range(npre_specs, len(wspecs)):
            if tid[n] not in pre_t:
                bg.append(n)
                pre_t.add(tid[n])
        wscr_state["bg"] = bg
        wscr_state["ap"] = nc.dram_tensor("wscr", [len(keys), 128, WSLOT], BF16, kind="Internal").ap()

    def issue_load(n):
        src, kt, C = wspecs[n]
        i = n % NW
        t = wscr_state["tid"][n]
        wscr = wscr_state["ap"]
        view = wring[:, i, 0:kt * C].rearrange("p (k c) -> p k c", k=kt)
        if any(pt_[0] == t for pt_ in bg_pending):
            bg_flush()
        if t in wscr_state["seen"]:
            P.dma("sp", wring[:, i, 0:kt * C], wscr[t, :, 0:kt * C], reads=[("wscr", t)], writes=[("w", i)])
            return
        bg_flush()
        wscr_state["seen"].add(t)
        nf = wscr_state["nfirst"]
        wscr_state["nfirst"] = nf + 1
        j = nf % NSTG
        sview = wstage[:, j, 0:kt * C].rearrange("p (k c) -> p k c", k=kt)
        P.dma("sp", sview, src.rearrange("(k p) c -> p k c", p=128), writes=[("wst", j)])
        eng = "act"
        if eng == "act":
            P.op("act", lambda e: e.activation(out=view, in_=sview, func=AF.Copy), reads=[("wst", j)], writes=[("w", i)])
        else:
            P.op(eng, lambda e: e.tensor_copy(out=view, in_=sview), reads=[("wst", j)], writes=[("w", i)])
        P.dma(eng, wscr[t, :, 0:kt * C], wring[:, i, 0:kt * C], reads=[("w", i)], writes=[("wscr", t)])

    bg_pending = []

    def bg_flush():
        wscr = wscr_state["ap"]
        for (t, j, kt, C, q) in bg_pending:
            sv = wstage[:, j, 0:kt * C]
            bs = q % NBG
            bv = bgbuf[:, bs, 0:kt * C]
            if q % 2 == 0:
                P.op("act", lambda e: e.activation(out=bv, in_=sv, func=AF.Copy), reads=[("wst", j)], writes=[("bg", bs)])
            else:
                P.op("dve", lambda e: e.tensor_copy(out=bv, in_=sv), reads=[("wst", j)], writes=[("bg", bs)])
            P.dma("act", wscr[t, :, 0:kt * C], bv, reads=[("bg", bs)], writes=[("wscr", t)])
        del bg_pending[:]

    def bg_step(k):
        if P.dry or wscr_state["tid"] is None or k <= 0:
            return
        bg_flush()
        done = 0
        while done < min(k, NSTG) and wscr_state["bgpos"] < len(wscr_state["bg"]):
            n = wscr_state["bg"][wscr_state["bgpos"]]
            wscr_state["bgpos"] += 1
            t = wscr_state["tid"][n]
            if t in wscr_state["seen"]:
                continue
            src, kt, C = wspecs[n]
            wscr_state["seen"].add(t)
            nf = wscr_state["nfirst"]
            wscr_state["nfirst"] = nf + 1
            j = nf % NSTG
            q = wscr_state["nbg"]
            wscr_state["nbg"] = q + 1
            P.dma("act", wstage[:, j, 0:kt * C].rearrange("p (k c) -> p k c", k=kt), src.rearrange("(k p) c -> p k c", p=128), writes=[("wst", j)])
            bg_pending.append((t, j, kt, C, q))
            done += 1

    def load_w(src, kt, C):
        assert kt * C <= WSLOT
        if P.dry:
            wspecs.append((src, kt, C))
            return wring[:, 0, 0:kt * C].rearrange("p (k c) -> p k c", k=kt), ("w", 0)
        n = w_idx[0]
        w_idx[0] += 1
        assert wspecs[n][1] == kt and wspecs[n][2] == C
        while w_issued[0] < min(n + DEPTH + 1, len(wspecs)):
            issue_load(w_issued[0])
            w_issued[0] += 1
        i = n % NW
        return wring[:, i, 0:kt * C].rearrange("p (k c) -> p k c", k=kt), ("w", i)

    W1all = sb("W1all", [128, G, 128], BF16)
    W2all = sb("W2all", [128, G, 128], BF16)
    Toep = sb("Toep", [128, G, 128], BF16)
    A8re = sb("A8re", [128, G])
    A8im = sb("A8im", [128, G])
    Am7re = sb("Am7re", [128, G])
    Am7im = sb("Am7im", [128, G])
    Dcol = sb("Dcol", [128, G])
    NK = 24
    GB = 8
    xn = sb("xn", [128, max(D, 2048)])
    scrF = sb("scrF", [128, max((NT + 1) * D, 6144, 2 * G * NCHT)])
    smalls = sb("smalls", [128, 12, GB])
    lre, lim, dtb, zre, zim, zt1, zt2, zt3 = [smalls[:, i, :] for i in range(8)]
    kv = sb("kv", [128, NK])
    cnat = sb("cnat", [128, 2, 2, 64])

    def xnv(off, shape):
        n = 1
        for d in shape:
            n *= d
        v = xn[:, off:off + n]
        if len(shape) == 2:
            v = v.rearrange("p (a b) -> p a b", a=shape[0])
        return v
    tA = xnv(0, [GB, NK]); tB = xnv(192, [GB, NK]); tC = xnv(384, [GB, NK]); Are = xnv(576, [GB, NK]); Aim = xnv(768, [GB, NK])
    Bre_sb = xnv(960, [GB, 16]); Bim_sb = xnv(1088, [GB, 16]); Cre_sb = xnv(1216, [GB, 16]); Cim_sb = xnv(1344, [GB, 16])
    T1 = xnv(1472, [GB, 8]); T2 = xnv(1536, [GB, 8]); Et = xnv(1600, [GB, 8])
    U1 = xnv(1664, [GB, 16]); U2 = xnv(1792, [GB, 16])
    tmsk = xn[:, 1920:2048]
    Pb = scrF[:, 0:1024].rearrange("p (g s q) -> p g s q", g=8, s=8)
    Pb2 = scrF[:, 1024:2048].rearrange("p (g s q) -> p g s q", g=8, s=8)
    Qb = scrF[:, 2048:4096].rearrange("p (g s q) -> p g s q", g=8, s=16)
    Qb2 = scrF[:, 4096:6144].rearrange("p (g s q) -> p g s q", g=8, s=16)

    for s in range(8):
        P.dma("sp", Dcol[s * 16:(s + 1) * 16, :], s5_D.rearrange("(g p) -> p g", p=16), writes=["Dcol"], allow_slow_non_contiguous=True)
    powers = list(range(-7, 9)) + list(range(7, -1, -1))
    for j, pw in enumerate(powers):
        P.op("pool", lambda e, j=j, pw=pw: e.memset(kv[:, j:j + 1], float(pw)), writes=["kv"])
    MAGIC = 12582912.0
    tR = wstage[:, 0, 0:GB * NK].rearrange("p (g k) -> p g k", g=GB)
    for bq in range(G // GB):
        gs = slice(bq * GB, (bq + 1) * GB)
        for hf in range(2):
            hs = slice(hf * 64, hf * 64 + 64)
            P.dma("sp", lre[hs, :], lam_re[gs].rearrange("g n -> n g"), writes=["lre"], allow_slow_non_contiguous=True)
            P.dma("sp", lim[hs, :], lam_im[gs].rearrange("g n -> n g"), writes=["lim"], allow_slow_non_contiguous=True)
            P.dma("sp", Bre_sb[hs, :, :], B_re[gs].rearrange("g n p -> n g p"), writes=["Bre"])
            P.dma("sp", Bim_sb[hs, :, :], B_im[gs].rearrange("g n p -> n g p"), writes=["Bim"])
        P.dma("sp", dtb, log_dt[gs].partition_broadcast(128), writes=["dtb"])
        P.op("act", lambda e: e.activation(out=dtb, in_=dtb, func=AF.Exp), reads=["dtb"], writes=["dtb"])
        P.op("dve", lambda e: e.tensor_tensor(out=zt1, in0=lre, in1=dtb, op=ALU.mult), reads=["lre", "dtb"], writes=["zt1"])
        P.op("dve", lambda e: e.tensor_tensor(out=zt2, in0=lim, in1=dtb, op=ALU.mult), reads=["lim", "dtb"], writes=["zt2"])
        kvb = kv[:].unsqueeze(1).to_broadcast([128, GB, NK])
        P.op("dve", lambda e: e.tensor_tensor(out=tA, in0=zt1.unsqueeze(2).to_broadcast([128, GB, NK]), in1=kvb, op=ALU.mult), reads=["zt1", "kv"], writes=["tA"])
        P.op("act", lambda e: e.activation(out=tA, in_=tA, func=AF.Exp), reads=["tA"], writes=["tA"])
        P.op("dve", lambda e: e.tensor_tensor(out=tB, in0=zt2.unsqueeze(2).to_broadcast([128, GB, NK]), in1=kvb, op=ALU.mult), reads=["zt2", "kv"], writes=["tB"])
        P.op("dve", lambda e: e.tensor_scalar(out=tC, in0=tB, scalar1=1.0 / (2 * math.pi), scalar2=0.25, op0=ALU.mult, op1=ALU.add), reads=["tB"], writes=["tC"])
        P.op("dve", lambda e: e.tensor_scalar(out=tB, in0=tB, scalar1=1.0 / (2 * math.pi), scalar2=None, op0=ALU.mult), reads=["tB"], writes=["tB"])
        for T_, tk in ((tB, "tB"), (tC, "tC")):
            P.op("dve", lambda e, T_=T_: e.tensor_scalar(out=tR[:], in0=T_, scalar1=MAGIC, scalar2=None, op0=ALU.add), reads=[tk], writes=["tR"])
            P.op("dve", lambda e: e.tensor_scalar(out=tR[:], in0=tR[:], scalar1=-MAGIC, scalar2=None, op0=ALU.add), reads=["tR"], writes=["tR"])
            P.op("dve", lambda e, T_=T_: e.tensor_tensor(out=T_, in0=T_, in1=tR[:], op=ALU.subtract), reads=[tk, "tR"], writes=[tk])
            P.op("act", lambda e, T_=T_: e.activation(out=T_, in_=T_, func=AF.Sin, scale=2 * math.pi), reads=[tk], writes=[tk])
        P.op("dve", lambda e: e.tensor_tensor(out=Are, in0=tA, in1=tC, op=ALU.mult), reads=["tA", "tC"], writes=["Are"])
        P.op("dve", lambda e: e.tensor_tensor(out=Aim, in0=tA, in1=tB, op=ALU.mult), reads=["tA", "tB"], writes=["Aim"])
        P.op("dve", lambda e, gs=gs: e.tensor_copy(out=A8re[:, gs], in_=Are[:, :, 15]), reads=["Are"], writes=["A8re"])
        P.op("dve", lambda e, gs=gs: e.tensor_copy(out=A8im[:, gs], in_=Aim[:, :, 15]), reads=["Aim"], writes=["A8im"])
        P.op("dve", lambda e, gs=gs: e.tensor_copy(out=Am7re[:, gs], in_=Are[:, :, 0]), reads=["Are"], writes=["Am7re"])
        P.op("dve", lambda e, gs=gs: e.tensor_copy(out=Am7im[:, gs], in_=Aim[:, :, 0]), reads=["Aim"], writes=["Am7im"])
        P.op("dve", lambda e: e.tensor_scalar(out=zt1, in0=Are[:, :, 8], scalar1=-1.0, scalar2=None, op0=ALU.add), reads=["Are"], writes=["zt1"])
        P.op("dve", lambda e: e.tensor_tensor(out=zt2, in0=lre, in1=lre, op=ALU.mult), reads=["lre"], writes=["zt2"])
        P.op("dve", lambda e: e.tensor_tensor(out=zt3, in0=lim, in1=lim, op=ALU.mult), reads=["lim"], writes=["zt3"])
        P.op("dve", lambda e: e.tensor_tensor(out=zt2, in0=zt2, in1=zt3, op=ALU.add), reads=["zt2", "zt3"], writes=["zt2"])
        P.op("dve", lambda e: e.reciprocal(out=zt2, in_=zt2), reads=["zt2"], writes=["zt2"])
        P.op("dve", lambda e: e.tensor_tensor(out=zre, in0=zt1, in1=lre, op=ALU.mult), reads=["zt1", "lre"], writes=["zre"])
        P.op("dve", lambda e: e.tensor_tensor(out=zt3, in0=Aim[:, :, 8], in1=lim, op=ALU.mult), reads=["Aim", "lim"], writes=["zt3"])
        P.op("dve", lambda e: e.tensor_tensor(out=zre, in0=zre, in1=zt3, op=ALU.add), reads=["zre", "zt3"], writes=["zre"])
        P.op("dve", lambda e: e.tensor_tensor(out=zre, in0=zre, in1=zt2, op=ALU.mult), reads=["zre", "zt2"], writes=["zre"])
        P.op("dve", lambda e: e.tensor_tensor(out=zim, in0=Aim[:, :, 8], in1=lre, op=ALU.mult), reads=["Aim", "lre"], writes=["zim"])
        P.op("dve", lambda e: e.tensor_tensor(out=zt3, in0=zt1, in1=lim, op=ALU.mult), reads=["zt1", "lim"], writes=["zt3"])
        P.op("dve", lambda e: e.tensor_tensor(out=zim, in0=zim, in1=zt3, op=ALU.subtract), reads=["zim", "zt3"], writes=["zim"])
        P.op("dve", lambda e: e.tensor_tensor(out=zim, in0=zim, in1=zt2, op=ALU.mult), reads=["zim", "zt2"], writes=["zim"])
        zreb = zre.unsqueeze(2).to_broadcast([128, GB, 8])
        zimb = zim.unsqueeze(2).to_broadcast([128, GB, 8])
        P.op("dve", lambda e: e.tensor_tensor(out=T1, in0=Are[:, :, 16:24], in1=zreb, op=ALU.mult), reads=["Are", "zre"], writes=["T1"])
        P.op("dve", lambda e: e.tensor_tensor(out=Et, in0=Aim[:, :, 16:24], in1=zimb, op=ALU.mult), reads=["Aim", "zim"], writes=["Et"])
        P.op("dve", lambda e: e.tensor_tensor(out=T1, in0=T1, in1=Et, op=ALU.subtract), reads=["T1", "Et"], writes=["T1"])
        P.op("dve", lambda e: e.tensor_tensor(out=T2, in0=Are[:, :, 16:24], in1=zimb, op=ALU.mult), reads=["Are", "zim"], writes=["T2"])
        P.op("dve", lambda e: e.tensor_tensor(out=Et, in0=Aim[:, :, 16:24], in1=zreb, op=ALU.mult), reads=["Aim", "zre"], writes=["Et"])
        P.op("dve", lambda e: e.tensor_tensor(out=T2, in0=T2, in1=Et, op=ALU.add), reads=["T2", "Et"], writes=["T2"])
        P.op("dve", lambda e: e.tensor_copy(out=Et[64:128], in_=T1[64:128]), reads=["T1"], writes=["Et"])
        P.op("dve", lambda e: e.tensor_copy(out=T1[64:128], in_=T2[64:128]), reads=["T2"], writes=["T1"])
        P.op("dve", lambda e: e.tensor_copy(out=T2[64:128], in_=Et[64:128]), reads=["Et"], writes=["T2"])
        P.op("dve", lambda e: e.tensor_scalar(out=T2[0:64], in0=T2[0:64], scalar1=-1.0, scalar2=None, op0=ALU.mult), reads=["T2"], writes=["T2"])
        P.op("dve", lambda e: e.tensor_copy(out=U1[0:64], in_=Are[0:64, :, 0:16]), reads=["Are"], writes=["U1"])
        P.op("dve", lambda e: e.tensor_scalar(out=U1[64:128], in0=Aim[64:128, :, 0:16], scalar1=-1.0, scalar2=None, op0=ALU.mult), reads=["Aim"], writes=["U1"])
        P.op("dve", lambda e: e.tensor_scalar(out=U2[0:64], in0=Aim[0:64, :, 0:16], scalar1=-1.0, scalar2=None, op0=ALU.mult), reads=["Aim"], writes=["U2"])
        P.op("dve", lambda e: e.tensor_scalar(out=U2[64:128], in0=Are[64:128, :, 0:16], scalar1=-1.0, scalar2=None, op0=ALU.mult), reads=["Are"], writes=["U2"])
        for ci, (Csrc, Cdst, ckey) in enumerate(((C_re, Cre_sb, "Cre"), (C_im, Cim_sb, "Cim"))):
            rows = Csrc[gs].rearrange("g p n -> (g p) n")
            for dup in range(2):
                P.dma("sp", cnat[:, ci, dup, :], rows, writes=[("cnat", ci)])
            pt, pk = ps()
            P.op("pe", lambda e, ci=ci, pt=pt: e.transpose(out=pt[:, 0:128], in_=cnat[:, ci, :, :], identity=ident[:]), reads=[("cnat", ci), "ident"], writes=[pk])
            P.op("dve", lambda e, pt=pt, Cdst=Cdst: e.tensor_copy(out=Cdst, in_=pt[:, 0:128].rearrange("p (g q) -> p g q", q=16)), reads=[pk], writes=[ckey])
        P.op("dve", lambda e: e.tensor_tensor(out=Pb, in0=T1.unsqueeze(3).to_broadcast([128, 8, 8, 16]), in1=Bre_sb.unsqueeze(2).to_broadcast([128, 8, 8, 16]), op=ALU.mult), reads=["T1", "Bre"], writes=["Pb"])
        P.op("dve", lambda e: e.tensor_tensor(out=Pb2, in0=T2.unsqueeze(3).to_broadcast([128, 8, 8, 16]), in1=Bim_sb.unsqueeze(2).to_broadcast([128, 8, 8, 16]), op=ALU.mult), reads=["T2", "Bim"], writes=["Pb2"])
        P.op("dve", lambda e: e.tensor_tensor(out=Pb, in0=Pb, in1=Pb2, op=ALU.add), reads=["Pb", "Pb2"], writes=["Pb"])
        P.op("dve", lambda e: e.tensor_tensor(out=Qb, in0=U1.unsqueeze(3).to_broadcast([128, 8, 16, 16]), in1=Cre_sb.unsqueeze(2).to_broadcast([128, 8, 16, 16]), op=ALU.mult), reads=["U1", "Cre"], writes=["Qb"])
        P.op("dve", lambda e: e.tensor_tensor(out=Qb2, in0=U2.unsqueeze(3).to_broadcast([128, 8, 16, 16]), in1=Cim_sb.unsqueeze(2).to_broadcast([128, 8, 16, 16]), op=ALU.mult), reads=["U2", "Cim"], writes=["Qb2"])
        P.op("dve", lambda e: e.tensor_tensor(out=Qb, in0=Qb, in1=Qb2, op=ALU.add), reads=["Qb", "Qb2"], writes=["Qb"])
        for gl in range(8):
            g = bq * 8 + gl
            P.op("act", lambda e, gl=gl, g=g: e.activation(out=W2all[:, g, :].rearrange("p (t q) -> p t q", q=16), in_=Qb[:, gl, 8:16, :], func=AF.Copy), reads=["Qb"], writes=["W2all"])
            pt, pk = ps()
            P.op("pe", lambda e, gl=gl, pt=pt: e.transpose(out=pt[:, 0:128], in_=Pb[:, gl, :, :], identity=ident[:]), reads=["Pb", "ident"], writes=[pk])
            P.op("act", lambda e, g=g, pt=pt: e.activation(out=W1all[:, g, :], in_=pt[:, 0:128], func=AF.Copy), reads=[pk], writes=["W1all"])
            pt2, pk2 = ps()
            P.op("pe", lambda e, gl=gl, pt2=pt2: e.matmul(pt2[:, 0:128], lhsT=Pb[:, gl, :, :], rhs=Qb[:, gl, 0:8, :], start=True, stop=True), reads=["Pb", "Qb"], writes=[pk2])
            P.op("dve", lambda e, pt2=pt2: e.tensor_tensor(out=tmsk, in0=pt2[:, 0:128], in1=mbc[:], op=ALU.mult), reads=[pk2, "mbc"], writes=["tmsk"])
            P.op("dve", lambda e, g=g: e.scalar_tensor_tensor(out=Toep[:, g, :], in0=ident[:], scalar=Dcol[:, g:g + 1], in1=tmsk, op0=ALU.mult, op1=ALU.add), reads=["ident", "Dcol", "tmsk"], writes=["Toep"])
    P.barrier()

    Pswap = sb("Pswap", [128, 128])
    Bcat = sb("Bcat", [128, G])
    Bswp = sb("Bswp", [128, G])
    P.op("pool", lambda e: e.memset(Pswap[:], 0.0), writes=["Pswap"])
    P.op("pool", lambda e: e.affine_select(out=Pswap[:], in_=Pswap[:], pattern=[[1, 128]], compare_op=ALU.not_equal, fill=1.0, base=-64, channel_multiplier=-1), reads=["Pswap"], writes=["Pswap"])
    P.op("pool", lambda e: e.affine_select(out=Pswap[:], in_=Pswap[:], pattern=[[1, 128]], compare_op=ALU.not_equal, fill=1.0, base=64, channel_multiplier=-1), reads=["Pswap"], writes=["Pswap"])
    P.op("dve", lambda e: e.tensor_copy(out=Bcat[:], in_=A8im[:]), reads=["A8im"], writes=["Bcat"])
    P.op("dve", lambda e: e.tensor_scalar(out=Bcat[0:64], in0=Bcat[0:64], scalar1=-1.0, scalar2=None, op0=ALU.mult), reads=["Bcat"], writes=["Bcat"])
    P.op("dve", lambda e: e.tensor_scalar(out=Bswp[:], in0=Bcat[:], scalar1=-1.0, scalar2=None, op0=ALU.mult), reads=["Bcat"], writes=["Bswp"])
    S = sb("S", [128, H, 128])
    Sbf = sb("Sbf", [128, H, 128], BF16)
    Wd = sb("Wd", [128, 3, G])
    A2 = sb("A2", [128, 2, G])
    B2 = sb("B2", [128, 2, G])
    P.op("dve", lambda e: e.tensor_copy(out=A2[:, 0, :], in_=A8re[:]), reads=["A8re"], writes=["A2"])
    P.op("dve", lambda e: e.tensor_copy(out=A2[:, 1, :], in_=A8re[:]), reads=["A8re"], writes=["A2"])
    P.op("dve", lambda e: e.tensor_copy(out=B2[:, 0, :], in_=Bcat[:]), reads=["Bcat"], writes=["B2"])
    P.op("dve", lambda e: e.tensor_copy(out=B2[:, 1, :], in_=Bswp[:]), reads=["Bswp"], writes=["B2"])
    P.op("pool", lambda e: e.memset(S[:], 0.0), writes=[("S", h) for h in range(H)])
    P.op("pool", lambda e: e.memset(Sbf[:], 0.0), writes=[("Sbf", h) for h in range(H)])
    P.op("pool", lambda e: e.memset(Wd[:], 0.0), writes=["Wd", "Wd2"])

    hT = sb("hT", [128, KD, TB], BF16)
    ss = sb("ss", [128, 4])
    qT = sb("qT", [128, TB])
    fT = sb("fT", [128, TB])
    ogs = sb("ogs", [128, TB])
    itok = sb("itok", [128, NT, 128], BF16)
    is32 = sb("is32", [16, 128])
    lg2 = sb("lg", [128, 2, 128])
    bcs2 = sb("bcs", [128, 2, 128])
    ex4 = sb("ex", [128, 4, 128])
    kk2 = sb("kk", [128, 2, 128])
    ku2 = sb("ku", [128, 2, 128])
    nb2 = sb("nb", [128, 2, 2])
    qs2 = sb("qs", [128, 2, 128], BF16)
    ks2 = sb("ks", [128, 2, 128], BF16)
    qin2 = sb("qin", [128, 2, 128], BF16)
    kutok2 = sb("kutok", [128, 2, 128], BF16)
    scT2 = sb("scT", [128, 2, 128], BF16)
    ex_rr = [0]
    sload_next = [0]
    kks = sb("kks", [128, 16])
    ktok = sb("ktok", [16, 128])
    km = xn[0:16, 0:2048].rearrange("p (b k) -> p b k", b=16)
    NSB = 4
    s0b = sb("s0b", [128, NSB, 128])
    snb = sb("snb", [128, NSB, 128])
    oaT = sb("oaT", [128, H, TB], BF16)
    ucm = sb("ucm", [128, 8, 8, 16])
    ucms = xn[0:16, 0:1024].rearrange("p (g s q) -> p g s q", g=8, s=8)
    Uall = sb("Uall", [128, G, NCHT], BF16)
    XR = scrF[:, 0:G * NCHT].rearrange("p (g c) -> p g c", g=G)
    XI = scrF[:, G * NCHT:2 * G * NCHT].rearrange("p (g c) -> p g c", g=G)
    Hin = sb("Hin", [128, G, NCHT], BF16)
    st1 = sb("st1", [128, 2, G])
    st2 = sb("st2", [128, 2, G])
    X2v = scrF[:, 0:2 * G * NCHT].rearrange("p (r g c) -> p r g c", r=2, g=G)
    Zg = sb("Zg", [128, 2, NCHT])
    zcm = ucm[:].rearrange("p a b c -> p a (b c)")
    zcms = sb("zcms", [16, 128])
    zT = sb("zT", [128, GT, TB], BF16)
    obT = sb("obT", [128, GT, TB], BF16)
    h0nat = cnat
    HS_ALIAS = TB >= 256
    hsT = sb("hsT", [128, 2, 64])
    sgt = sb("sgt", [128, 2, TB])
    mt1 = sb("mt1", [128, TB])
    mt2 = sb("mt2", [128, TB])
    NTT = NT + 1
    x2 = scrF[:, 0:NTT * D].rearrange("p (i d) -> p i d", i=NTT)
    X2K = [("x2", i) for i in range(NTT)]
    osq = sgt[:, 0, :]
    if HS_ALIAS:
        hs0R = sgt[:, 0, 0:128].rearrange("p (g b) -> p g b", g=8)
        hs0I = sgt[:, 0, 128:256].rearrange("p (g b) -> p g b", g=8)
        hsR = sgt[:, 1, 0:128].rearrange("p (g b) -> p g b", g=8)
        hsI = sgt[:, 1, 128:256].rearrange("p (g b) -> p g b", g=8)
        KH = {"hs0R": ("sgt", 0), "hs0I": ("sgt", 0), "hsR": ("sgt", 1), "hsI": ("sgt", 1)}
    else:
        hs0R = sb("hs0R", [128, 8, NS])
        hs0I = sb("hs0I", [128, 8, NS])
        hsR = sb("hsR", [128, 8, NS])
        hsI = sb("hsI", [128, 8, NS])
        KH = {"hs0R": "hs0R", "hs0I": "hs0I", "hsR": "hsR", "hsI": "hsI"}
    rn = mt1
    otmp = mt2
    NHH = 4 if FT >= 8 else 2
    FTH = (FT + NHH - 1) // NHH
    mh = sb("mh", [128, max(KD, FTH), TB], BF16)
    mergedT = mh[:, 0:KD, :]
    hid = mh[:, 0:FTH, :]
    junk = mh[:].rearrange("p a b -> p (a b)")
    yt = xn

    def tiles(has_s):
        lst = [(i, 128, slice(i * 128, (i + 1) * 128)) for i in range(NT)]
        if has_s:
            lst.append((NT, NS, slice(BT, BT + NS)))
        return lst

    def rms_to_hT(src_fn, tl, gT, tag):
        for (i, rows, cols) in tl:
            src, skey = src_fn(i, rows)
            P.op("act", lambda e, src=src, rows=rows: e.activation(out=junk[0:rows, 0:D], in_=src, func=AF.Square, accum_out=ss[0:rows, 0:1]), reads=[skey], writes=["mh", "ss"])
            P.op("dve", lambda e, rows=rows: e.tensor_scalar(out=ss[0:rows, 1:2], in0=ss[0:rows, 0:1], scalar1=1.0 / D, scalar2=EPS, op0=ALU.mult, op1=ALU.add), reads=["ss"], writes=["ss"])
            P.op("act", lambda e, rows=rows: e.activation(out=ss[0:rows, 1:2], in_=ss[0:rows, 1:2], func=AF.Sqrt), reads=["ss"], writes=["ss"])
            P.op("dve", lambda e, rows=rows: e.reciprocal(out=ss[0:rows, 1:2], in_=ss[0:rows, 1:2]), reads=["ss"], writes=["ss"])
            P.op("dve", lambda e, src=src, rows=rows: e.tensor_scalar(out=xn[0:rows, 0:D], in0=src, scalar1=ss[0:rows, 1:2], scalar2=None, op0=ALU.mult), reads=["ss", skey], writes=["xn"])
            for k0 in range(0, KD, 4):
                nk = min(4, KD - k0)
                pt, pk = ps()
                for kq in range(nk):
                    k = k0 + kq
                    P.op("pe", lambda e, k=k, kq=kq, pt=pt, rows=rows: e.transpose(out=pt[:, kq * 128:kq * 128 + rows], in_=xn[0:rows, k * 128:(k + 1) * 128], identity=ident[0:rows, 0:rows]), reads=["xn", "ident"], writes=[pk], track=(kq == nk - 1))
                P.op("dve", lambda e, k0=k0, nk=nk, pt=pt, rows=rows, cols=cols: e.tensor_tensor(
                    out=hT[:, k0:k0 + nk, cols],
                    in0=pt[:, 0:nk * 128].rearrange("p (k c) -> p k c", k=nk)[:, :, 0:rows],
                    in1=gT[:, k0:k0 + nk].unsqueeze(2).to_broadcast([128, nk, rows]), op=ALU.mult), reads=[pk, tag], writes=["hT"])

    def proj_fm(wv, wk, kt, rhs_fn, ncols, rkeys):
        pt, pk = ps()
        for k in range(kt):
            P.op("pe", lambda e, k=k, pt=pt: e.matmul(pt[:, 0:ncols], lhsT=wv[:, k, :], rhs=rhs_fn(k), start=(k == 0), stop=(k == kt - 1)), reads=[wk] + rkeys, writes=[pk], track=(k == kt - 1))
        return pt, pk

    def do_block(xsrc, mode, has_s, ydst, last):
        main = (mode == "M")
        if P.dry:
            blk_marks.append(len(wspecs))
        bgq = (BGQ_PRE if not main else (BGQ_MAIN0 if has_s else 0))
        ncol = TB if has_s else BT
        tl = tiles(has_s)
        nchn = NCHT if has_s else NCH

        def src1(i, rows):
            if rows == 128:
                P.dma("sp", xn[:, 0:D], xsrc[i * 128:(i + 1) * 128, :], writes=["xn"])
            else:
                P.dma("sp", xn[0:rows, 0:D], xs, writes=["xn"])
            return xn[0:rows, 0:D], "xn"
        rms_to_hT(src1, tl, g1T, "g1T")
        hrhs = lambda k: hT[:, k, 0:ncol]

        if has_s:
            P.op("pool", lambda e: e.memset(ucms, 0.0), writes=["xn"])
        for ct in range(GT):
            bg_step(bgq)
            wu, wuk = load_w(w_in[:, 4 * HW + ct * 128:4 * HW + (ct + 1) * 128], KD, 128)
            pu, puk = proj_fm(wu, wuk, KD, hrhs, ncol, ["hT"])
            P.op("act", lambda e, pu=pu: e.activation(out=qT[:, 0:ncol], in_=pu[:, 0:ncol], func=AF.Copy), reads=[puk], writes=["qT"])
            for s0 in range(0, 8, 4):
                pt, pk = ps()
                for sq in range(4):
                    s = s0 + sq
                    P.op("pe", lambda e, s=s, sq=sq, pt=pt: e.transpose(out=pt[0:NCH, sq * 128:(sq + 1) * 128], in_=qT[:, 0:BT].rearrange("p (c s) -> p s c", s=8)[:, s, :], identity=ident[:]), reads=["qT", "ident"], writes=[pk], track=(sq == 3))
                P.op("act", lambda e, s0=s0, pt=pt: e.activation(out=ucm[0:NCH, :, s0:s0 + 4, :], in_=pt[0:NCH, :].rearrange("p (s g q) -> p g s q", s=4, g=8), func=AF.Copy), reads=[pk], writes=["ucm"])
            if has_s:
                pt, pk = ps()
                P.op("pe", lambda e, pt=pt: e.transpose(out=pt[0:NS, 0:128], in_=qT[:, BT:BT + NS], identity=ident[:]), reads=["qT", "ident"], writes=[pk])
                P.op("act", lambda e, pt=pt: e.activation(out=ucms[:, :, 7, :], in_=pt[0:NS, 0:128].rearrange("p (g q) -> p g q", g=8), func=AF.Copy), reads=[pk], writes=["xn"])
            assert 8 * NCHT <= 512
            pt, pk = ps()
            for gl in range(8):
                P.op("pe", lambda e, gl=gl, pt=pt: e.transpose(out=pt[:, gl * NCHT:gl * NCHT + NCH], in_=ucm[0:NCH, gl, :, :], identity=ident[0:NCH, 0:NCH]), reads=["ucm", "ident"], writes=[pk], track=(gl == 7 and not has_s))
            if has_s:
                for gl in range(8):
                    P.op("pe", lambda e, gl=gl, pt=pt: e.transpose(out=pt[:, gl * NCHT + NCH:(gl + 1) * NCHT], in_=ucms[:, gl, :, :], identity=ident[0:NS, 0:NS]), reads=["xn", "ident"], writes=[pk], track=(gl == 7))
            P.op("dve", lambda e, ct=ct, pt=pt: e.tensor_copy(out=Uall[:, ct * 8:(ct + 1) * 8, 0:nchn], in_=pt[:, 0:8 * NCHT].rearrange("p (g c) -> p g c", g=8)[:, :, 0:nchn]), reads=[pk], writes=[("Uall", ct)])
            pr, prk = ps()
            for gl in range(8):
                g = ct * 8 + gl
                P.op("pe", lambda e, g=g, gl=gl, pr=pr: e.matmul(pr[:, gl * NCHT:gl * NCHT + nchn], lhsT=W1all[:, g, :], rhs=Uall[:, g, 0:nchn], start=True, stop=True), reads=["W1all", ("Uall", ct)], writes=[prk], track=(gl == 7))
            P.op("act", lambda e, ct=ct, pr=pr: e.activation(out=XR[:, ct * 8:(ct + 1) * 8, 0:nchn], in_=pr[:, 0:8 * NCHT].rearrange("p (g c) -> p g c", g=8)[:, :, 0:nchn], func=AF.Copy), reads=[prk], writes=["XR"] + X2K)
        XRf = scrF[:, 0:G * NCHT]
        XIf = scrF[:, G * NCHT:2 * G * NCHT]
        for c0 in range(0, G * NCHT, 512):
            cw = min(512, G * NCHT - c0)
            pw_, pwk = ps()
            P.op("pe", lambda e, c0=c0, cw=cw, pw_=pw_: e.matmul(pw_[:, 0:cw], lhsT=Pswap[:], rhs=XRf[:, c0:c0 + cw], start=True, stop=True), reads=["Pswap", "XR"], writes=[pwk])
            P.op("act", lambda e, c0=c0, cw=cw, pw_=pw_: e.activation(out=XIf[:, c0:c0 + cw], in_=pw_[:, 0:cw], func=AF.Copy), reads=[pwk], writes=["XI"] + X2K)
        if main:
            P.op("act", lambda e: e.activation(out=Hin[:, :, 0], in_=Wd[:, 2, :], func=AF.Copy), reads=["Wd2"], writes=["Hin"])
        for c in range(NCH):
            P.op("pool", lambda e: e.tensor_tensor(out=st2[:], in0=B2[:], in1=Wd[:, 1:3, :], op=ALU.mult), reads=["B2", "Wd", "Wd2"], writes=["st2"])
            P.op("pool", lambda e, c=c: e.tensor_tensor(out=st2[:], in0=st2[:], in1=X2v[:, :, :, c], op=ALU.add), reads=["st2", "XR", "XI"], writes=["st2"])
            P.op("pool", lambda e: e.tensor_tensor(out=st1[:], in0=A2[:], in1=Wd[:, 0:2, :], op=ALU.mult), reads=["A2", "Wd"], writes=["st1"])
            P.op("pool", lambda e: e.tensor_tensor(out=Wd[:, 0:2, :], in0=st1[:], in1=st2[:], op=ALU.add), reads=["st1", "st2"], writes=["Wd"])
            P.op("pool", lambda e: e.tensor_copy(out=Wd[:, 2, :], in_=Wd[:, 0, :]), reads=["Wd"], writes=["Wd2"])
            if main and c + 1 < NCH:
                P.op("act", lambda e, c=c: e.activation(out=Hin[:, :, c + 1], in_=Wd[:, 2, :], func=AF.Copy), reads=["Wd2"], writes=["Hin"])
        if main and has_s:
            assert G % 64 == 0 or G * 2 <= 128
            NPAIR = NS // 2
            GG = G
            PW = 2 * GG
            v3 = lambda t: t[:].rearrange("p g b -> p (g b)")[:, 0:PW].rearrange("p (b g) -> p b g", b=2)
            hs0Rv, hs0Iv, hsRv, hsIv = v3(hs0R), v3(hs0I), v3(hsR), v3(hsI)
            bcA = lambda t: t[:, :].unsqueeze(1).to_broadcast([128, 2, GG])
            for bh in range(NPAIR):
                scol = slice(NCH + 2 * bh, NCH + 2 * bh + 2)
                for ci, (src, dst, dkey) in enumerate(((s5re0, hs0Rv, KH["hs0R"]), (s5im0, hs0Iv, KH["hs0I"]))):
                    P.dma("sp", h0nat[0:PW, ci, 0, :], src[2 * bh:2 * bh + 2, :, :].rearrange("b g n -> (b g) n"), writes=[("h0nat", ci)])
                    P.op("act", lambda e, ci=ci: e.activation(out=h0nat[0:PW, ci, 1, :], in_=h0nat[0:PW, ci, 0, :], func=AF.Copy), reads=[("h0nat", ci)], writes=[("h0nat", ci)])
                    pt, pk = ps()
                    P.op("pe", lambda e, ci=ci, pt=pt: e.transpose(out=pt[:, 0:PW], in_=h0nat[0:PW, ci, :, :], identity=ident[0:PW, 0:PW]), reads=[("h0nat", ci), "ident"], writes=[pk])
                    P.op("dve", lambda e, pt=pt, dst=dst: e.tensor_copy(out=dst, in_=pt[:, 0:PW].rearrange("p (b g) -> p b g", b=2)), reads=[pk], writes=[dkey])
                P.op("dve", lambda e: e.tensor_tensor(out=hsRv, in0=hs0Rv, in1=bcA(Am7re), op=ALU.mult), reads=[KH["hs0R"], "Am7re"], writes=[KH["hsR"]])
                P.op("dve", lambda e: e.tensor_tensor(out=hsIv, in0=hs0Iv, in1=bcA(Am7im), op=ALU.mult), reads=[KH["hs0I"], "Am7im"], writes=[KH["hsI"]])
                P.op("dve", lambda e: e.tensor_tensor(out=hsRv, in0=hsRv, in1=hsIv, op=ALU.subtract), reads=[KH["hsR"], KH["hsI"]], writes=[KH["hsR"]])
                P.op("dve", lambda e: e.tensor_tensor(out=hsIv, in0=hs0Rv, in1=bcA(Am7im), op=ALU.mult), reads=[KH["hs0R"], "Am7im"], writes=[KH["hsI"]])
                P.op("dve", lambda e: e.tensor_tensor(out=hs0Rv, in0=hs0Iv, in1=bcA(Am7re), op=ALU.mult), reads=[KH["hs0I"], "Am7re"], writes=[KH["hs0R"]])
                P.op("dve", lambda e: e.tensor_tensor(out=hsIv, in0=hsIv, in1=hs0Rv, op=ALU.add), reads=[KH["hsI"], KH["hs0R"]], writes=[KH["hsI"]])
                P.op("act", lambda e, scol=scol: e.activation(out=Hin[0:64, :, scol], in_=hsRv[0:64].rearrange("p b g -> p g b"), func=AF.Copy), reads=[KH["hsR"]], writes=["Hin"])
                P.op("act", lambda e, scol=scol: e.activation(out=Hin[64:128, :, scol], in_=hsIv[64:128].rearrange("p b g -> p g b"), func=AF.Copy), reads=[KH["hsI"]], writes=["Hin"])
                P.op("dve", lambda e: e.tensor_tensor(out=hs0Rv, in0=hsRv, in1=bcA(A8re), op=ALU.mult), reads=[KH["hsR"], "A8re"], writes=[KH["hs0R"]])
                P.op("dve", lambda e: e.tensor_tensor(out=hs0Iv, in0=hsIv, in1=bcA(A8im), op=ALU.mult), reads=[KH["hsI"], "A8im"], writes=[KH["hs0I"]])
                P.op("dve", lambda e: e.tensor_tensor(out=hs0Rv, in0=hs0Rv, in1=hs0Iv, op=ALU.subtract), reads=[KH["hs0R"], KH["hs0I"]], writes=[KH["hs0R"]])
                P.op("dve", lambda e: e.tensor_tensor(out=hs0Iv, in0=hsRv, in1=bcA(A8im), op=ALU.mult), reads=[KH["hsR"], "A8im"], writes=[KH["hs0I"]])
                P.op("dve", lambda e, scol=scol: e.tensor_tensor(out=hsRv, in0=hs0Rv, in1=XR[:, :, scol].rearrange("p g b -> p b g"), op=ALU.add), reads=[KH["hs0R"], "XR"], writes=[KH["hsR"]])
                P.op("dve", lambda e: e.tensor_tensor(out=hs0Rv, in0=hsIv, in1=bcA(A8re), op=ALU.mult), reads=[KH["hsI"], "A8re"], writes=[KH["hs0R"]])
                P.op("dve", lambda e: e.tensor_tensor(out=hs0Iv, in0=hs0Iv, in1=hs0Rv, op=ALU.add), reads=[KH["hs0I"], KH["hs0R"]], writes=[KH["hs0I"]])
                P.op("dve", lambda e, scol=scol: e.tensor_tensor(out=hsIv, in0=hs0Iv, in1=XI[:, :, scol].rearrange("p g b -> p b g"), op=ALU.add), reads=[KH["hs0I"], "XI"], writes=[KH["hsI"]])
                for ci, (srct, dsto, skey) in enumerate(((hsRv, s5re_s, KH["hsR"]), (hsIv, s5im_s, KH["hsI"]))):
                    pt, pk = ps()
                    P.op("pe", lambda e, srct=srct, pt=pt: e.transpose(out=pt[0:PW, 0:64], in_=srct[0:64, :, :], identity=ident[0:64, 0:64]), reads=[skey, "ident"], writes=[pk])
                    P.op("dve", lambda e, ci=ci, pt=pt: e.tensor_copy(out=hsT[0:PW, ci, :], in_=pt[0:PW, 0:64]), reads=[pk], writes=[("hsT", ci)])
                    P.dma("sp", dsto[2 * bh:2 * bh + 2, :, :].rearrange("b g n -> (b g) n"), hsT[0:PW, ci, :], reads=[("hsT", ci)], is_output=True)
        for h in range(H):
            bg_step(bgq)
            c0 = h * 128
            if main:
                wq, wqk = load_w(w_in[:, c0:c0 + 128], KD, 128)
                pq, pqk = proj_fm(wq, wqk, KD, hrhs, ncol, ["hT"])
                P.op("act", lambda e, pq=pq: e.activation(out=qT[:, 0:ncol], in_=pq[:, 0:ncol], func=AF.Copy), reads=[pqk], writes=["qT"])
            wf, wfk = load_w(w_in[:, HW + c0:HW + c0 + 128], KD, 128)
            pf, pfk = proj_fm(wf, wfk, KD, hrhs, ncol, ["hT"])
            P.op("act", lambda e, pf=pf: e.activation(out=fT[:, 0:ncol], in_=pf[:, 0:ncol], func=AF.Sigmoid), reads=[pfk], writes=["fT"])
            P.op("dve", lambda e, h=h: e.tensor_scalar(out=fT[:, 0:ncol], in0=fT[:, 0:ncol], scalar1=omlT[:, h:h + 1], scalar2=lbT[:, h:h + 1], op0=ALU.mult, op1=ALU.add), reads=["fT", "omlT", "lbT"], writes=["fT"])
            wi, wik = load_w(w_in[:, 2 * HW + c0:2 * HW + c0 + 128], KD, 128)
            for (i, rows, cols) in tl:
                pt, pk = ps()
                for k in range(KD):
                    P.op("pe", lambda e, k=k, pt=pt, rows=rows, cols=cols: e.matmul(pt[0:rows, 0:128], lhsT=hT[:, k, cols], rhs=wi[:, k, :], start=(k == 0), stop=(k == KD - 1)), reads=[wik, "hT"], writes=[pk], track=(k == KD - 1))
                if rows == 128:
                    P.op("act", lambda e, pt=pt, i=i: e.activation(out=itok[:, i, :], in_=pt[:, 0:128], func=AF.Copy), reads=[pk], writes=["itok"])
                else:
                    P.op("act", lambda e, pt=pt: e.activation(out=is32[:, :], in_=pt[0:NS, 0:128], func=AF.Copy), reads=[pk], writes=["is32"])
            if main:
                wo, wok = load_w(w_in[:, 3 * HW + c0:3 * HW + c0 + 128], KD, 128)
                po_, pok = proj_fm(wo, wok, KD, hrhs, ncol, ["hT"])
                P.op("act", lambda e, po_=po_: e.activation(out=ogs[:, 0:ncol], in_=po_[:, 0:ncol], func=AF.Silu), reads=[pok], writes=["ogs"])
                pO, pOk = psb[NPS - 1], ("ps", NPS - 1)
            for j in range(NT):
                bg_step(bgq)
                cols = slice(j * 128, (j + 1) * 128)
                p = (h * NT + j) % 2
                lg, bcs, kk, ku, nb = lg2[:, p, :], bcs2[:, p, :], kk2[:, p, :], ku2[:, p, :], nb2[:, p, :]
                qs, ks, qin, kutok, scT = qs2[:, p, :], ks2[:, p, :], qin2[:, p, :], kutok2[:, p, :], scT2[:, p, :]
                Klg, Kbcs, Kkk, Kku, Knb = ("lg", p), ("bcs", p), ("kk", p), ("ku", p), ("nb", p)
                Kqs, Kks, Kqin, Kkutok, KscT = ("qs", p), ("ks", p), ("qin", p), ("kutok", p), ("scT", p)
                KS, KSbf = ("S", h), ("Sbf", h)

                def exbuf():
                    i = ex_rr[0]
                    if not P.dry:
                        ex_rr[0] = (i + 1) % 4
                    return ex4[:, i, :], ("ex", i)
                P.op("act", lambda e, cols=cols: e.activation(out=lg, in_=fT[:, cols], func=AF.Ln), reads=["fT"], writes=[Klg])
                P.op("dve", lambda e: e.tensor_tensor_scan(out=bcs, data0=ones[:], data1=lg, initial=0.0, op0=ALU.mult, op1=ALU.add), reads=["ones", Klg], writes=[Kbcs])
                P.op("dve", lambda e, cols=cols: e.tensor_scalar(out=kk, in0=fT[:, cols], scalar1=-1.0, scalar2=1.0, op0=ALU.mult, op1=ALU.add), reads=["fT"], writes=[Kkk])
                e4, Ke4 = exbuf()
                P.op("act", lambda e: e.activation(out=e4, in_=bcs, func=AF.Exp, bias=bcs[:, 127:128], scale=-1.0), reads=[Kbcs], writes=[Ke4])
                P.op("dve", lambda e: e.tensor_tensor(out=ku, in0=kk, in1=e4, op=ALU.mult), reads=[Kkk, Ke4], writes=[Kku])
                P.op("act", lambda e: e.activation(out=nb[:, 1:2], in_=bcs[:, 127:128], func=AF.Exp), reads=[Kbcs], writes=[Knb])
                ptk, ptkk = ps()
                P.op("pe", lambda e, ptk=ptk: e.transpose(out=ptk[:, 0:128], in_=ku, identity=ident[:]), reads=[Kku, "ident"], writes=[ptkk])
                P.op("act", lambda e, ptk=ptk: e.activation(out=kutok, in_=ptk[:, 0:128], func=AF.Copy), reads=[ptkk], writes=[Kkutok])
                if main:
                    P.op("dve", lambda e: e.tensor_scalar(out=nb[:, 0:1], in0=bcs[:, 64:65], scalar1=-1.0, scalar2=None, op0=ALU.mult), reads=[Kbcs], writes=[Knb])
                    e1, Ke1 = exbuf()
                    P.op("act", lambda e: e.activation(out=e1, in_=bcs, func=AF.Exp, bias=nb[:, 0:1], scale=1.0), reads=[Kbcs, Knb], writes=[Ke1])
                    P.op("dve", lambda e, cols=cols: e.tensor_tensor(out=qs, in0=qT[:, cols], in1=e1, op=ALU.mult), reads=["qT", Ke1], writes=[Kqs])
                    e2, Ke2 = exbuf()
                    P.op("act", lambda e: e.activation(out=e2, in_=bcs, func=AF.Exp, bias=bcs[:, 64:65], scale=-1.0), reads=[Kbcs], writes=[Ke2])
                    P.op("dve", lambda e: e.tensor_tensor(out=ks, in0=kk, in1=e2, op=ALU.mult), reads=[Kkk, Ke2], writes=[Kks])
                    e3, Ke3 = exbuf()
                    P.op("act", lambda e: e.activation(out=e3, in_=bcs, func=AF.Exp), reads=[Kbcs], writes=[Ke3])
                    P.op("dve", lambda e, cols=cols: e.tensor_tensor(out=qin, in0=qT[:, cols], in1=e3, op=ALU.mult), reads=["qT", Ke3], writes=[Kqin])
                    psc, psck = ps()
                    P.op("pe", lambda e, psc=psc: e.matmul(psc[:, 0:128], lhsT=ks, rhs=qs, start=True, stop=True), reads=[Kks, Kqs], writes=[psck])
                    P.op("dve", lambda e, psc=psc: e.tensor_tensor(out=scT, in0=psc[:, 0:128], in1=mut[:], op=ALU.mult), reads=[psck, "mut"], writes=[KscT])
                    P.op("pe", lambda e, j=j, cols=cols: e.matmul(pO[:, cols], lhsT=itok[:, j, :], rhs=scT, start=True, stop=False), reads=["itok", KscT], writes=[pOk], track=False)
                    P.op("pe", lambda e, h=h, cols=cols: e.matmul(pO[:, cols], lhsT=Sbf[:, h, :], rhs=qin, start=False, stop=True), reads=[KSbf, Kqin], writes=[pOk])
                pS, pSk = ps()
                P.op("pe", lambda e, j=j, pS=pS: e.matmul(pS[:, 0:128], lhsT=kutok, rhs=itok[:, j, :], start=True, stop=True), reads=[Kkutok, "itok"], writes=[pSk])
                P.op("dve", lambda e, h=h, pS=pS: e.scalar_tensor_tensor(out=S[:, h, :], in0=S[:, h, :], scalar=nb[:, 1:2], in1=pS[:, 0:128], op0=ALU.mult, op1=ALU.add), reads=[KS, Knb, pSk], writes=[KS])
                P.op("act", lambda e, h=h: e.activation(out=Sbf[:, h, :], in_=S[:, h, :], func=AF.Copy), reads=[KS], writes=[KSbf])
            if main and has_s:
                cs = slice(BT, BT + NS)
                P.op("dve", lambda e: e.tensor_scalar(out=kks[:], in0=fT[:, cs], scalar1=-1.0, scalar2=1.0, op0=ALU.mult, op1=ALU.add), reads=["fT"], writes=["kks"])
                ptk, ptkk = ps()
                P.op("pe", lambda e, ptk=ptk: e.transpose(out=ptk[0:NS, 0:128], in_=kks[:], identity=ident[:]), reads=["kks", "ident"], writes=[ptkk])
                P.op("act", lambda e, ptk=ptk: e.activation(out=ktok[:], in_=ptk[0:NS, 0:128], func=AF.Copy), reads=[ptkk], writes=["ktok"])
                P.op("dve", lambda e: e.tensor_tensor(out=km[:], in0=ktok[:].unsqueeze(1).to_broadcast([NS, NS, 128]), in1=ident[0:NS, 0:NS].unsqueeze(2).to_broadcast([NS, NS, 128]), op=ALU.mult), reads=["ktok", "ident"], writes=["xn"])
                for b in range(NS + 1):
                    if b < NS:
                        idx = h * NS + b
                        sl = idx % NSB
                        if not P.dry:
                            while sload_next[0] < min(idx + NSB, H * NS):
                                i2 = sload_next[0]
                                P.dma("sp", s0b[:, i2 % NSB, :], sh0[i2 % NS, i2 // NS], writes=[("s0b", i2 % NSB)])
                                sload_next[0] += 1
                        pkv, pkvk = ps()
                        P.op("pe", lambda e, b=b, pkv=pkv: e.matmul(pkv[:, 0:128], lhsT=km[:, b, :], rhs=is32[:, :], start=True, stop=True), reads=["xn", "is32"], writes=[pkvk])
                        P.op("dve", lambda e, b=b, sl=sl, pkv=pkv: e.scalar_tensor_tensor(out=snb[:, sl, :], in0=s0b[:, sl, :], scalar=fT[:, BT + b:BT + b + 1], in1=pkv[:, 0:128], op0=ALU.mult, op1=ALU.add), reads=[("s0b", sl), "fT", pkvk], writes=[("snb", sl)])
                        P.dma("sp", sh_s[b, h], snb[:, sl, :], reads=[("snb", sl)], is_output=True)
                    if b >= 1:
                        bp = b - 1
                        slp = (h * NS + bp) % NSB
                        P.op("pe", lambda e, bp=bp, slp=slp: e.matmul(pO[:, BT + bp:BT + bp + 1], lhsT=snb[:, slp, :], rhs=qT[:, BT + bp:BT + bp + 1], start=True, stop=True), reads=[("snb", slp), "qT"], writes=[pOk])
            if main:
                P.op("act", lambda e: e.activation(out=osq[:, 0:ncol], in_=pO[:, 0:ncol], func=AF.Square), reads=[pOk], writes=[("sgt", 0)])
                pn, pnk = ps()
                P.op("pe", lambda e, pn=pn: e.matmul(pn[:, 0:ncol], lhsT=ones[:], rhs=osq[:, 0:ncol], start=True, stop=True), reads=["ones", ("sgt", 0)], writes=[pnk])
                P.op("dve", lambda e, pn=pn: e.tensor_scalar(out=rn[:, 0:ncol], in0=pn[:, 0:ncol], scalar1=1.0 / 128, scalar2=EPS, op0=ALU.mult, op1=ALU.add), reads=[pnk], writes=["mt1"])
                P.op("act", lambda e: e.activation(out=rn[:, 0:ncol], in_=rn[:, 0:ncol], func=AF.Sqrt), reads=["mt1"], writes=["mt1"])
                P.op("dve", lambda e: e.reciprocal(out=rn[:, 0:ncol], in_=rn[:, 0:ncol]), reads=["mt1"], writes=["mt1"])
                P.op("dve", lambda e, h=h: e.scalar_tensor_tensor(out=otmp[:, 0:ncol], in0=pO[:, 0:ncol], scalar=ghT[:, h:h + 1], in1=rn[:, 0:ncol], op0=ALU.mult, op1=ALU.mult), reads=[pOk, "ghT", "mt1"], writes=["mt2"])
                P.op("dve", lambda e, h=h: e.tensor_tensor(out=oaT[:, h, 0:ncol], in0=otmp[:, 0:ncol], in1=ogs[:, 0:ncol], op=ALU.mult), reads=["mt2", "ogs"], writes=["oaT"])

        if main:
            for ct in range(GT):
                for gl in range(8):
                    g = ct * 8 + gl
                    zs = g % 2
                    py, pyk = ps()
                    P.op("pe", lambda e, g=g, py=py: e.matmul(py[:, 0:nchn], lhsT=Toep[:, g, :], rhs=Uall[:, g, 0:nchn], start=True, stop=False), reads=["Toep", ("Uall", ct)], writes=[pyk], track=False)
                    P.op("pe", lambda e, g=g, py=py: e.matmul(py[:, 0:nchn], lhsT=W2all[:, g, :], rhs=Hin[:, g, 0:nchn], start=False, stop=True), reads=["W2all", "Hin"], writes=[pyk])
                    P.op("act", lambda e, zs=zs, py=py: e.activation(out=Zg[:, zs, 0:nchn], in_=py[:, 0:nchn], func=AF.Gelu), reads=[pyk], writes=[("Zg", zs)])
                    pz, pzk = ps()
                    P.op("pe", lambda e, zs=zs, pz=pz: e.transpose(out=pz[0:NCH, 0:128], in_=Zg[:, zs, 0:NCH], identity=ident[:]), reads=[("Zg", zs), "ident"], writes=[pzk], track=not has_s)
                    if has_s:
                        P.op("pe", lambda e, zs=zs, pz=pz: e.transpose(out=pz[0:NS, 128:256], in_=Zg[:, zs, NCH:NCHT], identity=ident[:]), reads=[("Zg", zs), "ident"], writes=[pzk])
                    P.op("dve", lambda e, gl=gl, pz=pz: e.tensor_copy(out=zcm[0:NCH, :, gl * 16:(gl + 1) * 16], in_=pz[0:NCH, 0:128].rearrange("p (t q) -> p t q", q=16)), reads=[pzk], writes=["ucm"])
                    if has_s:
                        P.op("dve", lambda e, gl=gl, pz=pz: e.tensor_copy(out=zcms[:, gl * 16:(gl + 1) * 16], in_=pz[0:NS, 128 + 112:256]), reads=[pzk], writes=["zcms"])
                pt, pk = ps()
                for t in range(8):
                    P.op("pe", lambda e, t=t, pt=pt: e.transpose(out=pt[:, t * NCH:(t + 1) * NCH], in_=zcm[0:NCH, t, :], identity=ident[0:NCH, 0:NCH]), reads=["ucm", "ident"], writes=[pk], track=(t == 7 and not has_s))
                if has_s:
                    P.op("pe", lambda e, pt=pt: e.transpose(out=pt[:, 8 * NCH:8 * NCH + NS], in_=zcms[:, :], identity=ident[0:NS, 0:NS]), reads=["zcms", "ident"], writes=[pk])
                P.op("dve", lambda e, ct=ct, pt=pt: e.tensor_copy(out=zT[:, ct, 0:BT].rearrange("p (c t) -> p t c", t=8), in_=pt[:, 0:8 * NCH].rearrange("p (t c) -> p t c", t=8)), reads=[pk], writes=["zT"])
                if has_s:
                    P.op("dve", lambda e, ct=ct, pt=pt: e.tensor_copy(out=zT[:, ct, BT:BT + NS], in_=pt[:, 8 * NCH:8 * NCH + NS]), reads=[pk], writes=["zT"])
        bg_flush()
        if not main:
            return

        zrhs = lambda k: zT[:, k, 0:ncol]
        for m in range(GT):
            wl, wlk = load_w(w_glu[:, m * 128:(m + 1) * 128], GT, 128)
            pl, plk = proj_fm(wl, wlk, GT, zrhs, ncol, ["zT"])
            wg, wgk = load_w(w_glu[:, SW + m * 128:SW + (m + 1) * 128], GT, 128)
            pg, pgk = proj_fm(wg, wgk, GT, zrhs, ncol, ["zT"])
            P.op("act", lambda e, pg=pg: e.activation(out=sgt[:, 0, 0:ncol], in_=pg[:, 0:ncol], func=AF.Sigmoid), reads=[pgk], writes=[("sgt", 0)])
            P.op("dve", lambda e, m=m, pl=pl: e.tensor_tensor(out=obT[:, m, 0:ncol], in0=pl[:, 0:ncol], in1=sgt[:, 0, 0:ncol], op=ALU.mult), reads=[plk, ("sgt", 0)], writes=["obT"])

        GA0 = 4 * HW + SW
        for m in range(KD):
            wa, wak = load_w(w_in[:, GA0 + m * 128:GA0 + (m + 1) * 128], KD, 128)
            pga, pgak = proj_fm(wa, wak, KD, hrhs, ncol, ["hT"])
            P.op("act", lambda e, pga=pga: e.activation(out=sgt[:, 0, 0:ncol], in_=pga[:, 0:ncol], func=AF.Sigmoid), reads=[pgak], writes=[("sgt", 0)])
            wb, wbk = load_w(w_in[:, GA0 + D + m * 128:GA0 + D + (m + 1) * 128], KD, 128)
            pgb, pgbk = proj_fm(wb, wbk, KD, hrhs, ncol, ["hT"])
            P.op("act", lambda e, pgb=pgb: e.activation(out=sgt[:, 1, 0:ncol], in_=pgb[:, 0:ncol], func=AF.Sigmoid), reads=[pgbk], writes=[("sgt", 1)])
            wpa, wpak = load_w(w_pa[:, m * 128:(m + 1) * 128], H, 128)
            ppa, ppak = proj_fm(wpa, wpak, H, lambda k: oaT[:, k, 0:ncol], ncol, ["oaT"])
            P.op("dve", lambda e, ppa=ppa: e.tensor_tensor(out=mt1[:, 0:ncol], in0=ppa[:, 0:ncol], in1=sgt[:, 0, 0:ncol], op=ALU.mult), reads=[ppak, ("sgt", 0)], writes=["mt1"])
            wpb, wpbk = load_w(w_pb[:, m * 128:(m + 1) * 128], GT, 128)
            ppb, ppbk = proj_fm(wpb, wpbk, GT, lambda k: obT[:, k, 0:ncol], ncol, ["obT"])
            P.op("dve", lambda e, ppb=ppb: e.tensor_tensor(out=mt2[:, 0:ncol], in0=ppb[:, 0:ncol], in1=sgt[:, 1, 0:ncol], op=ALU.mult), reads=[ppbk, ("sgt", 1)], writes=["mt2"])
            P.op("dve", lambda e, m=m: e.tensor_tensor(out=mergedT[:, m, 0:ncol], in0=mt1[:, 0:ncol], in1=mt2[:, 0:ncol], op=ALU.add), reads=["mt1", "mt2"], writes=["mh"])

        def tm_matmul(wsrc, nk, lhs_fn, lkeys, finish_fn):
            for db in range(NDB):
                accs = [ps() for _ in tl]
                k = 0
                while k < nk:
                    nq = min(WSLOT // DB, nk - k)
                    wv, wk = load_w(wsrc[k * 128:(k + nq) * 128, db * DB:(db + 1) * DB], nq, DB)
                    for ti, (i, rows, cols) in enumerate(tl):
                        pa_, pak = accs[ti]
                        for q in range(nq):
                            P.op("pe", lambda e, k=k, q=q, pa_=pa_, rows=rows, cols=cols, wv=wv: e.matmul(pa_[0:rows, 0:DB], lhsT=lhs_fn(k + q, cols), rhs=wv[:, q, :], start=(k + q == 0), stop=(k + q == nk - 1)), reads=[wk] + lkeys, writes=[pak], track=(q == nq - 1))
                    k += nq
                for ti, (i, rows, cols) in enumerate(tl):
                    finish_fn(i, rows, db, accs[ti][0], accs[ti][1])

        for (i, rows, cols) in tl:
            srcx = xsrc[i * 128:(i + 1) * 128, :] if rows == 128 else xs
            P.dma("sp", x2[0:rows, i, :], srcx, writes=[("x2", i), "XR", "XI"])

        def fin_out(i, rows, db, pa_, pak):
            P.op("dve", lambda e: e.tensor_tensor(out=x2[0:rows, i, db * DB:(db + 1) * DB], in0=pa_[0:rows, 0:DB], in1=x2[0:rows, i, db * DB:(db + 1) * DB], op=ALU.add), reads=[pak, ("x2", i)], writes=[("x2", i)])
        tm_matmul(w_out, KD, lambda k, cols: mergedT[:, k, cols], ["mh"], fin_out)

        rms_to_hT(lambda i, rows: (x2[0:rows, i, :], ("x2", i)), tl, g2T, "g2T")

        for hh in range(NHH):
            j0 = hh * FTH
            nj = min(FTH, FT - j0)
            if nj <= 0:
                continue
            for jl in range(nj):
                j = j0 + jl
                wa, wak = load_w(w_up[:, j * 128:(j + 1) * 128], KD, 128)
                pa_, pak = proj_fm(wa, wak, KD, hrhs, ncol, ["hT"])
                P.op("act", lambda e, pa_=pa_: e.activation(out=sgt[:, 0, 0:ncol], in_=pa_[:, 0:ncol], func=AF.Silu), reads=[pak], writes=[("sgt", 0)])
                wb, wbk = load_w(w_up[:, FH + j * 128:FH + (j + 1) * 128], KD, 128)
                pb_, pbk = proj_fm(wb, wbk, KD, hrhs, ncol, ["hT"])
                P.op("dve", lambda e, jl=jl, pb_=pb_: e.tensor_tensor(out=hid[:, jl, 0:ncol], in0=pb_[:, 0:ncol], in1=sgt[:, 0, 0:ncol], op=ALU.mult), reads=[pbk, ("sgt", 0)], writes=["mh"])

            def fin_dn(i, rows, db, pa_, pak):
                P.op("dve", lambda e: e.tensor_tensor(out=x2[0:rows, i, db * DB:(db + 1) * DB], in0=pa_[0:rows, 0:DB], in1=x2[0:rows, i, db * DB:(db + 1) * DB], op=ALU.add), reads=[pak, ("x2", i)], writes=[("x2", i)])
            tm_matmul(w_down[j0 * 128:(j0 + nj) * 128, :], nj, lambda k, cols: hid[:, k, cols], ["mh"], fin_dn)

        for (i, rows, cols) in tl:
            P.op("act", lambda e, i=i, rows=rows: e.activation(out=junk[0:rows, 0:D], in_=x2[0:rows, i, :], func=AF.Square, accum_out=ss[0:rows, 2:3]), reads=[("x2", i)], writes=["mh", "ss"])
            P.op("dve", lambda e, rows=rows: e.tensor_scalar(out=ss[0:rows, 3:4], in0=ss[0:rows, 2:3], scalar1=1.0 / D, scalar2=EPS, op0=ALU.mult, op1=ALU.add), reads=["ss"], writes=["ss"])
            P.op("act", lambda e, rows=rows: e.activation(out=ss[0:rows, 3:4], in_=ss[0:rows, 3:4], func=AF.Sqrt), reads=["ss"], writes=["ss"])
            P.op("dve", lambda e, rows=rows: e.reciprocal(out=ss[0:rows, 3:4], in_=ss[0:rows, 3:4]), reads=["ss"], writes=["ss"])
            P.op("dve", lambda e, i=i, rows=rows: e.scalar_tensor_tensor(out=yt[0:rows, 0:D], in0=x2[0:rows, i, :], scalar=ss[0:rows, 3:4], in1=fgB[0:rows, :], op0=ALU.mult, op1=ALU.mult), reads=[("x2", i), "ss", "fgB"], writes=["xn"])
            if rows == 128:
                P.dma("sp", ydst[i * 128:(i + 1) * 128, :], yt[:, 0:D], reads=["xn"], is_output=True)
            else:
                P.dma("sp", y_s, yt[0:rows, 0:D], reads=["xn"], is_output=True)

    BGQ_PRE = cfg.get("BGQ_PRE", 2)
    BGQ_MAIN0 = cfg.get("BGQ_MAIN0", 2)

    def all_blocks():
        for bi in range(NPRE):
            do_block(xpre[bi * BT:(bi + 1) * BT, :], "P", False, None, False)
        for bi in range(NMAIN):
            do_block(xmain[bi * BT:(bi + 1) * BT, :], "M", bi == 0, y_main[bi * BT:(bi + 1) * BT, :], bi == NMAIN - 1)
    P.dry = True
    all_blocks()
    P.dry = False
    setup_wscr()
    all_blocks()
    assert w_idx[0] == len(wspecs)

    P.dma("sp", sh_p.rearrange("h k v -> k h v"), S[:], reads=[("S", h) for h in range(H)], is_output=True)
    P.dma("sp", s5re_p.rearrange("g n -> n g"), Wd[0:64, 0, :], reads=["Wd"], is_output=True, allow_slow_non_contiguous=True)
    P.dma("sp", s5im_p.rearrange("g n -> n g"), Wd[64:128, 0, :], reads=["Wd"], is_output=True, allow_slow_non_contiguous=True)

    SBUF_LEFT[0] = nc.sbuf_bytes_remaining
    P.emit()
    for cm in reversed(ctxs):
        cm.__exit__(None, None, None)
    P.close()
    return nc


WEIGHT_NAMES = ["lb_logits", "norm1_g", "w_in", "hgrn_norm_g", "s5_lam_re", "s5_lam_im", "s5_log_dt",
                "s5_B_re", "s5_B_im", "s5_C_re", "s5_C_im", "s5_D", "w_s5_glu", "w_proj_a", "w_proj_b",
                "w_out", "norm2_g", "w_ffn_up", "w_ffn_down"]


def make_in_maps(inputs, n_cores, half, ns):
    f32 = lambda a: np.ascontiguousarray(np.asarray(a, dtype=np.float32))
    shared = {k: f32(inputs[k][0]) for k in WEIGHT_NAMES if k != "lb_logits"}
    shared["lb_logits"] = f32(inputs["lb_logits"])
    shared["final_norm_g"] = f32(inputs["final_norm_g"])
    xp = np.asarray(inputs["x_prompt"], dtype=np.float32)
    xsm = np.asarray(inputs["x_sample"], dtype=np.float32)
    maps = []
    for c in range(n_cores):
        b, hf = c // 2, c % 2
        m = dict(shared)
        m["xmain"] = f32(xp[b, hf * half:(hf + 1) * half])
        m["xpre"] = f32(xp[b, 0:half]) if hf == 1 else np.zeros((half, xp.shape[2]), np.float32)
        m["xs"] = f32(xsm[c * ns:(c + 1) * ns, 0])
        m["sh0"] = f32(inputs["state_hgrn"][0, c * ns:(c + 1) * ns])
        m["s5re0"] = f32(inputs["state_s5_re"][0, c * ns:(c + 1) * ns])
        m["s5im0"] = f32(inputs["state_s5_im"][0, c * ns:(c + 1) * ns])
        maps.append(m)
    return maps


def assemble(results, n_cores, nb):
    y_prompt = np.stack([np.concatenate([results[2 * b]["y_main"], results[2 * b + 1]["y_main"]], axis=0) for b in range(nb)])
    y_sample = np.concatenate([results[c]["y_s"] for c in range(n_cores)], axis=0)[:, None, :]
    shp = np.stack([results[2 * b + 1]["sh_p"] for b in range(nb)])[None]
    rep = np.stack([results[2 * b + 1]["s5re_p"] for b in range(nb)])[None]
    imp = np.stack([results[2 * b + 1]["s5im_p"] for b in range(nb)])[None]
    shs = np.concatenate([results[c]["sh_s"] for c in range(n_cores)], axis=0)[None]
    res = np.concatenate([results[c]["s5re_s"] for c in range(n_cores)], axis=0)[None]
    ims = np.concatenate([results[c]["s5im_s"] for c in range(n_cores)], axis=0)[None]
    return tuple(np.ascontiguousarray(a, dtype=np.float32) for a in (y_prompt, y_sample, shp, rep, imp, shs, res, ims))


def kernel(**inputs):
    n = 8
    cfg = FULL_CFG
    nc = build(cfg)
    maps = make_in_maps(inputs, n, cfg["BT"] * cfg["NMAIN"], cfg["NS"])
    res = run_bass_kernel_spmd(nc, maps, core_ids=list(range(n)))
    return assemble(res.results, n, 4)
```

```python
import math
import types
import numpy as np
import concourse.bass as bass
import concourse.mybir as mybir
from concourse.bass_utils import run_bass_kernel_spmd

F32 = mybir.dt.float32
BF16 = mybir.dt.bfloat16
AF = mybir.ActivationFunctionType
ALU = mybir.AluOpType
ENGS = ("pe", "act", "dve", "pool", "sp")
EPS = 1e-6


def _freeze(fn):
    if fn.__closure__ is None:
        return fn
    cells = []
    for c in fn.__closure__:
        try:
            cells.append(types.CellType(c.cell_contents))
        except ValueError:
            cells.append(c)
    return types.FunctionType(fn.__code__, fn.__globals__, fn.__name__, fn.__defaults__, tuple(cells))


class Prog:
    def __init__(self, nc, n_dma_sems=40):
        self.nc = nc
        self.ops = {e: [] for e in ENGS}
        self.cnt = {e: 0 for e in ENGS}
        self.sems = {}
        self.known = {e: {} for e in ENGS}
        self.last_w = {}
        self.readers = {}
        self.n_dma_sems = n_dma_sems
        self.dma_sems = []
        self.dma_cnt = []
        self.dma_rr = 0
        self.out_tokens = []
        self._ctx = []
        self.dry = False

    def setup(self):
        nc = self.nc
        for e in ("pe", "act", "dve", "pool"):
            cm = nc.semaphore("s_" + e)
            self.sems[e] = cm.__enter__()
            self._ctx.append(cm)
        for i in range(self.n_dma_sems):
            cm = nc.semaphore("d_%d" % i)
            self.dma_sems.append(cm.__enter__())
            self._ctx.append(cm)
            self.dma_cnt.append(0)

    def close(self):
        for cm in reversed(self._ctx):
            cm.__exit__(None, None, None)

    def _waits(self, eng, reads, writes):
        toks = []
        for k in reads:
            t = self.last_w.get(k)
            if t is not None:
                toks.append(t)
        for k in writes:
            t = self.last_w.get(k)
            if t is not None:
                toks.append(t)
            toks.extend(self.readers.get(k, ()))
        need = {}
        for (sem, val, teng) in toks:
            if teng == "pe" and eng == "pe":
                continue
            key = id(sem)
            if self.known[eng].get(key, 0) >= val:
                continue
            if key not in need or need[key][1] < val:
                need[key] = (sem, val)
        for key, (sem, val) in need.items():
            self.known[eng][key] = val
        return list(need.values())

    def _record(self, tok, reads, writes):
        for k in writes:
            self.last_w[k] = tok
            self.readers[k] = []
        for k in reads:
            lst = self.readers.setdefault(k, [])
            lst.append(tok)
            if len(lst) > 64:
                best = {}
                for t in lst:
                    if id(t[0]) not in best or best[id(t[0])][1] < t[1]:
                        best[id(t[0])] = t
                self.readers[k] = list(best.values())

    def op(self, eng, fn, reads=(), writes=(), track=True):
        if self.dry:
            return None
        fn = _freeze(fn)
        waits = self._waits(eng, reads, writes)
        tok = (self.sems[eng], self.cnt[eng] + 1, eng)
        if track:
            self.cnt[eng] += 1
        self.ops[eng].append((waits, fn, (self.sems[eng], 1) if track else None))
        self._record(tok, reads, writes)
        return tok

    def dma(self, eng, out, in_, reads=(), writes=(), is_output=False, **kw):
        if self.dry:
            return None
        j = self.dma_rr
        self.dma_rr = (self.dma_rr + 1) % self.n_dma_sems
        sem = self.dma_sems[j]
        waits = self._waits(eng, reads, writes)
        prev = self.dma_cnt[j]
        if prev > 0 and self.known[eng].get(id(sem), 0) < prev:
            waits.append((sem, prev))
            self.known[eng][id(sem)] = prev
        self.dma_cnt[j] += 16
        tok = (sem, self.dma_cnt[j], "dma")

        def fn(e, out=out, in_=in_, kw=kw):
            return e.dma_start(out=out, in_=in_, **kw)

        self.ops[eng].append((waits, fn, (sem, 16)))
        self._record(tok, reads, writes)
        if is_output:
            self.out_tokens.append(tok)
        return tok

    def barrier(self):
        if self.dry:
            return
        for e in ENGS:
            waits = []
            for e2 in ("pe", "act", "dve", "pool"):
                if e2 != e and self.cnt[e2] > 0 and self.known[e].get(id(self.sems[e2]), 0) < self.cnt[e2]:
                    waits.append((self.sems[e2], self.cnt[e2]))
                    self.known[e][id(self.sems[e2])] = self.cnt[e2]
            for j, sem in enumerate(self.dma_sems):
                if self.dma_cnt[j] > 0 and self.known[e].get(id(sem), 0) < self.dma_cnt[j]:
                    waits.append((sem, self.dma_cnt[j]))
                    self.known[e][id(sem)] = self.dma_cnt[j]
            self.ops[e].append((waits, None, None))

    def emit(self):
        nc = self.nc
        need = {}
        for (sem, val, _) in self.out_tokens:
            if id(sem) not in need or need[id(sem)][1] < val:
                need[id(sem)] = (sem, val)
        self.ops["sp"].append((list(need.values()), None, None))
        with nc.Block() as block:
            def run(e, lst):
                for waits, fn, inc in lst:
                    for (sem, val) in waits:
                        e.wait_ge(sem, val)
                    if fn is None:
                        continue
                    inst = fn(e)
                    if inc is not None:
                        inst.then_inc(inc[0], inc[1])

            @block.tensor
            def _(e):
                run(e, self.ops["pe"])

            @block.scalar
            def _(e):
                run(e, self.ops["act"])

            @block.vector
            def _(e):
                run(e, self.ops["dve"])

            @block.gpsimd
            def _(e):
                run(e, self.ops["pool"])

            @block.sync
            def _(e):
                run(e, self.ops["sp"])


SBUF_LEFT = [0]
FULL_CFG = dict(D=2048, FH=5632, BT=256, NPRE=4, NMAIN=4, NS=16)


def build(cfg):
    D, FH, BT, NPRE, NMAIN, NS = cfg["D"], cfg["FH"], cfg["BT"], cfg["NPRE"], cfg["NMAIN"], cfg["NS"]
    HW = D // 2
    SW = D // 2
    KD = D // 128
    H = HW // 128
    GT = SW // 128
    G = SW // 16
    FT = FH // 128
    NT = BT // 128
    NCH = BT // 8
    TB = BT + NS
    NCHT = NCH + NS
    WIN = 4 * HW + SW + 2 * D
    DB = min(512, D)
    NDB = D // DB
    assert TB <= 512 and NS == 16

    nc = bass.Bass("TRN2", target_bir_lowering=False)

    def din(name, shape):
        return nc.dram_tensor(name, list(shape), F32, kind="ExternalInput").ap()

    def dout(name, shape):
        return nc.dram_tensor(name, list(shape), F32, kind="ExternalOutput").ap()

    xpre = din("xpre", [NPRE * BT, D])
    xmain = din("xmain", [NMAIN * BT, D])
    xs = din("xs", [NS, D])
    sh0 = din("sh0", [NS, H, 128, 128])
    s5re0 = din("s5re0", [NS, G, 64])
    s5im0 = din("s5im0", [NS, G, 64])
    lb_logits = din("lb_logits", [2, HW])
    norm1_g = din("norm1_g", [D])
    w_in = din("w_in", [D, WIN])
    hgrn_g = din("hgrn_norm_g", [HW])
    lam_re = din("s5_lam_re", [G, 64])
    lam_im = din("s5_lam_im", [G, 64])
    log_dt = din("s5_log_dt", [G])
    B_re = din("s5_B_re", [G, 64, 16])
    B_im = din("s5_B_im", [G, 64, 16])
    C_re = din("s5_C_re", [G, 16, 64])
    C_im = din("s5_C_im", [G, 16, 64])
    s5_D = din("s5_D", [SW])
    w_glu = din("w_s5_glu", [SW, 2 * SW])
    w_pa = din("w_proj_a", [HW, D])
    w_pb = din("w_proj_b", [SW, D])
    w_out = din("w_out", [D, D])
    norm2_g = din("norm2_g", [D])
    w_up = din("w_ffn_up", [D, 2 * FH])
    w_down = din("w_ffn_down", [FH, D])
    fnorm_g = din("final_norm_g", [D])

    y_main = dout("y_main", [NMAIN * BT, D])
    y_s = dout("y_s", [NS, D])
    sh_p = dout("sh_p", [H, 128, 128])
    s5re_p = dout("s5re_p", [G, 64])
    s5im_p = dout("s5im_p", [G, 64])
    sh_s = dout("sh_s", [NS, H, 128, 128])
    s5re_s = dout("s5re_s", [NS, G, 64])
    s5im_s = dout("s5im_s", [NS, G, 64])

    P = Prog(nc)
    P.setup()
    ctxs = []

    def sb(name, shape, dt=F32):
        cm = nc.sbuf_tensor(name, list(shape), dt)
        t = cm.__enter__()
        ctxs.append(cm)
        return t

    NPS = 8
    psb = []
    for i in range(NPS):
        cm = nc.psum_tensor("ps%d" % i, [128, 512], F32)
        psb.append(cm.__enter__())
        ctxs.append(cm)
    ps_rr = [0]

    def ps():
        i = ps_rr[0]
        if P.dry:
            return psb[i], ("ps", i)
        ps_rr[0] = (i + 1) % (NPS - 1)
        return psb[i], ("ps", i)

    ident = sb("ident", [128, 128])
    ones = sb("ones", [128, 128])
    mut = sb("mut", [128, 128])
    mbc = sb("mbc", [128, 128])
    P.op("pool", lambda e: e.memset(ones[:], 1.0), writes=["ones"])
    P.op("pool", lambda e: e.memset(ident[:], 1.0), writes=["ident"])
    P.op("pool", lambda e: e.affine_select(out=ident[:], in_=ident[:], pattern=[[-1, 128]], compare_op=ALU.is_ge, fill=0.0, base=0, channel_multiplier=1), reads=["ident"], writes=["ident"])
    P.op("pool", lambda e: e.affine_select(out=ident[:], in_=ident[:], pattern=[[1, 128]], compare_op=ALU.is_ge, fill=0.0, base=0, channel_multiplier=-1), reads=["ident"], writes=["ident"])
    P.op("pool", lambda e: e.memset(mut[:], 1.0), writes=["mut"])
    P.op("pool", lambda e: e.affine_select(out=mut[:], in_=mut[:], pattern=[[1, 128]], compare_op=ALU.is_ge, fill=0.0, base=0, channel_multiplier=-1), reads=["mut"], writes=["mut"])
    P.op("pool", lambda e: e.memset(mbc[:], 1.0), writes=["mbc"])
    P.op("pool", lambda e: e.affine_select(out=mbc[:].rearrange("p (t q) -> p t q", q=16), in_=mbc[:].rearrange("p (t q) -> p t q", q=16), pattern=[[16, 8], [0, 16]], compare_op=ALU.is_ge, fill=0.0, base=15, channel_multiplier=-1), reads=["mbc"], writes=["mbc"])

    g1T = sb("g1T", [128, KD])
    g2T = sb("g2T", [128, KD])
    ghT = sb("ghT", [128, H])
    lbT = sb("lbT", [128, H])
    omlT = sb("omlT", [128, H])
    l1T = sb("l1T", [128, H])
    fgB = sb("fgB", [128, D])
    P.dma("sp", g1T[:], norm1_g.rearrange("(k p) -> p k", p=128), writes=["g1T"], allow_slow_non_contiguous=True)
    P.dma("sp", g2T[:], norm2_g.rearrange("(k p) -> p k", p=128), writes=["g2T"], allow_slow_non_contiguous=True)
    P.dma("sp", ghT[:], hgrn_g.rearrange("(k p) -> p k", p=128), writes=["ghT"], allow_slow_non_contiguous=True)
    P.dma("sp", lbT[:], lb_logits[0].rearrange("(k p) -> p k", p=128), writes=["lbT"], allow_slow_non_contiguous=True)
    P.dma("sp", l1T[:], lb_logits[1].rearrange("(k p) -> p k", p=128), writes=["l1T"], allow_slow_non_contiguous=True)
    P.dma("sp", fgB[:], fnorm_g.partition_broadcast(128), writes=["fgB"])
    P.op("dve", lambda e: e.tensor_tensor(out=lbT[:], in0=lbT[:], in1=l1T[:], op=ALU.subtract), reads=["lbT", "l1T"], writes=["lbT"])
    P.op("act", lambda e: e.activation(out=lbT[:], in_=lbT[:], func=AF.Sigmoid), reads=["lbT"], writes=["lbT"])
    P.op("dve", lambda e: e.tensor_scalar(out=omlT[:], in0=lbT[:], scalar1=-1.0, scalar2=1.0, op0=ALU.mult, op1=ALU.add), reads=["lbT"], writes=["omlT"])

    NW = 5
    NSTG = 2
    NBG = 1
    DEPTH = 4
    WSLOT = 16 * 128
    wring = sb("wring", [128, NW, WSLOT], BF16)
    wstage = sb("wstage", [128, NSTG, WSLOT])
    bgbuf = sb("bgbuf", [128, NBG, 16], BF16)
    wspecs = []
    w_idx = [0]
    w_issued = [0]
    CAST_ENGS = ("act", "dve", "act", "dve", "act", "dve")

    wscr_state = {"tid": None, "ap": None, "seen": set(), "nfirst": 0, "bg": [], "bgpos": 0, "nbg": 0}
    blk_marks = []

    def setup_wscr():
        keys = {}
        tid = []
        for (src, kt, C) in wspecs:
            k = (str(src), kt, C)
            if k not in keys:
                keys[k] = len(keys)
            tid.append(keys[k])
        wscr_state["tid"] = tid
        npre_specs = blk_marks[NPRE] if len(blk_marks) > NPRE else len(wspecs)
        pre_t = set(tid[:npre_specs])
        bg = []
        for n in range(npre_specs, len(wspecs)):
            if tid[n] not in pre_t:
                bg.append(n)
                pre_t.add(tid[n])
        wscr_state["bg"] = bg
        wscr_state["ap"] = nc.dram_tensor("wscr", [len(keys), 128, WSLOT], BF16, kind="Internal").ap()

    def issue_load(n):
        src, kt, C = wspecs[n]
        i = n % NW
        t = wscr_state["tid"][n]
        wscr = wscr_state["ap"]
        view = wring[:, i, 0:kt * C].rearrange("p (k c) -> p k c", k=kt)
        if t in wscr_state["seen"]:
            P.dma("sp", wring[:, i, 0:kt * C], wscr[t, :, 0:kt * C], reads=[("wscr", t)], writes=[("w", i)])
            return
        wscr_state["seen"].add(t)
        nf = wscr_state["nfirst"]
        wscr_state["nfirst"] = nf + 1
        j = nf % NSTG
        sview = wstage[:, j, 0:kt * C].rearrange("p (k c) -> p k c", k=kt)
        P.dma("sp", sview, src.rearrange("(k p) c -> p k c", p=128), writes=[("wst", j)])
        eng = "act"
        if eng == "act":
            P.op("act", lambda e: e.activation(out=view, in_=sview, func=AF.Copy), reads=[("wst", j)], writes=[("w", i)])
        else:
            P.op(eng, lambda e: e.tensor_copy(out=view, in_=sview), reads=[("wst", j)], writes=[("w", i)])
        P.dma(eng, wscr[t, :, 0:kt * C], wring[:, i, 0:kt * C], reads=[("w", i)], writes=[("wscr", t)])

    def bg_step(k):
        if P.dry or wscr_state["tid"] is None or k <= 0:
            return
        wscr = wscr_state["ap"]
        done = 0
        while done < k and wscr_state["bgpos"] < len(wscr_state["bg"]):
            n = wscr_state["bg"][wscr_state["bgpos"]]
            wscr_state["bgpos"] += 1
            t = wscr_state["tid"][n]
            if t in wscr_state["seen"]:
                continue
            src, kt, C = wspecs[n]
            wscr_state["seen"].add(t)
            nf = wscr_state["nfirst"]
            wscr_state["nfirst"] = nf + 1
            j = nf % NSTG
            nb_ = wscr_state["nbg"]
            wscr_state["nbg"] = nb_ + 1
            bs = nb_ % NBG
            sview = wstage[:, j, 0:kt * C]
            P.dma("sp", sview.rearrange("p (k c) -> p k c", k=kt), src.rearrange("(k p) c -> p k c", p=128), writes=[("wst", j)])
            if nb_ % 2 == 0:
                P.op("act", lambda e: e.activation(out=bgbuf[:, bs, 0:kt * C], in_=sview, func=AF.Copy), reads=[("wst", j)], writes=[("bg", bs)])
            else:
                P.op("dve", lambda e: e.tensor_copy(out=bgbuf[:, bs, 0:kt * C], in_=sview), reads=[("wst", j)], writes=[("bg", bs)])
            P.dma("act", wscr[t, :, 0:kt * C], bgbuf[:, bs, 0:kt * C], reads=[("bg", bs)], writes=[("wscr", t)])
            done += 1

    def load_w(src, kt, C):
        assert kt * C <= WSLOT
        if P.dry:
            wspecs.append((src, kt, C))
            return wring[:, 0, 0:kt * C].rearrange("p (k c) -> p k c", k=kt), ("w", 0)
        n = w_idx[0]
        w_idx[0] += 1
        assert wspecs[n][1] == kt and wspecs[n][2] == C
        while w_issued[0] < min(n + DEPTH + 1, len(wspecs)):
            issue_load(w_issued[0])
            w_issued[0] += 1
        i = n % NW
        return wring[:, i, 0:kt * C].rearrange("p (k c) -> p k c", k=kt), ("w", i)

    W1all = sb("W1all", [128, G, 128], BF16)
    W2all = sb("W2all", [128, G, 128], BF16)
    Toep = sb("Toep", [128, G, 128], BF16)
    A8re = sb("A8re", [128, G])
    A8im = sb("A8im", [128, G])
    Am7re = sb("Am7re", [128, G])
    Am7im = sb("Am7im", [128, G])
    Dcol = sb("Dcol", [128, G])
    NK = 24
    GB = 8
    xn = sb("xn", [128, max(D, 2048)])
    scrF = sb("scrF", [128, max((NT + 1) * D, 6144, 2 * G * NCHT)])
    smalls = sb("smalls", [128, 12, GB])
    lre, lim, dtb, zre, zim, zt1, zt2, zt3 = [smalls[:, i, :] for i in range(8)]
    kv = sb("kv", [128, NK])
    cnat = sb("cnat", [128, 2, 2, 64])

    def xnv(off, shape):
        n = 1
        for d in shape:
            n *= d
        v = xn[:, off:off + n]
        if len(shape) == 2:
            v = v.rearrange("p (a b) -> p a b", a=shape[0])
        return v
    tA = xnv(0, [GB, NK]); tB = xnv(192, [GB, NK]); tC = xnv(384, [GB, NK]); Are = xnv(576, [GB, NK]); Aim = xnv(768, [GB, NK])
    Bre_sb = xnv(960, [GB, 16]); Bim_sb = xnv(1088, [GB, 16]); Cre_sb = xnv(1216, [GB, 16]); Cim_sb = xnv(1344, [GB, 16])
    T1 = xnv(1472, [GB, 8]); T2 = xnv(1536, [GB, 8]); Et = xnv(1600, [GB, 8])
    U1 = xnv(1664, [GB, 16]); U2 = xnv(1792, [GB, 16])
    tmsk = xn[:, 1920:2048]
    Pb = scrF[:, 0:1024].rearrange("p (g s q) -> p g s q", g=8, s=8)
    Pb2 = scrF[:, 1024:2048].rearrange("p (g s q) -> p g s q", g=8, s=8)
    Qb = scrF[:, 2048:4096].rearrange("p (g s q) -> p g s q", g=8, s=16)
    Qb2 = scrF[:, 4096:6144].rearrange("p (g s q) -> p g s q", g=8, s=16)

    for s in range(8):
        P.dma("sp", Dcol[s * 16:(s + 1) * 16, :], s5_D.rearrange("(g p) -> p g", p=16), writes=["Dcol"], allow_slow_non_contiguous=True)
    powers = list(range(-7, 9)) + list(range(7, -1, -1))
    for j, pw in enumerate(powers):
        P.op("pool", lambda e, j=j, pw=pw: e.memset(kv[:, j:j + 1], float(pw)), writes=["kv"])
    MAGIC = 12582912.0
    tR = wstage[:, 0, 0:GB * NK].rearrange("p (g k) -> p g k", g=GB)
    for bq in range(G // GB):
        gs = slice(bq * GB, (bq + 1) * GB)
        for hf in range(2):
            hs = slice(hf * 64, hf * 64 + 64)
            P.dma("sp", lre[hs, :], lam_re[gs].rearrange("g n -> n g"), writes=["lre"], allow_slow_non_contiguous=True)
            P.dma("sp", lim[hs, :], lam_im[gs].rearrange("g n -> n g"), writes=["lim"], allow_slow_non_contiguous=True)
            P.dma("sp", Bre_sb[hs, :, :], B_re[gs].rearrange("g n p -> n g p"), writes=["Bre"])
            P.dma("sp", Bim_sb[hs, :, :], B_im[gs].rearrange("g n p -> n g p"), writes=["Bim"])
        P.dma("sp", dtb, log_dt[gs].partition_broadcast(128), writes=["dtb"])
        P.op("act", lambda e: e.activation(out=dtb, in_=dtb, func=AF.Exp), reads=["dtb"], writes=["dtb"])
        P.op("dve", lambda e: e.tensor_tensor(out=zt1, in0=lre, in1=dtb, op=ALU.mult), reads=["lre", "dtb"], writes=["zt1"])
        P.op("dve", lambda e: e.tensor_tensor(out=zt2, in0=lim, in1=dtb, op=ALU.mult), reads=["lim", "dtb"], writes=["zt2"])
        kvb = kv[:].unsqueeze(1).to_broadcast([128, GB, NK])
        P.op("dve", lambda e: e.tensor_tensor(out=tA, in0=zt1.unsqueeze(2).to_broadcast([128, GB, NK]), in1=kvb, op=ALU.mult), reads=["zt1", "kv"], writes=["tA"])
        P.op("act", lambda e: e.activation(out=tA, in_=tA, func=AF.Exp), reads=["tA"], writes=["tA"])
        P.op("dve", lambda e: e.tensor_tensor(out=tB, in0=zt2.unsqueeze(2).to_broadcast([128, GB, NK]), in1=kvb, op=ALU.mult), reads=["zt2", "kv"], writes=["tB"])
        P.op("dve", lambda e: e.tensor_scalar(out=tC, in0=tB, scalar1=1.0 / (2 * math.pi), scalar2=0.25, op0=ALU.mult, op1=ALU.add), reads=["tB"], writes=["tC"])
        P.op("dve", lambda e: e.tensor_scalar(out=tB, in0=tB, scalar1=1.0 / (2 * math.pi), scalar2=None, op0=ALU.mult), reads=["tB"], writes=["tB"])
        for T_, tk in ((tB, "tB"), (tC, "tC")):
            P.op("dve", lambda e, T_=T_: e.tensor_scalar(out=tR[:], in0=T_, scalar1=MAGIC, scalar2=None, op0=ALU.add), reads=[tk], writes=["tR"])
            P.op("dve", lambda e: e.tensor_scalar(out=tR[:], in0=tR[:], scalar1=-MAGIC, scalar2=None, op0=ALU.add), reads=["tR"], writes=["tR"])
            P.op("dve", lambda e, T_=T_: e.tensor_tensor(out=T_, in0=T_, in1=tR[:], op=ALU.subtract), reads=[tk, "tR"], writes=[tk])
            P.op("act", lambda e, T_=T_: e.activation(out=T_, in_=T_, func=AF.Sin, scale=2 * math.pi), reads=[tk], writes=[tk])
        P.op("dve", lambda e: e.tensor_tensor(out=Are, in0=tA, in1=tC, op=ALU.mult), reads=["tA", "tC"], writes=["Are"])
        P.op("dve", lambda e: e.tensor_tensor(out=Aim, in0=tA, in1=tB, op=ALU.mult), reads=["tA", "tB"], writes=["Aim"])
        P.op("dve", lambda e, gs=gs: e.tensor_copy(out=A8re[:, gs], in_=Are[:, :, 15]), reads=["Are"], writes=["A8re"])
        P.op("dve", lambda e, gs=gs: e.tensor_copy(out=A8im[:, gs], in_=Aim[:, :, 15]), reads=["Aim"], writes=["A8im"])
        P.op("dve", lambda e, gs=gs: e.tensor_copy(out=Am7re[:, gs], in_=Are[:, :, 0]), reads=["Are"], writes=["Am7re"])
        P.op("dve", lambda e, gs=gs: e.tensor_copy(out=Am7im[:, gs], in_=Aim[:, :, 0]), reads=["Aim"], writes=["Am7im"])
        P.op("dve", lambda e: e.tensor_scalar(out=zt1, in0=Are[:, :, 8], scalar1=-1.0, scalar2=None, op0=ALU.add), reads=["Are"], writes=["zt1"])
        P.op("dve", lambda e: e.tensor_tensor(out=zt2, in0=lre, in1=lre, op=ALU.mult), reads=["lre"], writes=["zt2"])
        P.op("dve", lambda e: e.tensor_tensor(out=zt3, in0=lim, in1=lim, op=ALU.mult), reads=["lim"], writes=["zt3"])
        P.op("dve", lambda e: e.tensor_tensor(out=zt2, in0=zt2, in1=zt3, op=ALU.add), reads=["zt2", "zt3"], writes=["zt2"])
        P.op("dve", lambda e: e.reciprocal(out=zt2, in_=zt2), reads=["zt2"], writes=["zt2"])
        P.op("dve", lambda e: e.tensor_tensor(out=zre, in0=zt1, in1=lre, op=ALU.mult), reads=["zt1", "lre"], writes=["zre"])
        P.op("dve", lambda e: e.tensor_tensor(out=zt3, in0=Aim[:, :, 8], in1=lim, op=ALU.mult), reads=["Aim", "lim"], writes=["zt3"])
        P.op("dve", lambda e: e.tensor_tensor(out=zre, in0=zre, in1=zt3, op=ALU.add), reads=["zre", "zt3"], writes=["zre"])
        P.op("dve", lambda e: e.tensor_tensor(out=zre, in0=zre, in1=zt2, op=ALU.mult), reads=["zre", "zt2"], writes=["zre"])
        P.op("dve", lambda e: e.tensor_tensor(out=zim, in0=Aim[:, :, 8], in1=lre, op=ALU.mult), reads=["Aim", "lre"], writes=["zim"])
        P.op("dve", lambda e: e.tensor_tensor(out=zt3, in0=zt1, in1=lim, op=ALU.mult), reads=["zt1", "lim"], writes=["zt3"])
        P.op("dve", lambda e: e.tensor_tensor(out=zim, in0=zim, in1=zt3, op=ALU.subtract), reads=["zim", "zt3"], writes=["zim"])
        P.op("dve", lambda e: e.tensor_tensor(out=zim, in0=zim, in1=zt2, op=ALU.mult), reads=["zim", "zt2"], writes=["zim"])
        zreb = zre.unsqueeze(2).to_broadcast([128, GB, 8])
        zimb = zim.unsqueeze(2).to_broadcast([128, GB, 8])
        P.op("dve", lambda e: e.tensor_tensor(out=T1, in0=Are[:, :, 16:24], in1=zreb, op=ALU.mult), reads=["Are", "zre"], writes=["T1"])
        P.op("dve", lambda e: e.tensor_tensor(out=Et, in0=Aim[:, :, 16:24], in1=zimb, op=ALU.mult), reads=["Aim", "zim"], writes=["Et"])
        P.op("dve", lambda e: e.tensor_tensor(out=T1, in0=T1, in1=Et, op=ALU.subtract), reads=["T1", "Et"], writes=["T1"])
        P.op("dve", lambda e: e.tensor_tensor(out=T2, in0=Are[:, :, 16:24], in1=zimb, op=ALU.mult), reads=["Are", "zim"], writes=["T2"])
        P.op("dve", lambda e: e.tensor_tensor(out=Et, in0=Aim[:, :, 16:24], in1=zreb, op=ALU.mult), reads=["Aim", "zre"], writes=["Et"])
        P.op("dve", lambda e: e.tensor_tensor(out=T2, in0=T2, in1=Et, op=ALU.add), reads=["T2", "Et"], writes=["T2"])
        P.op("dve", lambda e: e.tensor_copy(out=Et[64:128], in_=T1[64:128]), reads=["T1"], writes=["Et"])
        P.op("dve", lambda e: e.tensor_copy(out=T1[64:128], in_=T2[64:128]), reads=["T2"], writes=["T1"])
        P.op("dve", lambda e: e.tensor_copy(out=T2[64:128], in_=Et[64:128]), reads=["Et"], writes=["T2"])
        P.op("dve", lambda e: e.tensor_scalar(out=T2[0:64], in0=T2[0:64], scalar1=-1.0, scalar2=None, op0=ALU.mult), reads=["T2"], writes=["T2"])
        P.op("dve", lambda e: e.tensor_copy(out=U1[0:64], in_=Are[0:64, :, 0:16]), reads=["Are"], writes=["U1"])
        P.op("dve", lambda e: e.tensor_scalar(out=U1[64:128], in0=Aim[64:128, :, 0:16], scalar1=-1.0, scalar2=None, op0=ALU.mult), reads=["Aim"], writes=["U1"])
        P.op("dve", lambda e: e.tensor_scalar(out=U2[0:64], in0=Aim[0:64, :, 0:16], scalar1=-1.0, scalar2=None, op0=ALU.mult), reads=["Aim"], writes=["U2"])
        P.op("dve", lambda e: e.tensor_scalar(out=U2[64:128], in0=Are[64:128, :, 0:16], scalar1=-1.0, scalar2=None, op0=ALU.mult), reads=["Are"], writes=["U2"])
        for ci, (Csrc, Cdst, ckey) in enumerate(((C_re, Cre_sb, "Cre"), (C_im, Cim_sb, "Cim"))):
            rows = Csrc[gs].rearrange("g p n -> (g p) n")
            for dup in range(2):
                P.dma("sp", cnat[:, ci, dup, :], rows, writes=[("cnat", ci)])
            pt, pk = ps()
            P.op("pe", lambda e, ci=ci, pt=pt: e.transpose(out=pt[:, 0:128], in_=cnat[:, ci, :, :], identity=ident[:]), reads=[("cnat", ci), "ident"], writes=[pk])
            P.op("dve", lambda e, pt=pt, Cdst=Cdst: e.tensor_copy(out=Cdst, in_=pt[:, 0:128].rearrange("p (g q) -> p g q", q=16)), reads=[pk], writes=[ckey])
        P.op("dve", lambda e: e.tensor_tensor(out=Pb, in0=T1.unsqueeze(3).to_broadcast([128, 8, 8, 16]), in1=Bre_sb.unsqueeze(2).to_broadcast([128, 8, 8, 16]), op=ALU.mult), reads=["T1", "Bre"], writes=["Pb"])
        P.op("dve", lambda e: e.tensor_tensor(out=Pb2, in0=T2.unsqueeze(3).to_broadcast([128, 8, 8, 16]), in1=Bim_sb.unsqueeze(2).to_broadcast([128, 8, 8, 16]), op=ALU.mult), reads=["T2", "Bim"], writes=["Pb2"])
        P.op("dve", lambda e: e.tensor_tensor(out=Pb, in0=Pb, in1=Pb2, op=ALU.add), reads=["Pb", "Pb2"], writes=["Pb"])
        P.op("dve", lambda e: e.tensor_tensor(out=Qb, in0=U1.unsqueeze(3).to_broadcast([128, 8, 16, 16]), in1=Cre_sb.unsqueeze(2).to_broadcast([128, 8, 16, 16]), op=ALU.mult), reads=["U1", "Cre"], writes=["Qb"])
        P.op("dve", lambda e: e.tensor_tensor(out=Qb2, in0=U2.unsqueeze(3).to_broadcast([128, 8, 16, 16]), in1=Cim_sb.unsqueeze(2).to_broadcast([128, 8, 16, 16]), op=ALU.mult), reads=["U2", "Cim"], writes=["Qb2"])
        P.op("dve", lambda e: e.tensor_tensor(out=Qb, in0=Qb, in1=Qb2, op=ALU.add), reads=["Qb", "Qb2"], writes=["Qb"])
        for gl in range(8):
            g = bq * 8 + gl
            P.op("act", lambda e, gl=gl, g=g: e.activation(out=W2all[:, g, :].rearrange("p (t q) -> p t q", q=16), in_=Qb[:, gl, 8:16, :], func=AF.Copy), reads=["Qb"], writes=["W2all"])
            pt, pk = ps()
            P.op("pe", lambda e, gl=gl, pt=pt: e.transpose(out=pt[:, 0:128], in_=Pb[:, gl, :, :], identity=ident[:]), reads=["Pb", "ident"], writes=[pk])
            P.op("act", lambda e, g=g, pt=pt: e.activation(out=W1all[:, g, :], in_=pt[:, 0:128], func=AF.Copy), reads=[pk], writes=["W1all"])
            pt2, pk2 = ps()
            P.op("pe", lambda e, gl=gl, pt2=pt2: e.matmul(pt2[:, 0:128], lhsT=Pb[:, gl, :, :], rhs=Qb[:, gl, 0:8, :], start=True, stop=True), reads=["Pb", "Qb"], writes=[pk2])
            P.op("dve", lambda e, pt2=pt2: e.tensor_tensor(out=tmsk, in0=pt2[:, 0:128], in1=mbc[:], op=ALU.mult), reads=[pk2, "mbc"], writes=["tmsk"])
            P.op("dve", lambda e, g=g: e.scalar_tensor_tensor(out=Toep[:, g, :], in0=ident[:], scalar=Dcol[:, g:g + 1], in1=tmsk, op0=ALU.mult, op1=ALU.add), reads=["ident", "Dcol", "tmsk"], writes=["Toep"])
    P.barrier()

    Pswap = sb("Pswap", [128, 128])
    Bcat = sb("Bcat", [128, G])
    Bswp = sb("Bswp", [128, G])
    P.op("pool", lambda e: e.memset(Pswap[:], 0.0), writes=["Pswap"])
    P.op("pool", lambda e: e.affine_select(out=Pswap[:], in_=Pswap[:], pattern=[[1, 128]], compare_op=ALU.not_equal, fill=1.0, base=-64, channel_multiplier=-1), reads=["Pswap"], writes=["Pswap"])
    P.op("pool", lambda e: e.affine_select(out=Pswap[:], in_=Pswap[:], pattern=[[1, 128]], compare_op=ALU.not_equal, fill=1.0, base=64, channel_multiplier=-1), reads=["Pswap"], writes=["Pswap"])
    P.op("dve", lambda e: e.tensor_copy(out=Bcat[:], in_=A8im[:]), reads=["A8im"], writes=["Bcat"])
    P.op("dve", lambda e: e.tensor_scalar(out=Bcat[0:64], in0=Bcat[0:64], scalar1=-1.0, scalar2=None, op0=ALU.mult), reads=["Bcat"], writes=["Bcat"])
    P.op("dve", lambda e: e.tensor_scalar(out=Bswp[:], in0=Bcat[:], scalar1=-1.0, scalar2=None, op0=ALU.mult), reads=["Bcat"], writes=["Bswp"])
    S = sb("S", [128, H, 128])
    Sbf = sb("Sbf", [128, H, 128], BF16)
    Wd = sb("Wd", [128, 3, G])
    A2 = sb("A2", [128, 2, G])
    B2 = sb("B2", [128, 2, G])
    P.op("dve", lambda e: e.tensor_copy(out=A2[:, 0, :], in_=A8re[:]), reads=["A8re"], writes=["A2"])
    P.op("dve", lambda e: e.tensor_copy(out=A2[:, 1, :], in_=A8re[:]), reads=["A8re"], writes=["A2"])
    P.op("dve", lambda e: e.tensor_copy(out=B2[:, 0, :], in_=Bcat[:]), reads=["Bcat"], writes=["B2"])
    P.op("dve", lambda e: e.tensor_copy(out=B2[:, 1, :], in_=Bswp[:]), reads=["Bswp"], writes=["B2"])
    P.op("pool", lambda e: e.memset(S[:], 0.0), writes=[("S", h) for h in range(H)])
    P.op("pool", lambda e: e.memset(Sbf[:], 0.0), writes=[("Sbf", h) for h in range(H)])
    P.op("pool", lambda e: e.memset(Wd[:], 0.0), writes=["Wd", "Wd2"])

    hT = sb("hT", [128, KD, TB], BF16)
    ss = sb("ss", [128, 4])
    qT = sb("qT", [128, TB])
    fT = sb("fT", [128, TB])
    ogs = sb("ogs", [128, TB])
    itok = sb("itok", [128, NT, 128], BF16)
    is32 = sb("is32", [16, 128])
    lg2 = sb("lg", [128, 2, 128])
    bcs2 = sb("bcs", [128, 2, 128])
    ex4 = sb("ex", [128, 4, 128])
    kk2 = sb("kk", [128, 2, 128])
    ku2 = sb("ku", [128, 2, 128])
    nb2 = sb("nb", [128, 2, 2])
    qs2 = sb("qs", [128, 2, 128], BF16)
    ks2 = sb("ks", [128, 2, 128], BF16)
    qin2 = sb("qin", [128, 2, 128], BF16)
    kutok2 = sb("kutok", [128, 2, 128], BF16)
    scT2 = sb("scT", [128, 2, 128], BF16)
    ex_rr = [0]
    sload_next = [0]
    kks = sb("kks", [128, 16])
    ktok = sb("ktok", [16, 128])
    km = xn[0:16, 0:2048].rearrange("p (b k) -> p b k", b=16)
    NSB = 4
    s0b = sb("s0b", [128, NSB, 128])
    snb = sb("snb", [128, NSB, 128])
    oaT = sb("oaT", [128, H, TB], BF16)
    ucm = sb("ucm", [128, 8, 8, 16])
    ucms = xn[0:16, 0:1024].rearrange("p (g s q) -> p g s q", g=8, s=8)
    Uall = sb("Uall", [128, G, NCHT], BF16)
    XR = scrF[:, 0:G * NCHT].rearrange("p (g c) -> p g c", g=G)
    XI = scrF[:, G * NCHT:2 * G * NCHT].rearrange("p (g c) -> p g c", g=G)
    Hin = sb("Hin", [128, G, NCHT], BF16)
    st1 = sb("st1", [128, 2, G])
    st2 = sb("st2", [128, 2, G])
    X2v = scrF[:, 0:2 * G * NCHT].rearrange("p (r g c) -> p r g c", r=2, g=G)
    Zg = sb("Zg", [128, 2, NCHT])
    zcm = ucm[:].rearrange("p a b c -> p a (b c)")
    zcms = sb("zcms", [16, 128])
    zT = sb("zT", [128, GT, TB], BF16)
    obT = sb("obT", [128, GT, TB], BF16)
    h0nat = cnat
    HS_ALIAS = TB >= 256
    hsT = sb("hsT", [128, 2, 64])
    sgt = sb("sgt", [128, 2, TB])
    mt1 = sb("mt1", [128, TB])
    mt2 = sb("mt2", [128, TB])
    NTT = NT + 1
    x2 = scrF[:, 0:NTT * D].rearrange("p (i d) -> p i d", i=NTT)
    X2K = [("x2", i) for i in range(NTT)]
    osq = sgt[:, 0, :]
    if HS_ALIAS:
        hs0R = sgt[:, 0, 0:128].rearrange("p (g b) -> p g b", g=8)
        hs0I = sgt[:, 0, 128:256].rearrange("p (g b) -> p g b", g=8)
        hsR = sgt[:, 1, 0:128].rearrange("p (g b) -> p g b", g=8)
        hsI = sgt[:, 1, 128:256].rearrange("p (g b) -> p g b", g=8)
        KH = {"hs0R": ("sgt", 0), "hs0I": ("sgt", 0), "hsR": ("sgt", 1), "hsI": ("sgt", 1)}
    else:
        hs0R = sb("hs0R", [128, 8, NS])
        hs0I = sb("hs0I", [128, 8, NS])
        hsR = sb("hsR", [128, 8, NS])
        hsI = sb("hsI", [128, 8, NS])
        KH = {"hs0R": "hs0R", "hs0I": "hs0I", "hsR": "hsR", "hsI": "hsI"}
    rn = mt1
    otmp = mt2
    NHH = 4 if FT >= 8 else 2
    FTH = (FT + NHH - 1) // NHH
    mh = sb("mh", [128, max(KD, FTH), TB], BF16)
    mergedT = mh[:, 0:KD, :]
    hid = mh[:, 0:FTH, :]
    junk = mh[:].rearrange("p a b -> p (a b)")
    yt = xn

    def tiles(has_s):
        lst = [(i, 128, slice(i * 128, (i + 1) * 128)) for i in range(NT)]
        if has_s:
            lst.append((NT, NS, slice(BT, BT + NS)))
        return lst

    def rms_to_hT(src_fn, tl, gT, tag):
        for (i, rows, cols) in tl:
            src, skey = src_fn(i, rows)
            P.op("act", lambda e, src=src, rows=rows: e.activation(out=junk[0:rows, 0:D], in_=src, func=AF.Square, accum_out=ss[0:rows, 0:1]), reads=[skey], writes=["mh", "ss"])
            P.op("dve", lambda e, rows=rows: e.tensor_scalar(out=ss[0:rows, 1:2], in0=ss[0:rows, 0:1], scalar1=1.0 / D, scalar2=EPS, op0=ALU.mult, op1=ALU.add), reads=["ss"], writes=["ss"])
            P.op("act", lambda e, rows=rows: e.activation(out=ss[0:rows, 1:2], in_=ss[0:rows, 1:2], func=AF.Ln), reads=["ss"], writes=["ss"])
            P.op("act", lambda e, rows=rows: e.activation(out=ss[0:rows, 1:2], in_=ss[0:rows, 1:2], func=AF.Exp, scale=-0.5), reads=["ss"], writes=["ss"])
            P.op("dve", lambda e, src=src, rows=rows: e.tensor_scalar(out=xn[0:rows, 0:D], in0=src, scalar1=ss[0:rows, 1:2], scalar2=None, op0=ALU.mult), reads=["ss", skey], writes=["xn"])
            for k0 in range(0, KD, 4):
                nk = min(4, KD - k0)
                pt, pk = ps()
                for kq in range(nk):
                    k = k0 + kq
                    P.op("pe", lambda e, k=k, kq=kq, pt=pt, rows=rows: e.transpose(out=pt[:, kq * 128:kq * 128 + rows], in_=xn[0:rows, k * 128:(k + 1) * 128], identity=ident[0:rows, 0:rows]), reads=["xn", "ident"], writes=[pk], track=(kq == nk - 1))
                P.op("dve", lambda e, k0=k0, nk=nk, pt=pt, rows=rows, cols=cols: e.tensor_tensor(
                    out=hT[:, k0:k0 + nk, cols],
                    in0=pt[:, 0:nk * 128].rearrange("p (k c) -> p k c", k=nk)[:, :, 0:rows],
                    in1=gT[:, k0:k0 + nk].unsqueeze(2).to_broadcast([128, nk, rows]), op=ALU.mult), reads=[pk, tag], writes=["hT"])

    def proj_fm(wv, wk, kt, rhs_fn, ncols, rkeys):
        pt, pk = ps()
        for k in range(kt):
            P.op("pe", lambda e, k=k, pt=pt: e.matmul(pt[:, 0:ncols], lhsT=wv[:, k, :], rhs=rhs_fn(k), start=(k == 0), stop=(k == kt - 1)), reads=[wk] + rkeys, writes=[pk], track=(k == kt - 1))
        return pt, pk

    def do_block(xsrc, mode, has_s, ydst, last):
        main = (mode == "M")
        if P.dry:
            blk_marks.append(len(wspecs))
        bgq = (BGQ_PRE if not main else (BGQ_MAIN0 if has_s else 0))
        ncol = TB if has_s else BT
        tl = tiles(has_s)
        nchn = NCHT if has_s else NCH

        def src1(i, rows):
            if rows == 128:
                P.dma("sp", xn[:, 0:D], xsrc[i * 128:(i + 1) * 128, :], writes=["xn"])
            else:
                P.dma("sp", xn[0:rows, 0:D], xs, writes=["xn"])
            return xn[0:rows, 0:D], "xn"
        rms_to_hT(src1, tl, g1T, "g1T")
        hrhs = lambda k: hT[:, k, 0:ncol]

        if has_s:
            P.op("pool", lambda e: e.memset(ucms, 0.0), writes=["xn"])
        for ct in range(GT):
            bg_step(bgq)
            wu, wuk = load_w(w_in[:, 4 * HW + ct * 128:4 * HW + (ct + 1) * 128], KD, 128)
            pu, puk = proj_fm(wu, wuk, KD, hrhs, ncol, ["hT"])
            P.op("act", lambda e, pu=pu: e.activation(out=qT[:, 0:ncol], in_=pu[:, 0:ncol], func=AF.Copy), reads=[puk], writes=["qT"])
            for s0 in range(0, 8, 4):
                pt, pk = ps()
                for sq in range(4):
                    s = s0 + sq
                    P.op("pe", lambda e, s=s, sq=sq, pt=pt: e.transpose(out=pt[0:NCH, sq * 128:(sq + 1) * 128], in_=qT[:, 0:BT].rearrange("p (c s) -> p s c", s=8)[:, s, :], identity=ident[:]), reads=["qT", "ident"], writes=[pk], track=(sq == 3))
                P.op("act", lambda e, s0=s0, pt=pt: e.activation(out=ucm[0:NCH, :, s0:s0 + 4, :], in_=pt[0:NCH, :].rearrange("p (s g q) -> p g s q", s=4, g=8), func=AF.Copy), reads=[pk], writes=["ucm"])
            if has_s:
                pt, pk = ps()
                P.op("pe", lambda e, pt=pt: e.transpose(out=pt[0:NS, 0:128], in_=qT[:, BT:BT + NS], identity=ident[:]), reads=["qT", "ident"], writes=[pk])
                P.op("act", lambda e, pt=pt: e.activation(out=ucms[:, :, 7, :], in_=pt[0:NS, 0:128].rearrange("p (g q) -> p g q", g=8), func=AF.Copy), reads=[pk], writes=["xn"])
            assert 8 * NCHT <= 512
            pt, pk = ps()
            for gl in range(8):
                P.op("pe", lambda e, gl=gl, pt=pt: e.transpose(out=pt[:, gl * NCHT:gl * NCHT + NCH], in_=ucm[0:NCH, gl, :, :], identity=ident[0:NCH, 0:NCH]), reads=["ucm", "ident"], writes=[pk], track=(gl == 7 and not has_s))
            if has_s:
                for gl in range(8):
                    P.op("pe", lambda e, gl=gl, pt=pt: e.transpose(out=pt[:, gl * NCHT + NCH:(gl + 1) * NCHT], in_=ucms[:, gl, :, :], identity=ident[0:NS, 0:NS]), reads=["xn", "ident"], writes=[pk], track=(gl == 7))
            P.op("dve", lambda e, ct=ct, pt=pt: e.tensor_copy(out=Uall[:, ct * 8:(ct + 1) * 8, 0:nchn], in_=pt[:, 0:8 * NCHT].rearrange("p (g c) -> p g c", g=8)[:, :, 0:nchn]), reads=[pk], writes=[("Uall", ct)])
            pr, prk = ps()
            for gl in range(8):
                g = ct * 8 + gl
                P.op("pe", lambda e, g=g, gl=gl, pr=pr: e.matmul(pr[:, gl * NCHT:gl * NCHT + nchn], lhsT=W1all[:, g, :], rhs=Uall[:, g, 0:nchn], start=True, stop=True), reads=["W1all", ("Uall", ct)], writes=[prk], track=(gl == 7))
            P.op("act", lambda e, ct=ct, pr=pr: e.activation(out=XR[:, ct * 8:(ct + 1) * 8, 0:nchn], in_=pr[:, 0:8 * NCHT].rearrange("p (g c) -> p g c", g=8)[:, :, 0:nchn], func=AF.Copy), reads=[prk], writes=["XR"] + X2K)
        XRf = scrF[:, 0:G * NCHT]
        XIf = scrF[:, G * NCHT:2 * G * NCHT]
        for c0 in range(0, G * NCHT, 512):
            cw = min(512, G * NCHT - c0)
            pw_, pwk = ps()
            P.op("pe", lambda e, c0=c0, cw=cw, pw_=pw_: e.matmul(pw_[:, 0:cw], lhsT=Pswap[:], rhs=XRf[:, c0:c0 + cw], start=True, stop=True), reads=["Pswap", "XR"], writes=[pwk])
            P.op("act", lambda e, c0=c0, cw=cw, pw_=pw_: e.activation(out=XIf[:, c0:c0 + cw], in_=pw_[:, 0:cw], func=AF.Copy), reads=[pwk], writes=["XI"] + X2K)
        if main:
            P.op("act", lambda e: e.activation(out=Hin[:, :, 0], in_=Wd[:, 2, :], func=AF.Copy), reads=["Wd2"], writes=["Hin"])
        for c in range(NCH):
            P.op("pool", lambda e: e.tensor_tensor(out=st2[:], in0=B2[:], in1=Wd[:, 1:3, :], op=ALU.mult), reads=["B2", "Wd", "Wd2"], writes=["st2"])
            P.op("pool", lambda e, c=c: e.tensor_tensor(out=st2[:], in0=st2[:], in1=X2v[:, :, :, c], op=ALU.add), reads=["st2", "XR", "XI"], writes=["st2"])
            P.op("pool", lambda e: e.tensor_tensor(out=st1[:], in0=A2[:], in1=Wd[:, 0:2, :], op=ALU.mult), reads=["A2", "Wd"], writes=["st1"])
            P.op("pool", lambda e: e.tensor_tensor(out=Wd[:, 0:2, :], in0=st1[:], in1=st2[:], op=ALU.add), reads=["st1", "st2"], writes=["Wd"])
            P.op("pool", lambda e: e.tensor_copy(out=Wd[:, 2, :], in_=Wd[:, 0, :]), reads=["Wd"], writes=["Wd2"])
            if main and c + 1 < NCH:
                P.op("act", lambda e, c=c: e.activation(out=Hin[:, :, c + 1], in_=Wd[:, 2, :], func=AF.Copy), reads=["Wd2"], writes=["Hin"])
        if main and has_s:
            assert G % 64 == 0 or G * 2 <= 128
            NPAIR = NS // 2
            GG = G
            PW = 2 * GG
            v3 = lambda t: t[:].rearrange("p g b -> p (g b)")[:, 0:PW].rearrange("p (b g) -> p b g", b=2)
            hs0Rv, hs0Iv, hsRv, hsIv = v3(hs0R), v3(hs0I), v3(hsR), v3(hsI)
            bcA = lambda t: t[:, :].unsqueeze(1).to_broadcast([128, 2, GG])
            for bh in range(NPAIR):
                scol = slice(NCH + 2 * bh, NCH + 2 * bh + 2)
                for ci, (src, dst, dkey) in enumerate(((s5re0, hs0Rv, KH["hs0R"]), (s5im0, hs0Iv, KH["hs0I"]))):
                    P.dma("sp", h0nat[0:PW, ci, 0, :], src[2 * bh:2 * bh + 2, :, :].rearrange("b g n -> (b g) n"), writes=[("h0nat", ci)])
                    P.op("act", lambda e, ci=ci: e.activation(out=h0nat[0:PW, ci, 1, :], in_=h0nat[0:PW, ci, 0, :], func=AF.Copy), reads=[("h0nat", ci)], writes=[("h0nat", ci)])
                    pt, pk = ps()
                    P.op("pe", lambda e, ci=ci, pt=pt: e.transpose(out=pt[:, 0:PW], in_=h0nat[0:PW, ci, :, :], identity=ident[0:PW, 0:PW]), reads=[("h0nat", ci), "ident"], writes=[pk])
                    P.op("dve", lambda e, pt=pt, dst=dst: e.tensor_copy(out=dst, in_=pt[:, 0:PW].rearrange("p (b g) -> p b g", b=2)), reads=[pk], writes=[dkey])
                P.op("dve", lambda e: e.tensor_tensor(out=hsRv, in0=hs0Rv, in1=bcA(Am7re), op=ALU.mult), reads=[KH["hs0R"], "Am7re"], writes=[KH["hsR"]])
                P.op("dve", lambda e: e.tensor_tensor(out=hsIv, in0=hs0Iv, in1=bcA(Am7im), op=ALU.mult), reads=[KH["hs0I"], "Am7im"], writes=[KH["hsI"]])
                P.op("dve", lambda e: e.tensor_tensor(out=hsRv, in0=hsRv, in1=hsIv, op=ALU.subtract), reads=[KH["hsR"], KH["hsI"]], writes=[KH["hsR"]])
                P.op("dve", lambda e: e.tensor_tensor(out=hsIv, in0=hs0Rv, in1=bcA(Am7im), op=ALU.mult), reads=[KH["hs0R"], "Am7im"], writes=[KH["hsI"]])
                P.op("dve", lambda e: e.tensor_tensor(out=hs0Rv, in0=hs0Iv, in1=bcA(Am7re), op=ALU.mult), reads=[KH["hs0I"], "Am7re"], writes=[KH["hs0R"]])
                P.op("dve", lambda e: e.tensor_tensor(out=hsIv, in0=hsIv, in1=hs0Rv, op=ALU.add), reads=[KH["hsI"], KH["hs0R"]], writes=[KH["hsI"]])
                P.op("act", lambda e, scol=scol: e.activation(out=Hin[0:64, :, scol], in_=hsRv[0:64].rearrange("p b g -> p g b"), func=AF.Copy), reads=[KH["hsR"]], writes=["Hin"])
                P.op("act", lambda e, scol=scol: e.activation(out=Hin[64:128, :, scol], in_=hsIv[64:128].rearrange("p b g -> p g b"), func=AF.Copy), reads=[KH["hsI"]], writes=["Hin"])
                P.op("dve", lambda e: e.tensor_tensor(out=hs0Rv, in0=hsRv, in1=bcA(A8re), op=ALU.mult), reads=[KH["hsR"], "A8re"], writes=[KH["hs0R"]])
                P.op("dve", lambda e: e.tensor_tensor(out=hs0Iv, in0=hsIv, in1=bcA(A8im), op=ALU.mult), reads=[KH["hsI"], "A8im"], writes=[KH["hs0I"]])
                P.op("dve", lambda e: e.tensor_tensor(out=hs0Rv, in0=hs0Rv, in1=hs0Iv, op=ALU.subtract), reads=[KH["hs0R"], KH["hs0I"]], writes=[KH["hs0R"]])
                P.op("dve", lambda e: e.tensor_tensor(out=hs0Iv, in0=hsRv, in1=bcA(A8im), op=ALU.mult), reads=[KH["hsR"], "A8im"], writes=[KH["hs0I"]])
                P.op("dve", lambda e, scol=scol: e.tensor_tensor(out=hsRv, in0=hs0Rv, in1=XR[:, :, scol].rearrange("p g b -> p b g"), op=ALU.add), reads=[KH["hs0R"], "XR"], writes=[KH["hsR"]])
                P.op("dve", lambda e: e.tensor_tensor(out=hs0Rv, in0=hsIv, in1=bcA(A8re), op=ALU.mult), reads=[KH["hsI"], "A8re"], writes=[KH["hs0R"]])
                P.op("dve", lambda e: e.tensor_tensor(out=hs0Iv, in0=hs0Iv, in1=hs0Rv, op=ALU.add), reads=[KH["hs0I"], KH["hs0R"]], writes=[KH["hs0I"]])
                P.op("dve", lambda e, scol=scol: e.tensor_tensor(out=hsIv, in0=hs0Iv, in1=XI[:, :, scol].rearrange("p g b -> p b g"), op=ALU.add), reads=[KH["hs0I"], "XI"], writes=[KH["hsI"]])
                for ci, (srct, dsto, skey) in enumerate(((hsRv, s5re_s, KH["hsR"]), (hsIv, s5im_s, KH["hsI"]))):
                    pt, pk = ps()
                    P.op("pe", lambda e, srct=srct, pt=pt: e.transpose(out=pt[0:PW, 0:64], in_=srct[0:64, :, :], identity=ident[0:64, 0:64]), reads=[skey, "ident"], writes=[pk])
                    P.op("dve", lambda e, ci=ci, pt=pt: e.tensor_copy(out=hsT[0:PW, ci, :], in_=pt[0:PW, 0:64]), reads=[pk], writes=[("hsT", ci)])
                    P.dma("sp", dsto[2 * bh:2 * bh + 2, :, :].rearrange("b g n -> (b g) n"), hsT[0:PW, ci, :], reads=[("hsT", ci)], is_output=True)
        for h in range(H):
            bg_step(bgq)
            c0 = h * 128
            if main:
                wq, wqk = load_w(w_in[:, c0:c0 + 128], KD, 128)
                pq, pqk = proj_fm(wq, wqk, KD, hrhs, ncol, ["hT"])
                P.op("act", lambda e, pq=pq: e.activation(out=qT[:, 0:ncol], in_=pq[:, 0:ncol], func=AF.Copy), reads=[pqk], writes=["qT"])
            wf, wfk = load_w(w_in[:, HW + c0:HW + c0 + 128], KD, 128)
            pf, pfk = proj_fm(wf, wfk, KD, hrhs, ncol, ["hT"])
            P.op("act", lambda e, pf=pf: e.activation(out=fT[:, 0:ncol], in_=pf[:, 0:ncol], func=AF.Sigmoid), reads=[pfk], writes=["fT"])
            P.op("dve", lambda e, h=h: e.tensor_scalar(out=fT[:, 0:ncol], in0=fT[:, 0:ncol], scalar1=omlT[:, h:h + 1], scalar2=lbT[:, h:h + 1], op0=ALU.mult, op1=ALU.add), reads=["fT", "omlT", "lbT"], writes=["fT"])
            wi, wik = load_w(w_in[:, 2 * HW + c0:2 * HW + c0 + 128], KD, 128)
            for (i, rows, cols) in tl:
                pt, pk = ps()
                for k in range(KD):
                    P.op("pe", lambda e, k=k, pt=pt, rows=rows, cols=cols: e.matmul(pt[0:rows, 0:128], lhsT=hT[:, k, cols], rhs=wi[:, k, :], start=(k == 0), stop=(k == KD - 1)), reads=[wik, "hT"], writes=[pk], track=(k == KD - 1))
                if rows == 128:
                    P.op("act", lambda e, pt=pt, i=i: e.activation(out=itok[:, i, :], in_=pt[:, 0:128], func=AF.Copy), reads=[pk], writes=["itok"])
                else:
                    P.op("act", lambda e, pt=pt: e.activation(out=is32[:, :], in_=pt[0:NS, 0:128], func=AF.Copy), reads=[pk], writes=["is32"])
            if main:
                wo, wok = load_w(w_in[:, 3 * HW + c0:3 * HW + c0 + 128], KD, 128)
                po_, pok = proj_fm(wo, wok, KD, hrhs, ncol, ["hT"])
                P.op("act", lambda e, po_=po_: e.activation(out=ogs[:, 0:ncol], in_=po_[:, 0:ncol], func=AF.Silu), reads=[pok], writes=["ogs"])
                pO, pOk = psb[NPS - 1], ("ps", NPS - 1)
            for j in range(NT):
                cols = slice(j * 128, (j + 1) * 128)
                p = (h * NT + j) % 2
                lg, bcs, kk, ku, nb = lg2[:, p, :], bcs2[:, p, :], kk2[:, p, :], ku2[:, p, :], nb2[:, p, :]
                qs, ks, qin, kutok, scT = qs2[:, p, :], ks2[:, p, :], qin2[:, p, :], kutok2[:, p, :], scT2[:, p, :]
                Klg, Kbcs, Kkk, Kku, Knb = ("lg", p), ("bcs", p), ("kk", p), ("ku", p), ("nb", p)
                Kqs, Kks, Kqin, Kkutok, KscT = ("qs", p), ("ks", p), ("qin", p), ("kutok", p), ("scT", p)
                KS, KSbf = ("S", h), ("Sbf", h)

                def exbuf():
                    i = ex_rr[0]
                    if not P.dry:
                        ex_rr[0] = (i + 1) % 4
                    return ex4[:, i, :], ("ex", i)
                P.op("act", lambda e, cols=cols: e.activation(out=lg, in_=fT[:, cols], func=AF.Ln), reads=["fT"], writes=[Klg])
                P.op("dve", lambda e: e.tensor_tensor_scan(out=bcs, data0=ones[:], data1=lg, initial=0.0, op0=ALU.mult, op1=ALU.add), reads=["ones", Klg], writes=[Kbcs])
                P.op("dve", lambda e, cols=cols: e.tensor_scalar(out=kk, in0=fT[:, cols], scalar1=-1.0, scalar2=1.0, op0=ALU.mult, op1=ALU.add), reads=["fT"], writes=[Kkk])
                e4, Ke4 = exbuf()
                P.op("act", lambda e: e.activation(out=e4, in_=bcs, func=AF.Exp, bias=bcs[:, 127:128], scale=-1.0), reads=[Kbcs], writes=[Ke4])
                P.op("dve", lambda e: e.tensor_tensor(out=ku, in0=kk, in1=e4, op=ALU.mult), reads=[Kkk, Ke4], writes=[Kku])
                P.op("act", lambda e: e.activation(out=nb[:, 1:2], in_=bcs[:, 127:128], func=AF.Exp), reads=[Kbcs], writes=[Knb])
                ptk, ptkk = ps()
                P.op("pe", lambda e, ptk=ptk: e.transpose(out=ptk[:, 0:128], in_=ku, identity=ident[:]), reads=[Kku, "ident"], writes=[ptkk])
                P.op("act", lambda e, ptk=ptk: e.activation(out=kutok, in_=ptk[:, 0:128], func=AF.Copy), reads=[ptkk], writes=[Kkutok])
                if main:
                    P.op("dve", lambda e: e.tensor_scalar(out=nb[:, 0:1], in0=bcs[:, 64:65], scalar1=-1.0, scalar2=None, op0=ALU.mult), reads=[Kbcs], writes=[Knb])
                    e1, Ke1 = exbuf()
                    P.op("act", lambda e: e.activation(out=e1, in_=bcs, func=AF.Exp, bias=nb[:, 0:1], scale=1.0), reads=[Kbcs, Knb], writes=[Ke1])
                    P.op("dve", lambda e, cols=cols: e.tensor_tensor(out=qs, in0=qT[:, cols], in1=e1, op=ALU.mult), reads=["qT", Ke1], writes=[Kqs])
                    e2, Ke2 = exbuf()
                    P.op("act", lambda e: e.activation(out=e2, in_=bcs, func=AF.Exp, bias=bcs[:, 64:65], scale=-1.0), reads=[Kbcs], writes=[Ke2])
                    P.op("dve", lambda e: e.tensor_tensor(out=ks, in0=kk, in1=e2, op=ALU.mult), reads=[Kkk, Ke2], writes=[Kks])
                    e3, Ke3 = exbuf()
                    P.op("act", lambda e: e.activation(out=e3, in_=bcs, func=AF.Exp), reads=[Kbcs], writes=[Ke3])
                    P.op("dve", lambda e, cols=cols: e.tensor_tensor(out=qin, in0=qT[:, cols], in1=e3, op=ALU.mult), reads=["qT", Ke3], writes=[Kqin])
                    psc, psck = ps()
                    P.op("pe", lambda e, psc=psc: e.matmul(psc[:, 0:128], lhsT=ks, rhs=qs, start=True, stop=True), reads=[Kks, Kqs], writes=[psck])
                    P.op("dve", lambda e, psc=psc: e.tensor_tensor(out=scT, in0=psc[:, 0:128], in1=mut[:], op=ALU.mult), reads=[psck, "mut"], writes=[KscT])
                    P.op("pe", lambda e, j=j, cols=cols: e.matmul(pO[:, cols], lhsT=itok[:, j, :], rhs=scT, start=True, stop=False), reads=["itok", KscT], writes=[pOk], track=False)
                    P.op("pe", lambda e, h=h, cols=cols: e.matmul(pO[:, cols], lhsT=Sbf[:, h, :], rhs=qin, start=False, stop=True), reads=[KSbf, Kqin], writes=[pOk])
                pS, pSk = ps()
                P.op("pe", lambda e, j=j, pS=pS: e.matmul(pS[:, 0:128], lhsT=kutok, rhs=itok[:, j, :], start=True, stop=True), reads=[Kkutok, "itok"], writes=[pSk])
                P.op("dve", lambda e, h=h, pS=pS: e.scalar_tensor_tensor(out=S[:, h, :], in0=S[:, h, :], scalar=nb[:, 1:2], in1=pS[:, 0:128], op0=ALU.mult, op1=ALU.add), reads=[KS, Knb, pSk], writes=[KS])
                P.op("act", lambda e, h=h: e.activation(out=Sbf[:, h, :], in_=S[:, h, :], func=AF.Copy), reads=[KS], writes=[KSbf])
            if main and has_s:
                cs = slice(BT, BT + NS)
                P.op("dve", lambda e: e.tensor_scalar(out=kks[:], in0=fT[:, cs], scalar1=-1.0, scalar2=1.0, op0=ALU.mult, op1=ALU.add), reads=["fT"], writes=["kks"])
                ptk, ptkk = ps()
                P.op("pe", lambda e, ptk=ptk: e.transpose(out=ptk[0:NS, 0:128], in_=kks[:], identity=ident[:]), reads=["kks", "ident"], writes=[ptkk])
                P.op("act", lambda e, ptk=ptk: e.activation(out=ktok[:], in_=ptk[0:NS, 0:128], func=AF.Copy), reads=[ptkk], writes=["ktok"])
                P.op("dve", lambda e: e.tensor_tensor(out=km[:], in0=ktok[:].unsqueeze(1).to_broadcast([NS, NS, 128]), in1=ident[0:NS, 0:NS].unsqueeze(2).to_broadcast([NS, NS, 128]), op=ALU.mult), reads=["ktok", "ident"], writes=["xn"])
                for b in range(NS + 1):
                    if b < NS:
                        idx = h * NS + b
                        sl = idx % NSB
                        if not P.dry:
                            while sload_next[0] < min(idx + NSB, H * NS):
                                i2 = sload_next[0]
                                P.dma("sp", s0b[:, i2 % NSB, :], sh0[i2 % NS, i2 // NS], writes=[("s0b", i2 % NSB)])
                                sload_next[0] += 1
                        pkv, pkvk = ps()
                        P.op("pe", lambda e, b=b, pkv=pkv: e.matmul(pkv[:, 0:128], lhsT=km[:, b, :], rhs=is32[:, :], start=True, stop=True), reads=["xn", "is32"], writes=[pkvk])
                        P.op("dve", lambda e, b=b, sl=sl, pkv=pkv: e.scalar_tensor_tensor(out=snb[:, sl, :], in0=s0b[:, sl, :], scalar=fT[:, BT + b:BT + b + 1], in1=pkv[:, 0:128], op0=ALU.mult, op1=ALU.add), reads=[("s0b", sl), "fT", pkvk], writes=[("snb", sl)])
                        P.dma("sp", sh_s[b, h], snb[:, sl, :], reads=[("snb", sl)], is_output=True)
                    if b >= 1:
                        bp = b - 1
                        slp = (h * NS + bp) % NSB
                        P.op("pe", lambda e, bp=bp, slp=slp: e.matmul(pO[:, BT + bp:BT + bp + 1], lhsT=snb[:, slp, :], rhs=qT[:, BT + bp:BT + bp + 1], start=True, stop=True), reads=[("snb", slp), "qT"], writes=[pOk])
            if main:
                P.op("act", lambda e: e.activation(out=osq[:, 0:ncol], in_=pO[:, 0:ncol], func=AF.Square), reads=[pOk], writes=[("sgt", 0)])
                pn, pnk = ps()
                P.op("pe", lambda e, pn=pn: e.matmul(pn[:, 0:ncol], lhsT=ones[:], rhs=osq[:, 0:ncol], start=True, stop=True), reads=["ones", ("sgt", 0)], writes=[pnk])
                P.op("dve", lambda e, pn=pn: e.tensor_scalar(out=rn[:, 0:ncol], in0=pn[:, 0:ncol], scalar1=1.0 / 128, scalar2=EPS, op0=ALU.mult, op1=ALU.add), reads=[pnk], writes=["mt1"])
                P.op("act", lambda e: e.activation(out=rn[:, 0:ncol], in_=rn[:, 0:ncol], func=AF.Ln), reads=["mt1"], writes=["mt1"])
                P.op("act", lambda e: e.activation(out=rn[:, 0:ncol], in_=rn[:, 0:ncol], func=AF.Exp, scale=-0.5), reads=["mt1"], writes=["mt1"])
                P.op("dve", lambda e, h=h: e.scalar_tensor_tensor(out=otmp[:, 0:ncol], in0=pO[:, 0:ncol], scalar=ghT[:, h:h + 1], in1=rn[:, 0:ncol], op0=ALU.mult, op1=ALU.mult), reads=[pOk, "ghT", "mt1"], writes=["mt2"])
                P.op("dve", lambda e, h=h: e.tensor_tensor(out=oaT[:, h, 0:ncol], in0=otmp[:, 0:ncol], in1=ogs[:, 0:ncol], op=ALU.mult), reads=["mt2", "ogs"], writes=["oaT"])

        if main:
            for ct in range(GT):
                for gl in range(8):
                    g = ct * 8 + gl
                    zs = g % 2
                    py, pyk = ps()
                    P.op("pe", lambda e, g=g, py=py: e.matmul(py[:, 0:nchn], lhsT=Toep[:, g, :], rhs=Uall[:, g, 0:nchn], start=True, stop=False), reads=["Toep", ("Uall", ct)], writes=[pyk], track=False)
                    P.op("pe", lambda e, g=g, py=py: e.matmul(py[:, 0:nchn], lhsT=W2all[:, g, :], rhs=Hin[:, g, 0:nchn], start=False, stop=True), reads=["W2all", "Hin"], writes=[pyk])
                    P.op("act", lambda e, zs=zs, py=py: e.activation(out=Zg[:, zs, 0:nchn], in_=py[:, 0:nchn], func=AF.Gelu), reads=[pyk], writes=[("Zg", zs)])
                    pz, pzk = ps()
                    P.op("pe", lambda e, zs=zs, pz=pz: e.transpose(out=pz[0:NCH, 0:128], in_=Zg[:, zs, 0:NCH], identity=ident[:]), reads=[("Zg", zs), "ident"], writes=[pzk], track=not has_s)
                    if has_s:
                        P.op("pe", lambda e, zs=zs, pz=pz: e.transpose(out=pz[0:NS, 128:256], in_=Zg[:, zs, NCH:NCHT], identity=ident[:]), reads=[("Zg", zs), "ident"], writes=[pzk])
                    P.op("dve", lambda e, gl=gl, pz=pz: e.tensor_copy(out=zcm[0:NCH, :, gl * 16:(gl + 1) * 16], in_=pz[0:NCH, 0:128].rearrange("p (t q) -> p t q", q=16)), reads=[pzk], writes=["ucm"])
                    if has_s:
                        P.op("dve", lambda e, gl=gl, pz=pz: e.tensor_copy(out=zcms[:, gl * 16:(gl + 1) * 16], in_=pz[0:NS, 128 + 112:256]), reads=[pzk], writes=["zcms"])
                pt, pk = ps()
                for t in range(8):
                    P.op("pe", lambda e, t=t, pt=pt: e.transpose(out=pt[:, t * NCH:(t + 1) * NCH], in_=zcm[0:NCH, t, :], identity=ident[0:NCH, 0:NCH]), reads=["ucm", "ident"], writes=[pk], track=(t == 7 and not has_s))
                if has_s:
                    P.op("pe", lambda e, pt=pt: e.transpose(out=pt[:, 8 * NCH:8 * NCH + NS], in_=zcms[:, :], identity=ident[0:NS, 0:NS]), reads=["zcms", "ident"], writes=[pk])
                P.op("dve", lambda e, ct=ct, pt=pt: e.tensor_copy(out=zT[:, ct, 0:BT].rearrange("p (c t) -> p t c", t=8), in_=pt[:, 0:8 * NCH].rearrange("p (t c) -> p t c", t=8)), reads=[pk], writes=["zT"])
                if has_s:
                    P.op("dve", lambda e, ct=ct, pt=pt: e.tensor_copy(out=zT[:, ct, BT:BT + NS], in_=pt[:, 8 * NCH:8 * NCH + NS]), reads=[pk], writes=["zT"])
        if not main:
            return

        zrhs = lambda k: zT[:, k, 0:ncol]
        for m in range(GT):
            wl, wlk = load_w(w_glu[:, m * 128:(m + 1) * 128], GT, 128)
            pl, plk = proj_fm(wl, wlk, GT, zrhs, ncol, ["zT"])
            wg, wgk = load_w(w_glu[:, SW + m * 128:SW + (m + 1) * 128], GT, 128)
            pg, pgk = proj_fm(wg, wgk, GT, zrhs, ncol, ["zT"])
            P.op("act", lambda e, pg=pg: e.activation(out=sgt[:, 0, 0:ncol], in_=pg[:, 0:ncol], func=AF.Sigmoid), reads=[pgk], writes=[("sgt", 0)])
            P.op("dve", lambda e, m=m, pl=pl: e.tensor_tensor(out=obT[:, m, 0:ncol], in0=pl[:, 0:ncol], in1=sgt[:, 0, 0:ncol], op=ALU.mult), reads=[plk, ("sgt", 0)], writes=["obT"])

        GA0 = 4 * HW + SW
        for m in range(KD):
            wa, wak = load_w(w_in[:, GA0 + m * 128:GA0 + (m + 1) * 128], KD, 128)
            pga, pgak = proj_fm(wa, wak, KD, hrhs, ncol, ["hT"])
            P.op("act", lambda e, pga=pga: e.activation(out=sgt[:, 0, 0:ncol], in_=pga[:, 0:ncol], func=AF.Sigmoid), reads=[pgak], writes=[("sgt", 0)])
            wb, wbk = load_w(w_in[:, GA0 + D + m * 128:GA0 + D + (m + 1) * 128], KD, 128)
            pgb, pgbk = proj_fm(wb, wbk, KD, hrhs, ncol, ["hT"])
            P.op("act", lambda e, pgb=pgb: e.activation(out=sgt[:, 1, 0:ncol], in_=pgb[:, 0:ncol], func=AF.Sigmoid), reads=[pgbk], writes=[("sgt", 1)])
            wpa, wpak = load_w(w_pa[:, m * 128:(m + 1) * 128], H, 128)
            ppa, ppak = proj_fm(wpa, wpak, H, lambda k: oaT[:, k, 0:ncol], ncol, ["oaT"])
            P.op("dve", lambda e, ppa=ppa: e.tensor_tensor(out=mt1[:, 0:ncol], in0=ppa[:, 0:ncol], in1=sgt[:, 0, 0:ncol], op=ALU.mult), reads=[ppak, ("sgt", 0)], writes=["mt1"])
            wpb, wpbk = load_w(w_pb[:, m * 128:(m + 1) * 128], GT, 128)
            ppb, ppbk = proj_fm(wpb, wpbk, GT, lambda k: obT[:, k, 0:ncol], ncol, ["obT"])
            P.op("dve", lambda e, ppb=ppb: e.tensor_tensor(out=mt2[:, 0:ncol], in0=ppb[:, 0:ncol], in1=sgt[:, 1, 0:ncol], op=ALU.mult), reads=[ppbk, ("sgt", 1)], writes=["mt2"])
            P.op("dve", lambda e, m=m: e.tensor_tensor(out=mergedT[:, m, 0:ncol], in0=mt1[:, 0:ncol], in1=mt2[:, 0:ncol], op=ALU.add), reads=["mt1", "mt2"], writes=["mh"])

        def tm_matmul(wsrc, nk, lhs_fn, lkeys, finish_fn):
            for db in range(NDB):
                accs = [ps() for _ in tl]
                k = 0
                while k < nk:
                    nq = min(WSLOT // DB, nk - k)
                    wv, wk = load_w(wsrc[k * 128:(k + nq) * 128, db * DB:(db + 1) * DB], nq, DB)
                    for ti, (i, rows, cols) in enumerate(tl):
                        pa_, pak = accs[ti]
                        for q in range(nq):
                            P.op("pe", lambda e, k=k, q=q, pa_=pa_, rows=rows, cols=cols, wv=wv: e.matmul(pa_[0:rows, 0:DB], lhsT=lhs_fn(k + q, cols), rhs=wv[:, q, :], start=(k + q == 0), stop=(k + q == nk - 1)), reads=[wk] + lkeys, writes=[pak], track=(q == nq - 1))
                    k += nq
                for ti, (i, rows, cols) in enumerate(tl):
                    finish_fn(i, rows, db, accs[ti][0], accs[ti][1])

        for (i, rows, cols) in tl:
            srcx = xsrc[i * 128:(i + 1) * 128, :] if rows == 128 else xs
            P.dma("sp", x2[0:rows, i, :], srcx, writes=[("x2", i), "XR", "XI"])

        def fin_out(i, rows, db, pa_, pak):
            P.op("dve", lambda e: e.tensor_tensor(out=x2[0:rows, i, db * DB:(db + 1) * DB], in0=pa_[0:rows, 0:DB], in1=x2[0:rows, i, db * DB:(db + 1) * DB], op=ALU.add), reads=[pak, ("x2", i)], writes=[("x2", i)])
        tm_matmul(w_out, KD, lambda k, cols: mergedT[:, k, cols], ["mh"], fin_out)

        rms_to_hT(lambda i, rows: (x2[0:rows, i, :], ("x2", i)), tl, g2T, "g2T")

        for hh in range(NHH):
            j0 = hh * FTH
            nj = min(FTH, FT - j0)
            if nj <= 0:
                continue
            for jl in range(nj):
                j = j0 + jl
                wa, wak = load_w(w_up[:, j * 128:(j + 1) * 128], KD, 128)
                pa_, pak = proj_fm(wa, wak, KD, hrhs, ncol, ["hT"])
                P.op("act", lambda e, pa_=pa_: e.activation(out=sgt[:, 0, 0:ncol], in_=pa_[:, 0:ncol], func=AF.Silu), reads=[pak], writes=[("sgt", 0)])
                wb, wbk = load_w(w_up[:, FH + j * 128:FH + (j + 1) * 128], KD, 128)
                pb_, pbk = proj_fm(wb, wbk, KD, hrhs, ncol, ["hT"])
                P.op("dve", lambda e, jl=jl, pb_=pb_: e.tensor_tensor(out=hid[:, jl, 0:ncol], in0=pb_[:, 0:ncol], in1=sgt[:, 0, 0:ncol], op=ALU.mult), reads=[pbk, ("sgt", 0)], writes=["mh"])

            def fin_dn(i, rows, db, pa_, pak):
                P.op("dve", lambda e: e.tensor_tensor(out=x2[0:rows, i, db * DB:(db + 1) * DB], in0=pa_[0:rows, 0:DB], in1=x2[0:rows, i, db * DB:(db + 1) * DB], op=ALU.add), reads=[pak, ("x2", i)], writes=[("x2", i)])
            tm_matmul(w_down[j0 * 128:(j0 + nj) * 128, :], nj, lambda k, cols: hid[:, k, cols], ["mh"], fin_dn)

        for (i, rows, cols) in tl:
            P.op("act", lambda e, i=i, rows=rows: e.activation(out=junk[0:rows, 0:D], in_=x2[0:rows, i, :], func=AF.Square, accum_out=ss[0:rows, 2:3]), reads=[("x2", i)], writes=["mh", "ss"])
            P.op("dve", lambda e, rows=rows: e.tensor_scalar(out=ss[0:rows, 3:4], in0=ss[0:rows, 2:3], scalar1=1.0 / D, scalar2=EPS, op0=ALU.mult, op1=ALU.add), reads=["ss"], writes=["ss"])
            P.op("act", lambda e, rows=rows: e.activation(out=ss[0:rows, 3:4], in_=ss[0:rows, 3:4], func=AF.Ln), reads=["ss"], writes=["ss"])
            P.op("act", lambda e, rows=rows: e.activation(out=ss[0:rows, 3:4], in_=ss[0:rows, 3:4], func=AF.Exp, scale=-0.5), reads=["ss"], writes=["ss"])
            P.op("dve", lambda e, i=i, rows=rows: e.scalar_tensor_tensor(out=yt[0:rows, 0:D], in0=x2[0:rows, i, :], scalar=ss[0:rows, 3:4], in1=fgB[0:rows, :], op0=ALU.mult, op1=ALU.mult), reads=[("x2", i), "ss", "fgB"], writes=["xn"])
            if rows == 128:
                P.dma("sp", ydst[i * 128:(i + 1) * 128, :], yt[:, 0:D], reads=["xn"], is_output=True)
            else:
                P.dma("sp", y_s, yt[0:rows, 0:D], reads=["xn"], is_output=True)

    BGQ_PRE = cfg.get("BGQ_PRE", 0)
    BGQ_MAIN0 = cfg.get("BGQ_MAIN0", 0)

    def all_blocks():
        for bi in range(NPRE):
            do_block(xpre[bi * BT:(bi + 1) * BT, :], "P", False, None, False)
        for bi in range(NMAIN):
            do_block(xmain[bi * BT:(bi + 1) * BT, :], "M", bi == 0, y_main[bi * BT:(bi + 1) * BT, :], bi == NMAIN - 1)
    P.dry = True
    all_blocks()
    P.dry = False
    setup_wscr()
    all_blocks()
    assert w_idx[0] == len(wspecs)

    P.dma("sp", sh_p.rearrange("h k v -> k h v"), S[:], reads=[("S", h) for h in range(H)], is_output=True)
    P.dma("sp", s5re_p.rearrange("g n -> n g"), Wd[0:64, 0, :], reads=["Wd"], is_output=True, allow_slow_non_contiguous=True)
    P.dma("sp", s5im_p.rearrange("g n -> n g"), Wd[64:128, 0, :], reads=["Wd"], is_output=True, allow_slow_non_contiguous=True)

    SBUF_LEFT[0] = nc.sbuf_bytes_remaining
    P.emit()
    for cm in reversed(ctxs):
        cm.__exit__(None, None, None)
    P.close()
    return nc


WEIGHT_NAMES = ["lb_logits", "norm1_g", "w_in", "hgrn_norm_g", "s5_lam_re", "s5_lam_im", "s5_log_dt",
                "s5_B_re", "s5_B_im", "s5_C_re", "s5_C_im", "s5_D", "w_s5_glu", "w_proj_a", "w_proj_b",
                "w_out", "norm2_g", "w_ffn_up", "w_ffn_down"]


def make_in_maps(inputs, n_cores, half, ns):
    f32 = lambda a: np.ascontiguousarray(np.asarray(a, dtype=np.float32))
    shared = {k: f32(inputs[k][0]) for k in WEIGHT_NAMES if k != "lb_logits"}
    shared["lb_logits"] = f32(inputs["lb_logits"])
    shared["final_norm_g"] = f32(inputs["final_norm_g"])
    xp = np.asarray(inputs["x_prompt"], dtype=np.float32)
    xsm = np.asarray(inputs["x_sample"], dtype=np.float32)
    maps = []
    for c in range(n_cores):
        b, hf = c // 2, c % 2
        m = dict(shared)
        m["xmain"] = f32(xp[b, hf * half:(hf + 1) * half])
        m["xpre"] = f32(xp[b, 0:half]) if hf == 1 else np.zeros((half, xp.shape[2]), np.float32)
        m["xs"] = f32(xsm[c * ns:(c + 1) * ns, 0])
        m["sh0"] = f32(inputs["state_hgrn"][0, c * ns:(c + 1) * ns])
        m["s5re0"] = f32(inputs["state_s5_re"][0, c * ns:(c + 1) * ns])
        m["s5im0"] = f32(inputs["state_s5_im"][0, c * ns:(c + 1) * ns])
        maps.append(m)
    return maps


def assemble(results, n_cores, nb):
    y_prompt = np.stack([np.concatenate([results[2 * b]["y_main"], results[2 * b + 1]["y_main"]], axis=0) for b in range(nb)])
    y_sample = np.concatenate([results[c]["y_s"] for c in range(n_cores)], axis=0)[:, None, :]
    shp = np.stack([results[2 * b + 1]["sh_p"] for b in range(nb)])[None]
    rep = np.stack([results[2 * b + 1]["s5re_p"] for b in range(nb)])[None]
    imp = np.stack([results[2 * b + 1]["s5im_p"] for b in range(nb)])[None]
    shs = np.concatenate([results[c]["sh_s"] for c in range(n_cores)], axis=0)[None]
    res = np.concatenate([results[c]["s5re_s"] for c in range(n_cores)], axis=0)[None]
    ims = np.concatenate([results[c]["s5im_s"] for c in range(n_cores)], axis=0)[None]
    return tuple(np.ascontiguousarray(a, dtype=np.float32) for a in (y_prompt, y_sample, shp, rep, imp, shs, res, ims))


def kernel(**inputs):
    n = 8
    cfg = FULL_CFG
    nc = build(cfg)
    maps = make_in_maps(inputs, n, cfg["BT"] * cfg["NMAIN"], cfg["NS"])
    res = run_bass_kernel_spmd(nc, maps, core_ids=list(range(n)))
    return assemble(res.results, n, 4)
```

```python
import math
import types
import numpy as np
import concourse.bass as bass
import concourse.mybir as mybir
from concourse.bass_utils import run_bass_kernel_spmd

F32 = mybir.dt.float32
BF16 = mybir.dt.bfloat16
AF = mybir.ActivationFunctionType
ALU = mybir.AluOpType
ENGS = ("pe", "act", "dve", "pool", "sp")
EPS = 1e-6


def _freeze(fn):
    if fn.__closure__ is None:
        return fn
    cells = []
    for c in fn.__closure__:
        try:
            cells.append(types.CellType(c.cell_contents))
        except ValueError:
            cells.append(c)
    return types.FunctionType(fn.__code__, fn.__globals__, fn.__name__, fn.__defaults__, tuple(cells))


class Prog:
    def __init__(self, nc, n_dma_sems=40):
        self.nc = nc
        self.ops = {e: [] for e in ENGS}
        self.cnt = {e: 0 for e in ENGS}
        self.sems = {}
        self.known = {e: {} for e in ENGS}
        self.last_w = {}
        self.readers = {}
        self.n_dma_sems = n_dma_sems
        self.dma_sems = []
        self.dma_cnt = []
        self.dma_rr = 0
        self.out_tokens = []
        self._ctx = []
        self.dry = False

    def setup(self):
        nc = self.nc
        for e in ("pe", "act", "dve", "pool"):
            cm = nc.semaphore("s_" + e)
            self.sems[e] = cm.__enter__()
            self._ctx.append(cm)
        for i in range(self.n_dma_sems):
            cm = nc.semaphore("d_%d" % i)
            self.dma_sems.append(cm.__enter__())
            self._ctx.append(cm)
            self.dma_cnt.append(0)

    def close(self):
        for cm in reversed(self._ctx):
            cm.__exit__(None, None, None)

    def _waits(self, eng, reads, writes):
        toks = []
        for k in reads:
            t = self.last_w.get(k)
            if t is not None:
                toks.append(t)
        for k in writes:
            t = self.last_w.get(k)
            if t is not None:
                toks.append(t)
            toks.extend(self.readers.get(k, ()))
        need = {}
        for (sem, val, teng) in toks:
            if teng == "pe" and eng == "pe":
                continue
            key = id(sem)
            if self.known[eng].get(key, 0) >= val:
                continue
            if key not in need or need[key][1] < val:
                need[key] = (sem, val)
        for key, (sem, val) in need.items():
            self.known[eng][key] = val
        return list(need.values())

    def _record(self, tok, reads, writes):
        for k in writes:
            self.last_w[k] = tok
            self.readers[k] = []
        for k in reads:
            lst = self.readers.setdefault(k, [])
            lst.append(tok)
            if len(lst) > 64:
                best = {}
                for t in lst:
                    if id(t[0]) not in best or best[id(t[0])][1] < t[1]:
                        best[id(t[0])] = t
                self.readers[k] = list(best.values())

    def op(self, eng, fn, reads=(), writes=(), track=True):
        if self.dry:
            return None
        fn = _freeze(fn)
        waits = self._waits(eng, reads, writes)
        tok = (self.sems[eng], self.cnt[eng] + 1, eng)
        if track:
            self.cnt[eng] += 1
        self.ops[eng].append((waits, fn, (self.sems[eng], 1) if track else None))
        self._record(tok, reads, writes)
        return tok

    def dma(self, eng, out, in_, reads=(), writes=(), is_output=False, **kw):
        if self.dry:
            return None
        j = self.dma_rr
        self.dma_rr = (self.dma_rr + 1) % self.n_dma_sems
        sem = self.dma_sems[j]
        waits = self._waits(eng, reads, writes)
        prev = self.dma_cnt[j]
        if prev > 0 and self.known[eng].get(id(sem), 0) < prev:
            waits.append((sem, prev))
            self.known[eng][id(sem)] = prev
        self.dma_cnt[j] += 16
        tok = (sem, self.dma_cnt[j], "dma")

        def fn(e, out=out, in_=in_, kw=kw):
            return e.dma_start(out=out, in_=in_, **kw)

        self.ops[eng].append((waits, fn, (sem, 16)))
        self._record(tok, reads, writes)
        if is_output:
            self.out_tokens.append(tok)
        return tok

    def barrier(self):
        if self.dry:
            return
        for e in ENGS:
            waits = []
            for e2 in ("pe", "act", "dve", "pool"):
                if e2 != e and self.cnt[e2] > 0 and self.known[e].get(id(self.sems[e2]), 0) < self.cnt[e2]:
                    waits.append((self.sems[e2], self.cnt[e2]))
                    self.known[e][id(self.sems[e2])] = self.cnt[e2]
            for j, sem in enumerate(self.dma_sems):
                if self.dma_cnt[j] > 0 and self.known[e].get(id(sem), 0) < self.dma_cnt[j]:
                    waits.append((sem, self.dma_cnt[j]))
                    self.known[e][id(sem)] = self.dma_cnt[j]
            self.ops[e].append((waits, None, None))

    def emit(self):
        nc = self.nc
        need = {}
        for (sem, val, _) in self.out_tokens:
            if id(sem) not in need or need[id(sem)][1] < val:
                need[id(sem)] = (sem, val)
        self.ops["sp"].append((list(need.values()), None, None))
        with nc.Block() as block:
            def run(e, lst):
                for waits, fn, inc in lst:
                    for (sem, val) in waits:
                        e.wait_ge(sem, val)
                    if fn is None:
                        continue
                    inst = fn(e)
                    if inc is not None:
                        inst.then_inc(inc[0], inc[1])

            @block.tensor
            def _(e):
                run(e, self.ops["pe"])

            @block.scalar
            def _(e):
                run(e, self.ops["act"])

            @block.vector
            def _(e):
                run(e, self.ops["dve"])

            @block.gpsimd
            def _(e):
                run(e, self.ops["pool"])

            @block.sync
            def _(e):
                run(e, self.ops["sp"])


SBUF_LEFT = [0]
FULL_CFG = dict(D=2048, FH=5632, BT=256, NPRE=4, NMAIN=4, NS=16)


def build(cfg):
    D, FH, BT, NPRE, NMAIN, NS = cfg["D"], cfg["FH"], cfg["BT"], cfg["NPRE"], cfg["NMAIN"], cfg["NS"]
    HW = D // 2
    SW = D // 2
    KD = D // 128
    H = HW // 128
    GT = SW // 128
    G = SW // 16
    FT = FH // 128
    NT = BT // 128
    NCH = BT // 8
    TB = BT + NS
    NCHT = NCH + NS
    WIN = 4 * HW + SW + 2 * D
    DB = min(512, D)
    NDB = D // DB
    assert TB <= 512 and NS == 16

    nc = bass.Bass("TRN2", target_bir_lowering=False)

    def din(name, shape):
        return nc.dram_tensor(name, list(shape), F32, kind="ExternalInput").ap()

    def dout(name, shape):
        return nc.dram_tensor(name, list(shape), F32, kind="ExternalOutput").ap()

    xpre = din("xpre", [NPRE * BT, D])
    xmain = din("xmain", [NMAIN * BT, D])
    xs = din("xs", [NS, D])
    sh0 = din("sh0", [NS, H, 128, 128])
    s5re0 = din("s5re0", [NS, G, 64])
    s5im0 = din("s5im0", [NS, G, 64])
    lb_logits = din("lb_logits", [2, HW])
    norm1_g = din("norm1_g", [D])
    w_in = din("w_in", [D, WIN])
    hgrn_g = din("hgrn_norm_g", [HW])
    lam_re = din("s5_lam_re", [G, 64])
    lam_im = din("s5_lam_im", [G, 64])
    log_dt = din("s5_log_dt", [G])
    B_re = din("s5_B_re", [G, 64, 16])
    B_im = din("s5_B_im", [G, 64, 16])
    C_re = din("s5_C_re", [G, 16, 64])
    C_im = din("s5_C_im", [G, 16, 64])
    s5_D = din("s5_D", [SW])
    w_glu = din("w_s5_glu", [SW, 2 * SW])
    w_pa = din("w_proj_a", [HW, D])
    w_pb = din("w_proj_b", [SW, D])
    w_out = din("w_out", [D, D])
    norm2_g = din("norm2_g", [D])
    w_up = din("w_ffn_up", [D, 2 * FH])
    w_down = din("w_ffn_down", [FH, D])
    fnorm_g = din("final_norm_g", [D])

    y_main = dout("y_main", [NMAIN * BT, D])
    y_s = dout("y_s", [NS, D])
    sh_p = dout("sh_p", [H, 128, 128])
    s5re_p = dout("s5re_p", [G, 64])
    s5im_p = dout("s5im_p", [G, 64])
    sh_s = dout("sh_s", [NS, H, 128, 128])
    s5re_s = dout("s5re_s", [NS, G, 64])
    s5im_s = dout("s5im_s", [NS, G, 64])

    P = Prog(nc)
    P.setup()
    ctxs = []

    def sb(name, shape, dt=F32):
        cm = nc.sbuf_tensor(name, list(shape), dt)
        t = cm.__enter__()
        ctxs.append(cm)
        return t

    NPS = 8
    psb = []
    for i in range(NPS):
        cm = nc.psum_tensor("ps%d" % i, [128, 512], F32)
        psb.append(cm.__enter__())
        ctxs.append(cm)
    ps_rr = [0]

    def ps():
        i = ps_rr[0]
        if P.dry:
            return psb[i], ("ps", i)
        ps_rr[0] = (i + 1) % (NPS - 1)
        return psb[i], ("ps", i)

    ident = sb("ident", [128, 128])
    ones = sb("ones", [128, 128])
    mut = sb("mut", [128, 128])
    mbc = sb("mbc", [128, 128])
    P.op("pool", lambda e: e.memset(ones[:], 1.0), writes=["ones"])
    P.op("pool", lambda e: e.memset(ident[:], 1.0), writes=["ident"])
    P.op("pool", lambda e: e.affine_select(out=ident[:], in_=ident[:], pattern=[[-1, 128]], compare_op=ALU.is_ge, fill=0.0, base=0, channel_multiplier=1), reads=["ident"], writes=["ident"])
    P.op("pool", lambda e: e.affine_select(out=ident[:], in_=ident[:], pattern=[[1, 128]], compare_op=ALU.is_ge, fill=0.0, base=0, channel_multiplier=-1), reads=["ident"], writes=["ident"])
    P.op("pool", lambda e: e.memset(mut[:], 1.0), writes=["mut"])
    P.op("pool", lambda e: e.affine_select(out=mut[:], in_=mut[:], pattern=[[1, 128]], compare_op=ALU.is_ge, fill=0.0, base=0, channel_multiplier=-1), reads=["mut"], writes=["mut"])
    P.op("pool", lambda e: e.memset(mbc[:], 1.0), writes=["mbc"])
    P.op("pool", lambda e: e.affine_select(out=mbc[:].rearrange("p (t q) -> p t q", q=16), in_=mbc[:].rearrange("p (t q) -> p t q", q=16), pattern=[[16, 8], [0, 16]], compare_op=ALU.is_ge, fill=0.0, base=15, channel_multiplier=-1), reads=["mbc"], writes=["mbc"])

    g1T = sb("g1T", [128, KD])
    g2T = sb("g2T", [128, KD])
    ghT = sb("ghT", [128, H])
    lbT = sb("lbT", [128, H])
    omlT = sb("omlT", [128, H])
    l1T = sb("l1T", [128, H])
    fgB = sb("fgB", [128, D])
    P.dma("sp", g1T[:], norm1_g.rearrange("(k p) -> p k", p=128), writes=["g1T"], allow_slow_non_contiguous=True)
    P.dma("sp", g2T[:], norm2_g.rearrange("(k p) -> p k", p=128), writes=["g2T"], allow_slow_non_contiguous=True)
    P.dma("sp", ghT[:], hgrn_g.rearrange("(k p) -> p k", p=128), writes=["ghT"], allow_slow_non_contiguous=True)
    P.dma("sp", lbT[:], lb_logits[0].rearrange("(k p) -> p k", p=128), writes=["lbT"], allow_slow_non_contiguous=True)
    P.dma("sp", l1T[:], lb_logits[1].rearrange("(k p) -> p k", p=128), writes=["l1T"], allow_slow_non_contiguous=True)
    P.dma("sp", fgB[:], fnorm_g.partition_broadcast(128), writes=["fgB"])
    P.op("dve", lambda e: e.tensor_tensor(out=lbT[:], in0=lbT[:], in1=l1T[:], op=ALU.subtract), reads=["lbT", "l1T"], writes=["lbT"])
    P.op("act", lambda e: e.activation(out=lbT[:], in_=lbT[:], func=AF.Sigmoid), reads=["lbT"], writes=["lbT"])
    P.op("dve", lambda e: e.tensor_scalar(out=omlT[:], in0=lbT[:], scalar1=-1.0, scalar2=1.0, op0=ALU.mult, op1=ALU.add), reads=["lbT"], writes=["omlT"])

    NW = 5
    NSTG = 2
    NBG = 1
    DEPTH = 4
    WSLOT = 16 * 128
    wring = sb("wring", [128, NW, WSLOT], BF16)
    wstage = sb("wstage", [128, NSTG, WSLOT])
    bgbuf = sb("bgbuf", [128, NBG, 16], BF16)
    wspecs = []
    w_idx = [0]
    w_issued = [0]
    CAST_ENGS = ("act", "dve", "act", "dve", "act", "dve")

    wscr_state = {"tid": None, "ap": None, "seen": set(), "nfirst": 0, "bg": [], "bgpos": 0, "nbg": 0}
    blk_marks = []

    def setup_wscr():
        keys = {}
        tid = []
        for (src, kt, C) in wspecs:
            k = (str(src), kt, C)
            if k not in keys:
                keys[k] = len(keys)
            tid.append(keys[k])
        wscr_state["tid"] = tid
        npre_specs = blk_marks[NPRE] if len(blk_marks) > NPRE else len(wspecs)
        pre_t = set(tid[:npre_specs])
        bg = []
        for n in range(npre_specs, len(wspecs)):
            if tid[n] not in pre_t:
                bg.append(n)
                pre_t.add(tid[n])
        wscr_state["bg"] = bg
        wscr_state["ap"] = nc.dram_tensor("wscr", [len(keys), 128, WSLOT], BF16, kind="Internal").ap()

    def issue_load(n):
        src, kt, C = wspecs[n]
        i = n % NW
        t = wscr_state["tid"][n]
        wscr = wscr_state["ap"]
        view = wring[:, i, 0:kt * C].rearrange("p (k c) -> p k c", k=kt)
        if t in wscr_state["seen"]:
            P.dma("sp", wring[:, i, 0:kt * C], wscr[t, :, 0:kt * C], reads=[("wscr", t)], writes=[("w", i)])
            return
        wscr_state["seen"].add(t)
        nf = wscr_state["nfirst"]
        wscr_state["nfirst"] = nf + 1
        j = nf % NSTG
        sview = wstage[:, j, 0:kt * C].rearrange("p (k c) -> p k c", k=kt)
        P.dma("sp", sview, src.rearrange("(k p) c -> p k c", p=128), writes=[("wst", j)])
        eng = "act"
        if eng == "act":
            P.op("act", lambda e: e.activation(out=view, in_=sview, func=AF.Copy), reads=[("wst", j)], writes=[("w", i)])
        else:
            P.op(eng, lambda e: e.tensor_copy(out=view, in_=sview), reads=[("wst", j)], writes=[("w", i)])
        P.dma(eng, wscr[t, :, 0:kt * C], wring[:, i, 0:kt * C], reads=[("w", i)], writes=[("wscr", t)])

    def bg_step(k):
        if P.dry or wscr_state["tid"] is None or k <= 0:
            return
        wscr = wscr_state["ap"]
        done = 0
        while done < k and wscr_state["bgpos"] < len(wscr_state["bg"]):
            n = wscr_state["bg"][wscr_state["bgpos"]]
            wscr_state["bgpos"] += 1
            t = wscr_state["tid"][n]
            if t in wscr_state["seen"]:
                continue
            src, kt, C = wspecs[n]
            wscr_state["seen"].add(t)
            nf = wscr_state["nfirst"]
            wscr_state["nfirst"] = nf + 1
            j = nf % NSTG
            nb_ = wscr_state["nbg"]
            wscr_state["nbg"] = nb_ + 1
            bs = nb_ % NBG
            sview = wstage[:, j, 0:kt * C]
            P.dma("sp", sview.rearrange("p (k c) -> p k c", k=kt), src.rearrange("(k p) c -> p k c", p=128), writes=[("wst", j)])
            if nb_ % 2 == 0:
                P.op("act", lambda e: e.activation(out=bgbuf[:, bs, 0:kt * C], in_=sview, func=AF.Copy), reads=[("wst", j)], writes=[("bg", bs)])
            else:
                P.op("dve", lambda e: e.tensor_copy(out=bgbuf[:, bs, 0:kt * C], in_=sview), reads=[("wst", j)], writes=[("bg", bs)])
            P.dma("act", wscr[t, :, 0:kt * C], bgbuf[:, bs, 0:kt * C], reads=[("bg", bs)], writes=[("wscr", t)])
            done += 1

    def load_w(src, kt, C):
        assert kt * C <= WSLOT
        if P.dry:
            wspecs.append((src, kt, C))
            return wring[:, 0, 0:kt * C].rearrange("p (k c) -> p k c", k=kt), ("w", 0)
        n = w_idx[0]
        w_idx[0] += 1
        assert wspecs[n][1] == kt and wspecs[n][2] == C
        while w_issued[0] < min(n + DEPTH + 1, len(wspecs)):
            issue_load(w_issued[0])
            w_issued[0] += 1
        i = n % NW
        return wring[:, i, 0:kt * C].rearrange("p (k c) -> p k c", k=kt), ("w", i)

    W1all = sb("W1all", [128, G, 128], BF16)
    W2all = sb("W2all", [128, G, 128], BF16)
    Toep = sb("Toep", [128, G, 128], BF16)
    A8re = sb("A8re", [128, G])
    A8im = sb("A8im", [128, G])
    Am7re = sb("Am7re", [128, G])
    Am7im = sb("Am7im", [128, G])
    Dcol = sb("Dcol", [128, G])
    NK = 24
    GB = 8
    xn = sb("xn", [128, max(D, 2048)])
    scrF = sb("scrF", [128, max((NT + 1) * D, 6144, 2 * G * NCHT)])
    smalls = sb("smalls", [128, 12, GB])
    lre, lim, dtb, zre, zim, zt1, zt2, zt3 = [smalls[:, i, :] for i in range(8)]
    kv = sb("kv", [128, NK])
    cnat = sb("cnat", [128, 2, 2, 64])

    def xnv(off, shape):
        n = 1
        for d in shape:
            n *= d
        v = xn[:, off:off + n]
        if len(shape) == 2:
            v = v.rearrange("p (a b) -> p a b", a=shape[0])
        return v
    tA = xnv(0, [GB, NK]); tB = xnv(192, [GB, NK]); tC = xnv(384, [GB, NK]); Are = xnv(576, [GB, NK]); Aim = xnv(768, [GB, NK])
    Bre_sb = xnv(960, [GB, 16]); Bim_sb = xnv(1088, [GB, 16]); Cre_sb = xnv(1216, [GB, 16]); Cim_sb = xnv(1344, [GB, 16])
    T1 = xnv(1472, [GB, 8]); T2 = xnv(1536, [GB, 8]); Et = xnv(1600, [GB, 8])
    U1 = xnv(1664, [GB, 16]); U2 = xnv(1792, [GB, 16])
    tmsk = xn[:, 1920:2048]
    Pb = scrF[:, 0:1024].rearrange("p (g s q) -> p g s q", g=8, s=8)
    Pb2 = scrF[:, 1024:2048].rearrange("p (g s q) -> p g s q", g=8, s=8)
    Qb = scrF[:, 2048:4096].rearrange("p (g s q) -> p g s q", g=8, s=16)
    Qb2 = scrF[:, 4096:6144].rearrange("p (g s q) -> p g s q", g=8, s=16)

    for s in range(8):
        P.dma("sp", Dcol[s * 16:(s + 1) * 16, :], s5_D.rearrange("(g p) -> p g", p=16), writes=["Dcol"], allow_slow_non_contiguous=True)
    powers = list(range(-7, 9)) + list(range(7, -1, -1))
    for j, pw in enumerate(powers):
        P.op("pool", lambda e, j=j, pw=pw: e.memset(kv[:, j:j + 1], float(pw)), writes=["kv"])
    MAGIC = 12582912.0
    tR = wstage[:, 0, 0:GB * NK].rearrange("p (g k) -> p g k", g=GB)
    for bq in range(G // GB):
        gs = slice(bq * GB, (bq + 1) * GB)
        for hf in range(2):
            hs = slice(hf * 64, hf * 64 + 64)
            P.dma("sp", lre[hs, :], lam_re[gs].rearrange("g n -> n g"), writes=["lre"], allow_slow_non_contiguous=True)
            P.dma("sp", lim[hs, :], lam_im[gs].rearrange("g n -> n g"), writes=["lim"], allow_slow_non_contiguous=True)
            P.dma("sp", Bre_sb[hs, :, :], B_re[gs].rearrange("g n p -> n g p"), writes=["Bre"])
            P.dma("sp", Bim_sb[hs, :, :], B_im[gs].rearrange("g n p -> n g p"), writes=["Bim"])
        P.dma("sp", dtb, log_dt[gs].partition_broadcast(128), writes=["dtb"])
        P.op("act", lambda e: e.activation(out=dtb, in_=dtb, func=AF.Exp), reads=["dtb"], writes=["dtb"])
        P.op("dve", lambda e: e.tensor_tensor(out=zt1, in0=lre, in1=dtb, op=ALU.mult), reads=["lre", "dtb"], writes=["zt1"])
        P.op("dve", lambda e: e.tensor_tensor(out=zt2, in0=lim, in1=dtb, op=ALU.mult), reads=["lim", "dtb"], writes=["zt2"])
        kvb = kv[:].unsqueeze(1).to_broadcast([128, GB, NK])
        P.op("dve", lambda e: e.tensor_tensor(out=tA, in0=zt1.unsqueeze(2).to_broadcast([128, GB, NK]), in1=kvb, op=ALU.mult), reads=["zt1", "kv"], writes=["tA"])
        P.op("act", lambda e: e.activation(out=tA, in_=tA, func=AF.Exp), reads=["tA"], writes=["tA"])
        P.op("dve", lambda e: e.tensor_tensor(out=tB, in0=zt2.unsqueeze(2).to_broadcast([128, GB, NK]), in1=kvb, op=ALU.mult), reads=["zt2", "kv"], writes=["tB"])
        P.op("dve", lambda e: e.tensor_scalar(out=tC, in0=tB, scalar1=1.0 / (2 * math.pi), scalar2=0.25, op0=ALU.mult, op1=ALU.add), reads=["tB"], writes=["tC"])
        P.op("dve", lambda e: e.tensor_scalar(out=tB, in0=tB, scalar1=1.0 / (2 * math.pi), scalar2=None, op0=ALU.mult), reads=["tB"], writes=["tB"])
        for T_, tk in ((tB, "tB"), (tC, "tC")):
            P.op("dve", lambda e, T_=T_: e.tensor_scalar(out=tR[:], in0=T_, scalar1=MAGIC, scalar2=None, op0=ALU.add), reads=[tk], writes=["tR"])
            P.op("dve", lambda e: e.tensor_scalar(out=tR[:], in0=tR[:], scalar1=-MAGIC, scalar2=None, op0=ALU.add), reads=["tR"], writes=["tR"])
            P.op("dve", lambda e, T_=T_: e.tensor_tensor(out=T_, in0=T_, in1=tR[:], op=ALU.subtract), reads=[tk, "tR"], writes=[tk])
            P.op("act", lambda e, T_=T_: e.activation(out=T_, in_=T_, func=AF.Sin, scale=2 * math.pi), reads=[tk], writes=[tk])
        P.op("dve", lambda e: e.tensor_tensor(out=Are, in0=tA, in1=tC, op=ALU.mult), reads=["tA", "tC"], writes=["Are"])
        P.op("dve", lambda e: e.tensor_tensor(out=Aim, in0=tA, in1=tB, op=ALU.mult), reads=["tA", "tB"], writes=["Aim"])
        P.op("dve", lambda e, gs=gs: e.tensor_copy(out=A8re[:, gs], in_=Are[:, :, 15]), reads=["Are"], writes=["A8re"])
        P.op("dve", lambda e, gs=gs: e.tensor_copy(out=A8im[:, gs], in_=Aim[:, :, 15]), reads=["Aim"], writes=["A8im"])
        P.op("dve", lambda e, gs=gs: e.tensor_copy(out=Am7re[:, gs], in_=Are[:, :, 0]), reads=["Are"], writes=["Am7re"])
        P.op("dve", lambda e, gs=gs: e.tensor_copy(out=Am7im[:, gs], in_=Aim[:, :, 0]), reads=["Aim"], writes=["Am7im"])
        P.op("dve", lambda e: e.tensor_scalar(out=zt1, in0=Are[:, :, 8], scalar1=-1.0, scalar2=None, op0=ALU.add), reads=["Are"], writes=["zt1"])
        P.op("dve", lambda e: e.tensor_tensor(out=zt2, in0=lre, in1=lre, op=ALU.mult), reads=["lre"], writes=["zt2"])
        P.op("dve", lambda e: e.tensor_tensor(out=zt3, in0=lim, in1=lim, op=ALU.mult), reads=["lim"], writes=["zt3"])
        P.op("dve", lambda e: e.tensor_tensor(out=zt2, in0=zt2, in1=zt3, op=ALU.add), reads=["zt2", "zt3"], writes=["zt2"])
        P.op("dve", lambda e: e.reciprocal(out=zt2, in_=zt2), reads=["zt2"], writes=["zt2"])
        P.op("dve", lambda e: e.tensor_tensor(out=zre, in0=zt1, in1=lre, op=ALU.mult), reads=["zt1", "lre"], writes=["zre"])
        P.op("dve", lambda e: e.tensor_tensor(out=zt3, in0=Aim[:, :, 8], in1=lim, op=ALU.mult), reads=["Aim", "lim"], writes=["zt3"])
        P.op("dve", lambda e: e.tensor_tensor(out=zre, in0=zre, in1=zt3, op=ALU.add), reads=["zre", "zt3"], writes=["zre"])
        P.op("dve", lambda e: e.tensor_tensor(out=zre, in0=zre, in1=zt2, op=ALU.mult), reads=["zre", "zt2"], writes=["zre"])
        P.op("dve", lambda e: e.tensor_tensor(out=zim, in0=Aim[:, :, 8], in1=lre, op=ALU.mult), reads=["Aim", "lre"], writes=["zim"])
        P.op("dve", lambda e: e.tensor_tensor(out=zt3, in0=zt1, in1=lim, op=ALU.mult), reads=["zt1", "lim"], writes=["zt3"])
        P.op("dve", lambda e: e.tensor_tensor(out=zim, in0=zim, in1=zt3, op=ALU.subtract), reads=["zim", "zt3"], writes=["zim"])
        P.op("dve", lambda e: e.tensor_tensor(out=zim, in0=zim, in1=zt2, op=ALU.mult), reads=["zim", "zt2"], writes=["zim"])
        zreb = zre.unsqueeze(2).to_broadcast([128, GB, 8])
        zimb = zim.unsqueeze(2).to_broadcast([128, GB, 8])
        P.op("dve", lambda e: e.tensor_tensor(out=T1, in0=Are[:, :, 16:24], in1=zreb, op=ALU.mult), reads=["Are", "zre"], writes=["T1"])
        P.op("dve", lambda e: e.tensor_tensor(out=Et, in0=Aim[:, :, 16:24], in1=zimb, op=ALU.mult), reads=["Aim", "zim"], writes=["Et"])
        P.op("dve", lambda e: e.tensor_tensor(out=T1, in0=T1, in1=Et, op=ALU.subtract), reads=["T1", "Et"], writes=["T1"])
        P.op("dve", lambda e: e.tensor_tensor(out=T2, in0=Are[:, :, 16:24], in1=zimb, op=ALU.mult), reads=["Are", "zim"], writes=["T2"])
        P.op("dve", lambda e: e.tensor_tensor(out=Et, in0=Aim[:, :, 16:24], in1=zreb, op=ALU.mult), reads=["Aim", "zre"], writes=["Et"])
        P.op("dve", lambda e: e.tensor_tensor(out=T2, in0=T2, in1=Et, op=ALU.add), reads=["T2", "Et"], writes=["T2"])
        P.op("dve", lambda e: e.tensor_copy(out=Et[64:128], in_=T1[64:128]), reads=["T1"], writes=["Et"])
        P.op("dve", lambda e: e.tensor_copy(out=T1[64:128], in_=T2[64:128]), reads=["T2"], writes=["T1"])
        P.op("dve", lambda e: e.tensor_copy(out=T2[64:128], in_=Et[64:128]), reads=["Et"], writes=["T2"])
        P.op("dve", lambda e: e.tensor_scalar(out=T2[0:64], in0=T2[0:64], scalar1=-1.0, scalar2=None, op0=ALU.mult), reads=["T2"], writes=["T2"])
        P.op("dve", lambda e: e.tensor_copy(out=U1[0:64], in_=Are[0:64, :, 0:16]), reads=["Are"], writes=["U1"])
        P.op("dve", lambda e: e.tensor_scalar(out=U1[64:128], in0=Aim[64:128, :, 0:16], scalar1=-1.0, scalar2=None, op0=ALU.mult), reads=["Aim"], writes=["U1"])
        P.op("dve", lambda e: e.tensor_scalar(out=U2[0:64], in0=Aim[0:64, :, 0:16], scalar1=-1.0, scalar2=None, op0=ALU.mult), reads=["Aim"], writes=["U2"])
        P.op("dve", lambda e: e.tensor_scalar(out=U2[64:128], in0=Are[64:128, :, 0:16], scalar1=-1.0, scalar2=None, op0=ALU.mult), reads=["Are"], writes=["U2"])
        for ci, (Csrc, Cdst, ckey) in enumerate(((C_re, Cre_sb, "Cre"), (C_im, Cim_sb, "Cim"))):
            rows = Csrc[gs].rearrange("g p n -> (g p) n")
            for dup in range(2):
                P.dma("sp", cnat[:, ci, dup, :], rows, writes=[("cnat", ci)])
            pt, pk = ps()
            P.op("pe", lambda e, ci=ci, pt=pt: e.transpose(out=pt[:, 0:128], in_=cnat[:, ci, :, :], identity=ident[:]), reads=[("cnat", ci), "ident"], writes=[pk])
            P.op("dve", lambda e, pt=pt, Cdst=Cdst: e.tensor_copy(out=Cdst, in_=pt[:, 0:128].rearrange("p (g q) -> p g q", q=16)), reads=[pk], writes=[ckey])
        P.op("dve", lambda e: e.tensor_tensor(out=Pb, in0=T1.unsqueeze(3).to_broadcast([128, 8, 8, 16]), in1=Bre_sb.unsqueeze(2).to_broadcast([128, 8, 8, 16]), op=ALU.mult), reads=["T1", "Bre"], writes=["Pb"])
        P.op("dve", lambda e: e.tensor_tensor(out=Pb2, in0=T2.unsqueeze(3).to_broadcast([128, 8, 8, 16]), in1=Bim_sb.unsqueeze(2).to_broadcast([128, 8, 8, 16]), op=ALU.mult), reads=["T2", "Bim"], writes=["Pb2"])
        P.op("dve", lambda e: e.tensor_tensor(out=Pb, in0=Pb, in1=Pb2, op=ALU.add), reads=["Pb", "Pb2"], writes=["Pb"])
        P.op("dve", lambda e: e.tensor_tensor(out=Qb, in0=U1.unsqueeze(3).to_broadcast([128, 8, 16, 16]), in1=Cre_sb.unsqueeze(2).to_broadcast([128, 8, 16, 16]), op=ALU.mult), reads=["U1", "Cre"], writes=["Qb"])
        P.op("dve", lambda e: e.tensor_tensor(out=Qb2, in0=U2.unsqueeze(3).to_broadcast([128, 8, 16, 16]), in1=Cim_sb.unsqueeze(2).to_broadcast([128, 8, 16, 16]), op=ALU.mult), reads=["U2", "Cim"], writes=["Qb2"])
        P.op("dve", lambda e: e.tensor_tensor(out=Qb, in0=Qb, in1=Qb2, op=ALU.add), reads=["Qb", "Qb2"], writes=["Qb"])
        for gl in range(8):
            g = bq * 8 + gl
            P.op("act", lambda e, gl=gl, g=g: e.activation(out=W2all[:, g, :].rearrange("p (t q) -> p t q", q=16), in_=Qb[:, gl, 8:16, :], func=AF.Copy), reads=["Qb"], writes=["W2all"])
            pt, pk = ps()
            P.op("pe", lambda e, gl=gl, pt=pt: e.transpose(out=pt[:, 0:128], in_=Pb[:, gl, :, :], identity=ident[:]), reads=["Pb", "ident"], writes=[pk])
            P.op("act", lambda e, g=g, pt=pt: e.activation(out=W1all[:, g, :], in_=pt[:, 0:128], func=AF.Copy), reads=[pk], writes=["W1all"])
            pt2, pk2 = ps()
            P.op("pe", lambda e, gl=gl, pt2=pt2: e.matmul(pt2[:, 0:128], lhsT=Pb[:, gl, :, :], rhs=Qb[:, gl, 0:8, :], start=True, stop=True), reads=["Pb", "Qb"], writes=[pk2])
            P.op("dve", lambda e, pt2=pt2: e.tensor_tensor(out=tmsk, in0=pt2[:, 0:128], in1=mbc[:], op=ALU.mult), reads=[pk2, "mbc"], writes=["tmsk"])
            P.op("dve", lambda e, g=g: e.scalar_tensor_tensor(out=Toep[:, g, :], in0=ident[:], scalar=Dcol[:, g:g + 1], in1=tmsk, op0=ALU.mult, op1=ALU.add), reads=["ident", "Dcol", "tmsk"], writes=["Toep"])
    P.barrier()

    Pswap = sb("Pswap", [128, 128])
    Bcat = sb("Bcat", [128, G])
    Bswp = sb("Bswp", [128, G])
    P.op("pool", lambda e: e.memset(Pswap[:], 0.0), writes=["Pswap"])
    P.op("pool", lambda e: e.affine_select(out=Pswap[:], in_=Pswap[:], pattern=[[1, 128]], compare_op=ALU.not_equal, fill=1.0, base=-64, channel_multiplier=-1), reads=["Pswap"], writes=["Pswap"])
    P.op("pool", lambda e: e.affine_select(out=Pswap[:], in_=Pswap[:], pattern=[[1, 128]], compare_op=ALU.not_equal, fill=1.0, base=64, channel_multiplier=-1), reads=["Pswap"], writes=["Pswap"])
    P.op("dve", lambda e: e.tensor_copy(out=Bcat[:], in_=A8im[:]), reads=["A8im"], writes=["Bcat"])
    P.op("dve", lambda e: e.tensor_scalar(out=Bcat[0:64], in0=Bcat[0:64], scalar1=-1.0, scalar2=None, op0=ALU.mult), reads=["Bcat"], writes=["Bcat"])
    P.op("dve", lambda e: e.tensor_scalar(out=Bswp[:], in0=Bcat[:], scalar1=-1.0, scalar2=None, op0=ALU.mult), reads=["Bcat"], writes=["Bswp"])
    S = sb("S", [128, H, 128])
    Sbf = sb("Sbf", [128, H, 128], BF16)
    Wd = sb("Wd", [128, 3, G])
    A2 = sb("A2", [128, 2, G])
    B2 = sb("B2", [128, 2, G])
    P.op("dve", lambda e: e.tensor_copy(out=A2[:, 0, :], in_=A8re[:]), reads=["A8re"], writes=["A2"])
    P.op("dve", lambda e: e.tensor_copy(out=A2[:, 1, :], in_=A8re[:]), reads=["A8re"], writes=["A2"])
    P.op("dve", lambda e: e.tensor_copy(out=B2[:, 0, :], in_=Bcat[:]), reads=["Bcat"], writes=["B2"])
    P.op("dve", lambda e: e.tensor_copy(out=B2[:, 1, :], in_=Bswp[:]), reads=["Bswp"], writes=["B2"])
    P.op("pool", lambda e: e.memset(S[:], 0.0), writes=[("S", h) for h in range(H)])
    P.op("pool", lambda e: e.memset(Sbf[:], 0.0), writes=[("Sbf", h) for h in range(H)])
    P.op("pool", lambda e: e.memset(Wd[:], 0.0), writes=["Wd", "Wd2"])

    hT = sb("hT", [128, KD, TB], BF16)
    ss = sb("ss", [128, 4])
    qT = sb("qT", [128, TB])
    fT = sb("fT", [128, TB])
    ogs = sb("ogs", [128, TB])
    itok = sb("itok", [128, NT, 128], BF16)
    is32 = sb("is32", [16, 128])
    lg2 = sb("lg", [128, 2, 128])
    bcs2 = sb("bcs", [128, 2, 128])
    ex4 = sb("ex", [128, 4, 128])
    kk2 = sb("kk", [128, 2, 128])
    ku2 = sb("ku", [128, 2, 128])
    nb2 = sb("nb", [128, 2, 2])
    qs2 = sb("qs", [128, 2, 128], BF16)
    ks2 = sb("ks", [128, 2, 128], BF16)
    qin2 = sb("qin", [128, 2, 128], BF16)
    kutok2 = sb("kutok", [128, 2, 128], BF16)
    scT2 = sb("scT", [128, 2, 128], BF16)
    ex_rr = [0]
    sload_next = [0]
    kks = sb("kks", [128, 16])
    ktok = sb("ktok", [16, 128])
    km = xn[0:16, 0:2048].rearrange("p (b k) -> p b k", b=16)
    NSB = 4
    s0b = sb("s0b", [128, NSB, 128])
    snb = sb("snb", [128, NSB, 128])
    oaT = sb("oaT", [128, H, TB], BF16)
    ucm = sb("ucm", [128, 8, 8, 16])
    ucms = xn[0:16, 0:1024].rearrange("p (g s q) -> p g s q", g=8, s=8)
    Uall = sb("Uall", [128, G, NCHT], BF16)
    XR = scrF[:, 0:G * NCHT].rearrange("p (g c) -> p g c", g=G)
    XI = scrF[:, G * NCHT:2 * G * NCHT].rearrange("p (g c) -> p g c", g=G)
    Hin = sb("Hin", [128, G, NCHT], BF16)
    st1 = sb("st1", [128, 2, G])
    st2 = sb("st2", [128, 2, G])
    X2v = scrF[:, 0:2 * G * NCHT].rearrange("p (r g c) -> p r g c", r=2, g=G)
    Zg = sb("Zg", [128, 2, NCHT])
    zcm = ucm[:].rearrange("p a b c -> p a (b c)")
    zcms = sb("zcms", [16, 128])
    zT = sb("zT", [128, GT, TB], BF16)
    obT = sb("obT", [128, GT, TB], BF16)
    h0nat = cnat
    HS_ALIAS = TB >= 256
    hsT = sb("hsT", [128, 2, 64])
    sgt = sb("sgt", [128, 2, TB])
    mt1 = sb("mt1", [128, TB])
    mt2 = sb("mt2", [128, TB])
    NTT = NT + 1
    x2 = scrF[:, 0:NTT * D].rearrange("p (i d) -> p i d", i=NTT)
    X2K = [("x2", i) for i in range(NTT)]
    osq = sgt[:, 0, :]
    if HS_ALIAS:
        hs0R = sgt[:, 0, 0:128].rearrange("p (g b) -> p g b", g=8)
        hs0I = sgt[:, 0, 128:256].rearrange("p (g b) -> p g b", g=8)
        hsR = sgt[:, 1, 0:128].rearrange("p (g b) -> p g b", g=8)
        hsI = sgt[:, 1, 128:256].rearrange("p (g b) -> p g b", g=8)
        KH = {"hs0R": ("sgt", 0), "hs0I": ("sgt", 0), "hsR": ("sgt", 1), "hsI": ("sgt", 1)}
    else:
        hs0R = sb("hs0R", [128, 8, NS])
        hs0I = sb("hs0I", [128, 8, NS])
        hsR = sb("hsR", [128, 8, NS])
        hsI = sb("hsI", [128, 8, NS])
        KH = {"hs0R": "hs0R", "hs0I": "hs0I", "hsR": "hsR", "hsI": "hsI"}
    rn = mt1
    otmp = mt2
    NHH = 4 if FT >= 8 else 2
    FTH = (FT + NHH - 1) // NHH
    mh = sb("mh", [128, max(KD, FTH), TB], BF16)
    mergedT = mh[:, 0:KD, :]
    hid = mh[:, 0:FTH, :]
    junk = mh[:].rearrange("p a b -> p (a b)")
    yt = xn

    def tiles(has_s):
        lst = [(i, 128, slice(i * 128, (i + 1) * 128)) for i in range(NT)]
        if has_s:
            lst.append((NT, NS, slice(BT, BT + NS)))
        return lst

    def rms_to_hT(src_fn, tl, gT, tag):
        for (i, rows, cols) in tl:
            src, skey = src_fn(i, rows)
            P.op("act", lambda e, src=src, rows=rows: e.activation(out=junk[0:rows, 0:D], in_=src, func=AF.Square, accum_out=ss[0:rows, 0:1]), reads=[skey], writes=["mh", "ss"])
            P.op("dve", lambda e, rows=rows: e.tensor_scalar(out=ss[0:rows, 1:2], in0=ss[0:rows, 0:1], scalar1=1.0 / D, scalar2=EPS, op0=ALU.mult, op1=ALU.add), reads=["ss"], writes=["ss"])
            P.op("act", lambda e, rows=rows: e.activation(out=ss[0:rows, 1:2], in_=ss[0:rows, 1:2], func=AF.Ln), reads=["ss"], writes=["ss"])
            P.op("act", lambda e, rows=rows: e.activation(out=ss[0:rows, 1:2], in_=ss[0:rows, 1:2], func=AF.Exp, scale=-0.5), reads=["ss"], writes=["ss"])
            P.op("dve", lambda e, src=src, rows=rows: e.tensor_scalar(out=xn[0:rows, 0:D], in0=src, scalar1=ss[0:rows, 1:2], scalar2=None, op0=ALU.mult), reads=["ss", skey], writes=["xn"])
            for k0 in range(0, KD, 4):
                nk = min(4, KD - k0)
                pt, pk = ps()
                for kq in range(nk):
                    k = k0 + kq
                    P.op("pe", lambda e, k=k, kq=kq, pt=pt, rows=rows: e.transpose(out=pt[:, kq * 128:kq * 128 + rows], in_=xn[0:rows, k * 128:(k + 1) * 128], identity=ident[0:rows, 0:rows]), reads=["xn", "ident"], writes=[pk], track=(kq == nk - 1))
                P.op("dve", lambda e, k0=k0, nk=nk, pt=pt, rows=rows, cols=cols: e.tensor_tensor(
                    out=hT[:, k0:k0 + nk, cols],
                    in0=pt[:, 0:nk * 128].rearrange("p (k c) -> p k c", k=nk)[:, :, 0:rows],
                    in1=gT[:, k0:k0 + nk].unsqueeze(2).to_broadcast([128, nk, rows]), op=ALU.mult), reads=[pk, tag], writes=["hT"])

    def proj_fm(wv, wk, kt, rhs_fn, ncols, rkeys):
        pt, pk = ps()
        for k in range(kt):
            P.op("pe", lambda e, k=k, pt=pt: e.matmul(pt[:, 0:ncols], lhsT=wv[:, k, :], rhs=rhs_fn(k), start=(k == 0), stop=(k == kt - 1)), reads=[wk] + rkeys, writes=[pk], track=(k == kt - 1))
        return pt, pk

    def do_block(xsrc, mode, has_s, ydst, last):
        main = (mode == "M")
        if P.dry:
            blk_marks.append(len(wspecs))
        bgq = (BGQ_PRE if not main else (BGQ_MAIN0 if has_s else 0))
        ncol = TB if has_s else BT
        tl = tiles(has_s)
        nchn = NCHT if has_s else NCH

        def src1(i, rows):
            if rows == 128:
                P.dma("sp", xn[:, 0:D], xsrc[i * 128:(i + 1) * 128, :], writes=["xn"])
            else:
                P.dma("sp", xn[0:rows, 0:D], xs, writes=["xn"])
            return xn[0:rows, 0:D], "xn"
        rms_to_hT(src1, tl, g1T, "g1T")
        hrhs = lambda k: hT[:, k, 0:ncol]

        if has_s:
            P.op("pool", lambda e: e.memset(ucms, 0.0), writes=["xn"])
        for ct in range(GT):
            bg_step(bgq)
            wu, wuk = load_w(w_in[:, 4 * HW + ct * 128:4 * HW + (ct + 1) * 128], KD, 128)
            pu, puk = proj_fm(wu, wuk, KD, hrhs, ncol, ["hT"])
            P.op("act", lambda e, pu=pu: e.activation(out=qT[:, 0:ncol], in_=pu[:, 0:ncol], func=AF.Copy), reads=[puk], writes=["qT"])
            for s0 in range(0, 8, 4):
                pt, pk = ps()
                for sq in range(4):
                    s = s0 + sq
                    P.op("pe", lambda e, s=s, sq=sq, pt=pt: e.transpose(out=pt[0:NCH, sq * 128:(sq + 1) * 128], in_=qT[:, 0:BT].rearrange("p (c s) -> p s c", s=8)[:, s, :], identity=ident[:]), reads=["qT", "ident"], writes=[pk], track=(sq == 3))
                P.op("act", lambda e, s0=s0, pt=pt: e.activation(out=ucm[0:NCH, :, s0:s0 + 4, :], in_=pt[0:NCH, :].rearrange("p (s g q) -> p g s q", s=4, g=8), func=AF.Copy), reads=[pk], writes=["ucm"])
            if has_s:
                pt, pk = ps()
                P.op("pe", lambda e, pt=pt: e.transpose(out=pt[0:NS, 0:128], in_=qT[:, BT:BT + NS], identity=ident[:]), reads=["qT", "ident"], writes=[pk])
                P.op("act", lambda e, pt=pt: e.activation(out=ucms[:, :, 7, :], in_=pt[0:NS, 0:128].rearrange("p (g q) -> p g q", g=8), func=AF.Copy), reads=[pk], writes=["xn"])
            assert 8 * NCHT <= 512
            pt, pk = ps()
            for gl in range(8):
                P.op("pe", lambda e, gl=gl, pt=pt: e.transpose(out=pt[:, gl * NCHT:gl * NCHT + NCH], in_=ucm[0:NCH, gl, :, :], identity=ident[0:NCH, 0:NCH]), reads=["ucm", "ident"], writes=[pk], track=(gl == 7 and not has_s))
            if has_s:
                for gl in range(8):
                    P.op("pe", lambda e, gl=gl, pt=pt: e.transpose(out=pt[:, gl * NCHT + NCH:(gl + 1) * NCHT], in_=ucms[:, gl, :, :], identity=ident[0:NS, 0:NS]), reads=["xn", "ident"], writes=[pk], track=(gl == 7))
            P.op("dve", lambda e, ct=ct, pt=pt: e.tensor_copy(out=Uall[:, ct * 8:(ct + 1) * 8, 0:nchn], in_=pt[:, 0:8 * NCHT].rearrange("p (g c) -> p g c", g=8)[:, :, 0:nchn]), reads=[pk], writes=[("Uall", ct)])
            pr, prk = ps()
            for gl in range(8):
                g = ct * 8 + gl
                P.op("pe", lambda e, g=g, gl=gl, pr=pr: e.matmul(pr[:, gl * NCHT:gl * NCHT + nchn], lhsT=W1all[:, g, :], rhs=Uall[:, g, 0:nchn], start=True, stop=True), reads=["W1all", ("Uall", ct)], writes=[prk], track=(gl == 7))
            P.op("act", lambda e, ct=ct, pr=pr: e.activation(out=XR[:, ct * 8:(ct + 1) * 8, 0:nchn], in_=pr[:, 0:8 * NCHT].rearrange("p (g c) -> p g c", g=8)[:, :, 0:nchn], func=AF.Copy), reads=[prk], writes=["XR"] + X2K)
        XRf = scrF[:, 0:G * NCHT]
        XIf = scrF[:, G * NCHT:2 * G * NCHT]
        for c0 in range(0, G * NCHT, 512):
            cw = min(512, G * NCHT - c0)
            pw_, pwk = ps()
            P.op("pe", lambda e, c0=c0, cw=cw, pw_=pw_: e.matmul(pw_[:, 0:cw], lhsT=Pswap[:], rhs=XRf[:, c0:c0 + cw], start=True, stop=True), reads=["Pswap", "XR"], writes=[pwk])
            P.op("act", lambda e, c0=c0, cw=cw, pw_=pw_: e.activation(out=XIf[:, c0:c0 + cw], in_=pw_[:, 0:cw], func=AF.Copy), reads=[pwk], writes=["XI"] + X2K)
        if main:
            P.op("pool", lambda e: e.tensor_copy(out=Hin[:, :, 0], in_=Wd[:, 2, :]), reads=["Wd2"], writes=["Hin"])
        for c in range(NCH):
            P.op("pool", lambda e: e.tensor_tensor(out=st2[:], in0=B2[:], in1=Wd[:, 1:3, :], op=ALU.mult), reads=["B2", "Wd", "Wd2"], writes=["st2"])
            P.op("pool", lambda e, c=c: e.tensor_tensor(out=st2[:], in0=st2[:], in1=X2v[:, :, :, c], op=ALU.add), reads=["st2", "XR", "XI"], writes=["st2"])
            P.op("pool", lambda e: e.tensor_tensor(out=st1[:], in0=A2[:], in1=Wd[:, 0:2, :], op=ALU.mult), reads=["A2", "Wd"], writes=["st1"])
            P.op("pool", lambda e: e.tensor_tensor(out=Wd[:, 0:2, :], in0=st1[:], in1=st2[:], op=ALU.add), reads=["st1", "st2"], writes=["Wd"])
            P.op("pool", lambda e: e.tensor_copy(out=Wd[:, 2, :], in_=Wd[:, 0, :]), reads=["Wd"], writes=["Wd2"])
            if main and c + 1 < NCH:
                P.op("pool", lambda e, c=c: e.tensor_copy(out=Hin[:, :, c + 1], in_=Wd[:, 2, :]), reads=["Wd2"], writes=["Hin"])
        if main and has_s:
            assert G % 64 == 0 or G * 2 <= 128
            NPAIR = NS // 2
            GG = G
            PW = 2 * GG
            v3 = lambda t: t[:].rearrange("p g b -> p (g b)")[:, 0:PW].rearrange("p (b g) -> p b g", b=2)
            hs0Rv, hs0Iv, hsRv, hsIv = v3(hs0R), v3(hs0I), v3(hsR), v3(hsI)
            bcA = lambda t: t[:, :].unsqueeze(1).to_broadcast([128, 2, GG])
            for bh in range(NPAIR):
                scol = slice(NCH + 2 * bh, NCH + 2 * bh + 2)
                for ci, (src, dst, dkey) in enumerate(((s5re0, hs0Rv, KH["hs0R"]), (s5im0, hs0Iv, KH["hs0I"]))):
                    P.dma("sp", h0nat[0:PW, ci, 0, :], src[2 * bh:2 * bh + 2, :, :].rearrange("b g n -> (b g) n"), writes=[("h0nat", ci)])
                    P.op("act", lambda e, ci=ci: e.activation(out=h0nat[0:PW, ci, 1, :], in_=h0nat[0:PW, ci, 0, :], func=AF.Copy), reads=[("h0nat", ci)], writes=[("h0nat", ci)])
                    pt, pk = ps()
                    P.op("pe", lambda e, ci=ci, pt=pt: e.transpose(out=pt[:, 0:PW], in_=h0nat[0:PW, ci, :, :], identity=ident[0:PW, 0:PW]), reads=[("h0nat", ci), "ident"], writes=[pk])
                    P.op("dve", lambda e, pt=pt, dst=dst: e.tensor_copy(out=dst, in_=pt[:, 0:PW].rearrange("p (b g) -> p b g", b=2)), reads=[pk], writes=[dkey])
                P.op("dve", lambda e: e.tensor_tensor(out=hsRv, in0=hs0Rv, in1=bcA(Am7re), op=ALU.mult), reads=[KH["hs0R"], "Am7re"], writes=[KH["hsR"]])
                P.op("dve", lambda e: e.tensor_tensor(out=hsIv, in0=hs0Iv, in1=bcA(Am7im), op=ALU.mult), reads=[KH["hs0I"], "Am7im"], writes=[KH["hsI"]])
                P.op("dve", lambda e: e.tensor_tensor(out=hsRv, in0=hsRv, in1=hsIv, op=ALU.subtract), reads=[KH["hsR"], KH["hsI"]], writes=[KH["hsR"]])
                P.op("dve", lambda e: e.tensor_tensor(out=hsIv, in0=hs0Rv, in1=bcA(Am7im), op=ALU.mult), reads=[KH["hs0R"], "Am7im"], writes=[KH["hsI"]])
                P.op("dve", lambda e: e.tensor_tensor(out=hs0Rv, in0=hs0Iv, in1=bcA(Am7re), op=ALU.mult), reads=[KH["hs0I"], "Am7re"], writes=[KH["hs0R"]])
                P.op("dve", lambda e: e.tensor_tensor(out=hsIv, in0=hsIv, in1=hs0Rv, op=ALU.add), reads=[KH["hsI"], KH["hs0R"]], writes=[KH["hsI"]])
                P.op("act", lambda e, scol=scol: e.activation(out=Hin[0:64, :, scol], in_=hsRv[0:64].rearrange("p b g -> p g b"), func=AF.Copy), reads=[KH["hsR"]], writes=["HinS"])
                P.op("act", lambda e, scol=scol: e.activation(out=Hin[64:128, :, scol], in_=hsIv[64:128].rearrange("p b g -> p g b"), func=AF.Copy), reads=[KH["hsI"]], writes=["HinS"])
                P.op("dve", lambda e: e.tensor_tensor(out=hs0Rv, in0=hsRv, in1=bcA(A8re), op=ALU.mult), reads=[KH["hsR"], "A8re"], writes=[KH["hs0R"]])
                P.op("dve", lambda e: e.tensor_tensor(out=hs0Iv, in0=hsIv, in1=bcA(A8im), op=ALU.mult), reads=[KH["hsI"], "A8im"], writes=[KH["hs0I"]])
                P.op("dve", lambda e: e.tensor_tensor(out=hs0Rv, in0=hs0Rv, in1=hs0Iv, op=ALU.subtract), reads=[KH["hs0R"], KH["hs0I"]], writes=[KH["hs0R"]])
                P.op("dve", lambda e: e.tensor_tensor(out=hs0Iv, in0=hsRv, in1=bcA(A8im), op=ALU.mult), reads=[KH["hsR"], "A8im"], writes=[KH["hs0I"]])
                P.op("dve", lambda e, scol=scol: e.tensor_tensor(out=hsRv, in0=hs0Rv, in1=XR[:, :, scol].rearrange("p g b -> p b g"), op=ALU.add), reads=[KH["hs0R"], "XR"], writes=[KH["hsR"]])
                P.op("dve", lambda e: e.tensor_tensor(out=hs0Rv, in0=hsIv, in1=bcA(A8re), op=ALU.mult), reads=[KH["hsI"], "A8re"], writes=[KH["hs0R"]])
                P.op("dve", lambda e: e.tensor_tensor(out=hs0Iv, in0=hs0Iv, in1=hs0Rv, op=ALU.add), reads=[KH["hs0I"], KH["hs0R"]], writes=[KH["hs0I"]])
                P.op("dve", lambda e, scol=scol: e.tensor_tensor(out=hsIv, in0=hs0Iv, in1=XI[:, :, scol].rearrange("p g b -> p b g"), op=ALU.add), reads=[KH["hs0I"], "XI"], writes=[KH["hsI"]])
                for ci, (srct, dsto, skey) in enumerate(((hsRv, s5re_s, KH["hsR"]), (hsIv, s5im_s, KH["hsI"]))):
                    pt, pk = ps()
                    P.op("pe", lambda e, srct=srct, pt=pt: e.transpose(out=pt[0:PW, 0:64], in_=srct[0:64, :, :], identity=ident[0:64, 0:64]), reads=[skey, "ident"], writes=[pk])
                    P.op("dve", lambda e, ci=ci, pt=pt: e.tensor_copy(out=hsT[0:PW, ci, :], in_=pt[0:PW, 0:64]), reads=[pk], writes=[("hsT", ci)])
                    P.dma("sp", dsto[2 * bh:2 * bh + 2, :, :].rearrange("b g n -> (b g) n"), hsT[0:PW, ci, :], reads=[("hsT", ci)], is_output=True)
        for h in range(H):
            bg_step(bgq)
            c0 = h * 128
            if main:
                wq, wqk = load_w(w_in[:, c0:c0 + 128], KD, 128)
                pq, pqk = proj_fm(wq, wqk, KD, hrhs, ncol, ["hT"])
                P.op("act", lambda e, pq=pq: e.activation(out=qT[:, 0:ncol], in_=pq[:, 0:ncol], func=AF.Copy), reads=[pqk], writes=["qT"])
            wf, wfk = load_w(w_in[:, HW + c0:HW + c0 + 128], KD, 128)
            pf, pfk = proj_fm(wf, wfk, KD, hrhs, ncol, ["hT"])
            P.op("act", lambda e, pf=pf: e.activation(out=fT[:, 0:ncol], in_=pf[:, 0:ncol], func=AF.Sigmoid), reads=[pfk], writes=["fT"])
            P.op("dve", lambda e, h=h: e.tensor_scalar(out=fT[:, 0:ncol], in0=fT[:, 0:ncol], scalar1=omlT[:, h:h + 1], scalar2=lbT[:, h:h + 1], op0=ALU.mult, op1=ALU.add), reads=["fT", "omlT", "lbT"], writes=["fT"])
            wi, wik = load_w(w_in[:, 2 * HW + c0:2 * HW + c0 + 128], KD, 128)
            for (i, rows, cols) in tl:
                pt, pk = ps()
                for k in range(KD):
                    P.op("pe", lambda e, k=k, pt=pt, rows=rows, cols=cols: e.matmul(pt[0:rows, 0:128], lhsT=hT[:, k, cols], rhs=wi[:, k, :], start=(k == 0), stop=(k == KD - 1)), reads=[wik, "hT"], writes=[pk], track=(k == KD - 1))
                if rows == 128:
                    P.op("act", lambda e, pt=pt, i=i: e.activation(out=itok[:, i, :], in_=pt[:, 0:128], func=AF.Copy), reads=[pk], writes=["itok"])
                else:
                    P.op("act", lambda e, pt=pt: e.activation(out=is32[:, :], in_=pt[0:NS, 0:128], func=AF.Copy), reads=[pk], writes=["is32"])
            if main:
                wo, wok = load_w(w_in[:, 3 * HW + c0:3 * HW + c0 + 128], KD, 128)
                po_, pok = proj_fm(wo, wok, KD, hrhs, ncol, ["hT"])
                P.op("act", lambda e, po_=po_: e.activation(out=ogs[:, 0:ncol], in_=po_[:, 0:ncol], func=AF.Silu), reads=[pok], writes=["ogs"])
                pO, pOk = psb[NPS - 1], ("ps", NPS - 1)
            for j in range(NT):
                cols = slice(j * 128, (j + 1) * 128)
                p = (h * NT + j) % 2
                lg, bcs, kk, ku, nb = lg2[:, p, :], bcs2[:, p, :], kk2[:, p, :], ku2[:, p, :], nb2[:, p, :]
                qs, ks, qin, kutok, scT = qs2[:, p, :], ks2[:, p, :], qin2[:, p, :], kutok2[:, p, :], scT2[:, p, :]
                Klg, Kbcs, Kkk, Kku, Knb = ("lg", p), ("bcs", p), ("kk", p), ("ku", p), ("nb", p)
                Kqs, Kks, Kqin, Kkutok, KscT = ("qs", p), ("ks", p), ("qin", p), ("kutok", p), ("scT", p)
                KS, KSbf = ("S", h), ("Sbf", h)

                def exbuf():
                    i = ex_rr[0]
                    if not P.dry:
                        ex_rr[0] = (i + 1) % 4
                    return ex4[:, i, :], ("ex", i)
                P.op("act", lambda e, cols=cols: e.activation(out=lg, in_=fT[:, cols], func=AF.Ln), reads=["fT"], writes=[Klg])
                P.op("dve", lambda e: e.tensor_tensor_scan(out=bcs, data0=ones[:], data1=lg, initial=0.0, op0=ALU.mult, op1=ALU.add), reads=["ones", Klg], writes=[Kbcs])
                P.op("dve", lambda e, cols=cols: e.tensor_scalar(out=kk, in0=fT[:, cols], scalar1=-1.0, scalar2=1.0, op0=ALU.mult, op1=ALU.add), reads=["fT"], writes=[Kkk])
                e4, Ke4 = exbuf()
                P.op("act", lambda e: e.activation(out=e4, in_=bcs, func=AF.Exp, bias=bcs[:, 127:128], scale=-1.0), reads=[Kbcs], writes=[Ke4])
                P.op("dve", lambda e: e.tensor_tensor(out=ku, in0=kk, in1=e4, op=ALU.mult), reads=[Kkk, Ke4], writes=[Kku])
                P.op("act", lambda e: e.activation(out=nb[:, 1:2], in_=bcs[:, 127:128], func=AF.Exp), reads=[Kbcs], writes=[Knb])
                ptk, ptkk = ps()
                P.op("pe", lambda e, ptk=ptk: e.transpose(out=ptk[:, 0:128], in_=ku, identity=ident[:]), reads=[Kku, "ident"], writes=[ptkk])
                P.op("act", lambda e, ptk=ptk: e.activation(out=kutok, in_=ptk[:, 0:128], func=AF.Copy), reads=[ptkk], writes=[Kkutok])
                if main:
                    P.op("dve", lambda e: e.tensor_scalar(out=nb[:, 0:1], in0=bcs[:, 64:65], scalar1=-1.0, scalar2=None, op0=ALU.mult), reads=[Kbcs], writes=[Knb])
                    e1, Ke1 = exbuf()
                    P.op("act", lambda e: e.activation(out=e1, in_=bcs, func=AF.Exp, bias=nb[:, 0:1], scale=1.0), reads=[Kbcs, Knb], writes=[Ke1])
                    P.op("dve", lambda e, cols=cols: e.tensor_tensor(out=qs, in0=qT[:, cols], in1=e1, op=ALU.mult), reads=["qT", Ke1], writes=[Kqs])
                    e2, Ke2 = exbuf()
                    P.op("act", lambda e: e.activation(out=e2, in_=bcs, func=AF.Exp, bias=bcs[:, 64:65], scale=-1.0), reads=[Kbcs], writes=[Ke2])
                    P.op("dve", lambda e: e.tensor_tensor(out=ks, in0=kk, in1=e2, op=ALU.mult), reads=[Kkk, Ke2], writes=[Kks])
                    e3, Ke3 = exbuf()
                    P.op("act", lambda e: e.activation(out=e3, in_=bcs, func=AF.Exp), reads=[Kbcs], writes=[Ke3])
                    P.op("dve", lambda e, cols=cols: e.tensor_tensor(out=qin, in0=qT[:, cols], in1=e3, op=ALU.mult), reads=["qT", Ke3], writes=[Kqin])
                    psc, psck = ps()
                    P.op("pe", lambda e, psc=psc: e.matmul(psc[:, 0:128], lhsT=ks, rhs=qs, start=True, stop=True), reads=[Kks, Kqs], writes=[psck])
                    P.op("dve", lambda e, psc=psc: e.tensor_tensor(out=scT, in0=psc[:, 0:128], in1=mut[:], op=ALU.mult), reads=[psck, "mut"], writes=[KscT])
                    P.op("pe", lambda e, j=j, cols=cols: e.matmul(pO[:, cols], lhsT=itok[:, j, :], rhs=scT, start=True, stop=False), reads=["itok", KscT], writes=[pOk], track=False)
                    P.op("pe", lambda e, h=h, cols=cols: e.matmul(pO[:, cols], lhsT=Sbf[:, h, :], rhs=qin, start=False, stop=True), reads=[KSbf, Kqin], writes=[pOk])
                pS, pSk = ps()
                P.op("pe", lambda e, j=j, pS=pS: e.matmul(pS[:, 0:128], lhsT=kutok, rhs=itok[:, j, :], start=True, stop=True), reads=[Kkutok, "itok"], writes=[pSk])
                P.op("dve", lambda e, h=h, pS=pS: e.scalar_tensor_tensor(out=S[:, h, :], in0=S[:, h, :], scalar=nb[:, 1:2], in1=pS[:, 0:128], op0=ALU.mult, op1=ALU.add), reads=[KS, Knb, pSk], writes=[KS])
                P.op("act", lambda e, h=h: e.activation(out=Sbf[:, h, :], in_=S[:, h, :], func=AF.Copy), reads=[KS], writes=[KSbf])
            if main and has_s:
                cs = slice(BT, BT + NS)
                P.op("dve", lambda e: e.tensor_scalar(out=kks[:], in0=fT[:, cs], scalar1=-1.0, scalar2=1.0, op0=ALU.mult, op1=ALU.add), reads=["fT"], writes=["kks"])
                ptk, ptkk = ps()
                P.op("pe", lambda e, ptk=ptk: e.transpose(out=ptk[0:NS, 0:128], in_=kks[:], identity=ident[:]), reads=["kks", "ident"], writes=[ptkk])
                P.op("act", lambda e, ptk=ptk: e.activation(out=ktok[:], in_=ptk[0:NS, 0:128], func=AF.Copy), reads=[ptkk], writes=["ktok"])
                P.op("dve", lambda e: e.tensor_tensor(out=km[:], in0=ktok[:].unsqueeze(1).to_broadcast([NS, NS, 128]), in1=ident[0:NS, 0:NS].unsqueeze(2).to_broadcast([NS, NS, 128]), op=ALU.mult), reads=["ktok", "ident"], writes=["xn"])
                for b in range(NS + 1):
                    if b < NS:
                        idx = h * NS + b
                        sl = idx % NSB
                        if not P.dry:
                            while sload_next[0] < min(idx + NSB, H * NS):
                                i2 = sload_next[0]
                                P.dma("sp", s0b[:, i2 % NSB, :], sh0[i2 % NS, i2 // NS], writes=[("s0b", i2 % NSB)])
                                sload_next[0] += 1
                        pkv, pkvk = ps()
                        P.op("pe", lambda e, b=b, pkv=pkv: e.matmul(pkv[:, 0:128], lhsT=km[:, b, :], rhs=is32[:, :], start=True, stop=True), reads=["xn", "is32"], writes=[pkvk])
                        P.op("dve", lambda e, b=b, sl=sl, pkv=pkv: e.scalar_tensor_tensor(out=snb[:, sl, :], in0=s0b[:, sl, :], scalar=fT[:, BT + b:BT + b + 1], in1=pkv[:, 0:128], op0=ALU.mult, op1=ALU.add), reads=[("s0b", sl), "fT", pkvk], writes=[("snb", sl)])
                        P.dma("sp", sh_s[b, h], snb[:, sl, :], reads=[("snb", sl)], is_output=True)
                    if b >= 1:
                        bp = b - 1
                        slp = (h * NS + bp) % NSB
                        P.op("pe", lambda e, bp=bp, slp=slp: e.matmul(pO[:, BT + bp:BT + bp + 1], lhsT=snb[:, slp, :], rhs=qT[:, BT + bp:BT + bp + 1], start=True, stop=True), reads=[("snb", slp), "qT"], writes=[pOk])
            if main:
                P.op("act", lambda e: e.activation(out=osq[:, 0:ncol], in_=pO[:, 0:ncol], func=AF.Square), reads=[pOk], writes=[("sgt", 0)])
                pn, pnk = ps()
                P.op("pe", lambda e, pn=pn: e.matmul(pn[:, 0:ncol], lhsT=ones[:], rhs=osq[:, 0:ncol], start=True, stop=True), reads=["ones", ("sgt", 0)], writes=[pnk])
                P.op("dve", lambda e, pn=pn: e.tensor_scalar(out=rn[:, 0:ncol], in0=pn[:, 0:ncol], scalar1=1.0 / 128, scalar2=EPS, op0=ALU.mult, op1=ALU.add), reads=[pnk], writes=["mt1"])
                P.op("act", lambda e: e.activation(out=rn[:, 0:ncol], in_=rn[:, 0:ncol], func=AF.Ln), reads=["mt1"], writes=["mt1"])
                P.op("act", lambda e: e.activation(out=rn[:, 0:ncol], in_=rn[:, 0:ncol], func=AF.Exp, scale=-0.5), reads=["mt1"], writes=["mt1"])
                P.op("dve", lambda e, h=h: e.scalar_tensor_tensor(out=otmp[:, 0:ncol], in0=pO[:, 0:ncol], scalar=ghT[:, h:h + 1], in1=rn[:, 0:ncol], op0=ALU.mult, op1=ALU.mult), reads=[pOk, "ghT", "mt1"], writes=["mt2"])
                P.op("dve", lambda e, h=h: e.tensor_tensor(out=oaT[:, h, 0:ncol], in0=otmp[:, 0:ncol], in1=ogs[:, 0:ncol], op=ALU.mult), reads=["mt2", "ogs"], writes=["oaT"])

        if main:
            for ct in range(GT):
                Zg8 = xn[:, 1024:1024 + 8 * NCHT].rearrange("p (g c) -> p g c", g=8)
                py, pyk = ps()
                for gl in range(8):
                    g = ct * 8 + gl
                    P.op("pe", lambda e, g=g, gl=gl, py=py: e.matmul(py[:, gl * NCHT:gl * NCHT + nchn], lhsT=Toep[:, g, :], rhs=Uall[:, g, 0:nchn], start=True, stop=False), reads=["Toep", ("Uall", ct)], writes=[pyk], track=False)
                    P.op("pe", lambda e, g=g, gl=gl, py=py: e.matmul(py[:, gl * NCHT:gl * NCHT + nchn], lhsT=W2all[:, g, :], rhs=Hin[:, g, 0:nchn], start=False, stop=True), reads=["W2all", "Hin", "HinS"], writes=[pyk], track=(gl == 7))
                P.op("act", lambda e, py=py: e.activation(out=Zg8[:, :, 0:nchn], in_=py[:, 0:8 * NCHT].rearrange("p (g c) -> p g c", g=8)[:, :, 0:nchn], func=AF.Gelu), reads=[pyk], writes=["xn"])
                for g0 in range(0, 8, 4):
                    pz, pzk = ps()
                    for gq in range(4):
                        P.op("pe", lambda e, g0=g0, gq=gq, pz=pz: e.transpose(out=pz[0:NCH, gq * 128:(gq + 1) * 128], in_=Zg8[:, g0 + gq, 0:NCH], identity=ident[:]), reads=["xn", "ident"], writes=[pzk], track=(gq == 3))
                    P.op("dve", lambda e, g0=g0, pz=pz: e.tensor_copy(out=zcm[0:NCH, :, g0 * 16:(g0 + 4) * 16].rearrange("p t (g q) -> p t g q", g=4), in_=pz[0:NCH, 0:512].rearrange("p (g t q) -> p t g q", g=4, t=8)), reads=[pzk], writes=["ucm"])
                    if has_s:
                        psz, pszk = ps()
                        for gq in range(4):
                            P.op("pe", lambda e, g0=g0, gq=gq, psz=psz: e.transpose(out=psz[0:NS, gq * 128:(gq + 1) * 128], in_=Zg8[:, g0 + gq, NCH:NCHT], identity=ident[:]), reads=["xn", "ident"], writes=[pszk], track=(gq == 3))
                        P.op("dve", lambda e, g0=g0, psz=psz: e.tensor_copy(out=zcms[:, g0 * 16:(g0 + 4) * 16].rearrange("p (g q) -> p g q", g=4), in_=psz[0:NS, 0:512].rearrange("p (g c) -> p g c", g=4)[:, :, 112:128]), reads=[pszk], writes=["zcms"])
                pt, pk = ps()
                for t in range(8):
                    P.op("pe", lambda e, t=t, pt=pt: e.transpose(out=pt[:, t * NCH:(t + 1) * NCH], in_=zcm[0:NCH, t, :], identity=ident[0:NCH, 0:NCH]), reads=["ucm", "ident"], writes=[pk], track=(t == 7 and not has_s))
                if has_s:
                    P.op("pe", lambda e, pt=pt: e.transpose(out=pt[:, 8 * NCH:8 * NCH + NS], in_=zcms[:, :], identity=ident[0:NS, 0:NS]), reads=["zcms", "ident"], writes=[pk])
                P.op("dve", lambda e, ct=ct, pt=pt: e.tensor_copy(out=zT[:, ct, 0:BT].rearrange("p (c t) -> p t c", t=8), in_=pt[:, 0:8 * NCH].rearrange("p (t c) -> p t c", t=8)), reads=[pk], writes=["zT"])
                if has_s:
                    P.op("dve", lambda e, ct=ct, pt=pt: e.tensor_copy(out=zT[:, ct, BT:BT + NS], in_=pt[:, 8 * NCH:8 * NCH + NS]), reads=[pk], writes=["zT"])
        if not main:
            return

        zrhs = lambda k: zT[:, k, 0:ncol]
        for m in range(GT):
            wl, wlk = load_w(w_glu[:, m * 128:(m + 1) * 128], GT, 128)
            pl, plk = proj_fm(wl, wlk, GT, zrhs, ncol, ["zT"])
            wg, wgk = load_w(w_glu[:, SW + m * 128:SW + (m + 1) * 128], GT, 128)
            pg, pgk = proj_fm(wg, wgk, GT, zrhs, ncol, ["zT"])
            P.op("act", lambda e, pg=pg: e.activation(out=sgt[:, 0, 0:ncol], in_=pg[:, 0:ncol], func=AF.Sigmoid), reads=[pgk], writes=[("sgt", 0)])
            P.op("dve", lambda e, m=m, pl=pl: e.tensor_tensor(out=obT[:, m, 0:ncol], in0=pl[:, 0:ncol], in1=sgt[:, 0, 0:ncol], op=ALU.mult), reads=[plk, ("sgt", 0)], writes=["obT"])

        GA0 = 4 * HW + SW
        for m in range(KD):
            wa, wak = load_w(w_in[:, GA0 + m * 128:GA0 + (m + 1) * 128], KD, 128)
            pga, pgak = proj_fm(wa, wak, KD, hrhs, ncol, ["hT"])
            P.op("act", lambda e, pga=pga: e.activation(out=sgt[:, 0, 0:ncol], in_=pga[:, 0:ncol], func=AF.Sigmoid), reads=[pgak], writes=[("sgt", 0)])
            wb, wbk = load_w(w_in[:, GA0 + D + m * 128:GA0 + D + (m + 1) * 128], KD, 128)
            pgb, pgbk = proj_fm(wb, wbk, KD, hrhs, ncol, ["hT"])
            P.op("act", lambda e, pgb=pgb: e.activation(out=sgt[:, 1, 0:ncol], in_=pgb[:, 0:ncol], func=AF.Sigmoid), reads=[pgbk], writes=[("sgt", 1)])
            wpa, wpak = load_w(w_pa[:, m * 128:(m + 1) * 128], H, 128)
            ppa, ppak = proj_fm(wpa, wpak, H, lambda k: oaT[:, k, 0:ncol], ncol, ["oaT"])
            P.op("dve", lambda e, ppa=ppa: e.tensor_tensor(out=mt1[:, 0:ncol], in0=ppa[:, 0:ncol], in1=sgt[:, 0, 0:ncol], op=ALU.mult), reads=[ppak, ("sgt", 0)], writes=["mt1"])
            wpb, wpbk = load_w(w_pb[:, m * 128:(m + 1) * 128], GT, 128)
            ppb, ppbk = proj_fm(wpb, wpbk, GT, lambda k: obT[:, k, 0:ncol], ncol, ["obT"])
            P.op("dve", lambda e, ppb=ppb: e.tensor_tensor(out=mt2[:, 0:ncol], in0=ppb[:, 0:ncol], in1=sgt[:, 1, 0:ncol], op=ALU.mult), reads=[ppbk, ("sgt", 1)], writes=["mt2"])
            P.op("dve", lambda e, m=m: e.tensor_tensor(out=mergedT[:, m, 0:ncol], in0=mt1[:, 0:ncol], in1=mt2[:, 0:ncol], op=ALU.add), reads=["mt1", "mt2"], writes=["mh"])

        def tm_matmul(wsrc, nk, lhs_fn, lkeys, finish_fn):
            for db in range(NDB):
                accs = [ps() for _ in tl]
                k = 0
                while k < nk:
                    nq = min(WSLOT // DB, nk - k)
                    wv, wk = load_w(wsrc[k * 128:(k + nq) * 128, db * DB:(db + 1) * DB], nq, DB)
                    for ti, (i, rows, cols) in enumerate(tl):
                        pa_, pak = accs[ti]
                        for q in range(nq):
                            P.op("pe", lambda e, k=k, q=q, pa_=pa_, rows=rows, cols=cols, wv=wv: e.matmul(pa_[0:rows, 0:DB], lhsT=lhs_fn(k + q, cols), rhs=wv[:, q, :], start=(k + q == 0), stop=(k + q == nk - 1)), reads=[wk] + lkeys, writes=[pak], track=(q == nq - 1))
                    k += nq
                for ti, (i, rows, cols) in enumerate(tl):
                    finish_fn(i, rows, db, accs[ti][0], accs[ti][1])

        for (i, rows, cols) in tl:
            srcx = xsrc[i * 128:(i + 1) * 128, :] if rows == 128 else xs
            P.dma("sp", x2[0:rows, i, :], srcx, writes=[("x2", i), "XR", "XI"])

        def fin_out(i, rows, db, pa_, pak):
            P.op("dve", lambda e: e.tensor_tensor(out=x2[0:rows, i, db * DB:(db + 1) * DB], in0=pa_[0:rows, 0:DB], in1=x2[0:rows, i, db * DB:(db + 1) * DB], op=ALU.add), reads=[pak, ("x2", i)], writes=[("x2", i)])
        tm_matmul(w_out, KD, lambda k, cols: mergedT[:, k, cols], ["mh"], fin_out)

        rms_to_hT(lambda i, rows: (x2[0:rows, i, :], ("x2", i)), tl, g2T, "g2T")

        for hh in range(NHH):
            j0 = hh * FTH
            nj = min(FTH, FT - j0)
            if nj <= 0:
                continue
            for jl in range(nj):
                j = j0 + jl
                wa, wak = load_w(w_up[:, j * 128:(j + 1) * 128], KD, 128)
                pa_, pak = proj_fm(wa, wak, KD, hrhs, ncol, ["hT"])
                P.op("act", lambda e, pa_=pa_: e.activation(out=sgt[:, 0, 0:ncol], in_=pa_[:, 0:ncol], func=AF.Silu), reads=[pak], writes=[("sgt", 0)])
                wb, wbk = load_w(w_up[:, FH + j * 128:FH + (j + 1) * 128], KD, 128)
                pb_, pbk = proj_fm(wb, wbk, KD, hrhs, ncol, ["hT"])
                P.op("dve", lambda e, jl=jl, pb_=pb_: e.tensor_tensor(out=hid[:, jl, 0:ncol], in0=pb_[:, 0:ncol], in1=sgt[:, 0, 0:ncol], op=ALU.mult), reads=[pbk, ("sgt", 0)], writes=["mh"])

            def fin_dn(i, rows, db, pa_, pak):
                P.op("dve", lambda e: e.tensor_tensor(out=x2[0:rows, i, db * DB:(db + 1) * DB], in0=pa_[0:rows, 0:DB], in1=x2[0:rows, i, db * DB:(db + 1) * DB], op=ALU.add), reads=[pak, ("x2", i)], writes=[("x2", i)])
            tm_matmul(w_down[j0 * 128:(j0 + nj) * 128, :], nj, lambda k, cols: hid[:, k, cols], ["mh"], fin_dn)

        for (i, rows, cols) in tl:
            P.op("act", lambda e, i=i, rows=rows: e.activation(out=junk[0:rows, 0:D], in_=x2[0:rows, i, :], func=AF.Square, accum_out=ss[0:rows, 2:3]), reads=[("x2", i)], writes=["mh", "ss"])
            P.op("dve", lambda e, rows=rows: e.tensor_scalar(out=ss[0:rows, 3:4], in0=ss[0:rows, 2:3], scalar1=1.0 / D, scalar2=EPS, op0=ALU.mult, op1=ALU.add), reads=["ss"], writes=["ss"])
            P.op("act", lambda e, rows=rows: e.activation(out=ss[0:rows, 3:4], in_=ss[0:rows, 3:4], func=AF.Ln), reads=["ss"], writes=["ss"])
            P.op("act", lambda e, rows=rows: e.activation(out=ss[0:rows, 3:4], in_=ss[0:rows, 3:4], func=AF.Exp, scale=-0.5), reads=["ss"], writes=["ss"])
            P.op("dve", lambda e, i=i, rows=rows: e.scalar_tensor_tensor(out=yt[0:rows, 0:D], in0=x2[0:rows, i, :], scalar=ss[0:rows, 3:4], in1=fgB[0:rows, :], op0=ALU.mult, op1=ALU.mult), reads=[("x2", i), "ss", "fgB"], writes=["xn"])
            if rows == 128:
                P.dma("sp", ydst[i * 128:(i + 1) * 128, :], yt[:, 0:D], reads=["xn"], is_output=True)
            else:
                P.dma("sp", y_s, yt[0:rows, 0:D], reads=["xn"], is_output=True)

    BGQ_PRE = cfg.get("BGQ_PRE", 0)
    BGQ_MAIN0 = cfg.get("BGQ_MAIN0", 0)

    def all_blocks():
        for bi in range(NPRE):
            do_block(xpre[bi * BT:(bi + 1) * BT, :], "P", False, None, False)
        for bi in range(NMAIN):
            do_block(xmain[bi * BT:(bi + 1) * BT, :], "M", bi == 0, y_main[bi * BT:(bi + 1) * BT, :], bi == NMAIN - 1)
    P.dry = True
    all_blocks()
    P.dry = False
    setup_wscr()
    all_blocks()
    assert w_idx[0] == len(wspecs)

    P.dma("sp", sh_p.rearrange("h k v -> k h v"), S[:], reads=[("S", h) for h in range(H)], is_output=True)
    P.dma("sp", s5re_p.rearrange("g n -> n g"), Wd[0:64, 0, :], reads=["Wd"], is_output=True, allow_slow_non_contiguous=True)
    P.dma("sp", s5im_p.rearrange("g n -> n g"), Wd[64:128, 0, :], reads=["Wd"], is_output=True, allow_slow_non_contiguous=True)

    SBUF_LEFT[0] = nc.sbuf_bytes_remaining
    P.emit()
    for cm in reversed(ctxs):
        cm.__exit__(None, None, None)
    P.close()
    return nc


WEIGHT_NAMES = ["lb_logits", "norm1_g", "w_in", "hgrn_norm_g", "s5_lam_re", "s5_lam_im", "s5_log_dt",
                "s5_B_re", "s5_B_im", "s5_C_re", "s5_C_im", "s5_D", "w_s5_glu", "w_proj_a", "w_proj_b",
                "w_out", "norm2_g", "w_ffn_up", "w_ffn_down"]


def make_in_maps(inputs, n_cores, half, ns):
    f32 = lambda a: np.ascontiguousarray(np.asarray(a, dtype=np.float32))
    shared = {k: f32(inputs[k][0]) for k in WEIGHT_NAMES if k != "lb_logits"}
    shared["lb_logits"] = f32(inputs["lb_logits"])
    shared["final_norm_g"] = f32(inputs["final_norm_g"])
    xp = np.asarray(inputs["x_prompt"], dtype=np.float32)
    xsm = np.asarray(inputs["x_sample"], dtype=np.float32)
    maps = []
    for c in range(n_cores):
        b, hf = c // 2, c % 2
        m = dict(shared)
        m["xmain"] = f32(xp[b, hf * half:(hf + 1) * half])
        m["xpre"] = f32(xp[b, 0:half]) if hf == 1 else np.zeros((half, xp.shape[2]), np.float32)
        m["xs"] = f32(xsm[c * ns:(c + 1) * ns, 0])
        m["sh0"] = f32(inputs["state_hgrn"][0, c * ns:(c + 1) * ns])
        m["s5re0"] = f32(inputs["state_s5_re"][0, c * ns:(c + 1) * ns])
        m["s5im0"] = f32(inputs["state_s5_im"][0, c * ns:(c + 1) * ns])
        maps.append(m)
    return maps


def assemble(results, n_cores, nb):
    y_prompt = np.stack([np.concatenate([results[2 * b]["y_main"], results[2 * b + 1]["y_main"]], axis=0) for b in range(nb)])
    y_sample = np.concatenate([results[c]["y_s"] for c in range(n_cores)], axis=0)[:, None, :]
    shp = np.stack([results[2 * b + 1]["sh_p"] for b in range(nb)])[None]
    rep = np.stack([results[2 * b + 1]["s5re_p"] for b in range(nb)])[None]
    imp = np.stack([results[2 * b + 1]["s5im_p"] for b in range(nb)])[None]
    shs = np.concatenate([results[c]["sh_s"] for c in range(n_cores)], axis=0)[None]
    res = np.concatenate([results[c]["s5re_s"] for c in range(n_cores)], axis=0)[None]
    ims = np.concatenate([results[c]["s5im_s"] for c in range(n_cores)], axis=0)[None]
    return tuple(np.ascontiguousarray(a, dtype=np.float32) for a in (y_prompt, y_sample, shp, rep, imp, shs, res, ims))


def kernel(**inputs):
    n = 8
    cfg = FULL_CFG
    nc = build(cfg)
    maps = make_in_maps(inputs, n, cfg["BT"] * cfg["NMAIN"], cfg["NS"])
    res = run_bass_kernel_spmd(nc, maps, core_ids=list(range(n)))
    return assemble(res.results, n, 4)
```
